# Optimizing a Trainium2 kernel written in Bass

```python
import functools
import math
import jax
import jax.numpy as jnp
from jax import lax
import numpy as np

D_MODEL = 1024
BATCH = 16
SEQ = 4096
DEPTH = 1
DEC_BATCH = 128
DEC_SEQ = 1
PAST_LEN = 8192
PAGE_SIZE = 128

D_SSM = D_MODEL // 2
SSM_GROUP = 16
N_SSM_GROUPS = D_SSM // SSM_GROUP
SSM_STATE = 64
SSM_CHUNK = 128
DT_MIN = 1e-3
DT_MAX = 1e-1
HEAD_DIM = 64
D_ATTN = D_MODEL - D_SSM
N_HEADS = D_ATTN // HEAD_DIM
N_KV_HEADS = 2
GROUP_SIZE = N_HEADS // N_KV_HEADS
D_KV = N_KV_HEADS * HEAD_DIM
CMP_BLOCK = 32
CMP_STRIDE = 16
CMP_HIDDEN = 2 * HEAD_DIM
SLC_BLOCK = 64
N_SELECT = 16
N_LOCAL = 2
WINDOW = 512
Q_BLOCK = 32
D_FF = 4 * D_MODEL
D_IN = D_SSM + D_ATTN + 6 * D_KV + 3 * N_HEADS
EPS = 1e-6

kernel_name = 'hybrid_s5_nsa_adaln_decode_step'


def rmsnorm(x, g):
    xf = x.astype(jnp.float32)
    y = xf * lax.rsqrt(jnp.mean(xf * xf, axis=-1, keepdims=True) + EPS)
    return (y * g.astype(jnp.float32)).astype(x.dtype)


def modulate(x, g, shift, scale):
    return rmsnorm(x, g) * (1.0 + scale[:, None, :]) + shift[:, None, :]


def masked_softmax(s, mask, axes):
    s = jnp.where(mask, s.astype(jnp.float32), -jnp.inf)
    m = jnp.max(s, axis=axes, keepdims=True)
    m = jnp.where(jnp.isfinite(m), m, 0.0)
    e = jnp.exp(s - m)
    den = jnp.sum(e, axis=axes, keepdims=True)
    return e / jnp.where(den > 0, den, 1.0)


def ssm_mixer(u, s0, lp):
    bsz, t, _ = u.shape
    lam = lax.complex(lp['ssm_lambda_re'].astype(jnp.float32), lp['ssm_lambda_im'].astype(jnp.float32))
    lam_dt = lam * jnp.exp(lp['ssm_log_dt'].astype(jnp.float32))[:, None]
    lam_bar = jnp.exp(lam_dt)
    b_mat = lax.complex(lp['ssm_b_re'].astype(jnp.float32), lp['ssm_b_im'].astype(jnp.float32))
    b_bar = ((lam_bar - 1.0) / lam)[:, :, None] * b_mat
    c_mat = lax.complex(lp['ssm_c_re'].astype(jnp.float32), lp['ssm_c_im'].astype(jnp.float32))
    chunk = SSM_CHUNK if t % SSM_CHUNK == 0 else t
    n_chunks = t // chunk
    decay = jnp.exp(lam_dt[None] * jnp.arange(1, chunk + 1, dtype=jnp.float32)[:, None, None])

    def combine(l, r):
        return (l[0] * r[0], r[0] * l[1] + r[1])

    def step(s, u_c):
        bu = jnp.einsum('bcgi,gpi->bcgp', u_c, b_bar)
        a = jnp.broadcast_to(lam_bar, bu.shape)
        _, local = lax.associative_scan(combine, (a, bu), axis=1)
        s_all = local + decay[None] * s[:, None]
        y = jnp.einsum('bcgp,gip->bcgi', s_all, c_mat).real
        return s_all[:, -1], y

    u_f = u.astype(jnp.float32)
    u_chunks = u_f.reshape(bsz, n_chunks, chunk, N_SSM_GROUPS, SSM_GROUP).swapaxes(0, 1)
    s_last, y = lax.scan(step, s0, u_chunks)
    y = y.swapaxes(0, 1).reshape(bsz, t, D_SSM) + lp['ssm_d'].astype(jnp.float32) * u_f
    z = jax.nn.gelu(y)
    out = z * jax.nn.sigmoid(jnp.dot(z, lp['ssm_w_glu'].astype(jnp.float32)))
    return out.astype(u.dtype), s_last


def compress(kv, pe, w1, w2):
    bsz, l = kv.shape[:2]
    nc = l // CMP_STRIDE
    chunks = kv[:, :nc * CMP_STRIDE].reshape(bsz, nc, CMP_STRIDE, N_KV_HEADS, HEAD_DIM)
    h_first = jnp.einsum('bcjgd,jdh->bcgh', chunks, w1[:CMP_STRIDE])
    h_second = jnp.einsum('bcjgd,jdh->bcgh', chunks, w1[CMP_STRIDE:])
    h = h_first[:, :-1] + h_second[:, 1:] + jnp.einsum('jd,jdh->h', pe, w1)
    return jnp.einsum('bngh,hd->bngd', jax.nn.gelu(h), w2)


def to_blocks(kv, n_blocks):
    pad = n_blocks * SLC_BLOCK - kv.shape[1]
    kv = jnp.pad(kv, ((0, 0), (0, pad), (0, 0), (0, 0)))
    return kv.reshape(kv.shape[0], n_blocks, SLC_BLOCK, N_KV_HEADS, HEAD_DIM)


def nsa_block(q, gates, q_pos0, k_cmp, v_cmp, k_blk, v_blk, k_band, v_band, band_pos0):
    bsz, _, _, nq, _ = q.shape
    qpos = q_pos0 + jnp.arange(nq, dtype=jnp.int32)
    n_cmp = k_cmp.shape[1]
    cmp_end = jnp.arange(n_cmp, dtype=jnp.int32) * CMP_STRIDE + (CMP_BLOCK - 1)
    p_cmp = masked_softmax(jnp.einsum('bgrqd,bngd->bgrqn', q, k_cmp), cmp_end[None, :] <= qpos[:, None], -1)
    o_cmp = jnp.einsum('bgrqn,bngd->bgrqd', p_cmp, v_cmp)
    n_slc = k_blk.shape[1]
    ratio = SLC_BLOCK // CMP_STRIDE
    imp = jnp.sum(p_cmp, axis=2)
    imp = jnp.pad(imp, ((0, 0), (0, 0), (0, 0), (1, ratio * n_slc + ratio - 1 - n_cmp)))
    w_head = jnp.array([1.0] + [2.0] * (ratio - 1), dtype=jnp.float32)
    imp_slc = jnp.einsum('bgqji,i->bgqj', imp[..., :ratio * n_slc].reshape(bsz, N_KV_HEADS, nq, n_slc, ratio), w_head)
    imp_slc = imp_slc + imp[..., ratio::ratio][..., :n_slc]
    blk = jnp.arange(n_slc, dtype=jnp.int32)
    cur = qpos // SLC_BLOCK
    valid = blk[None, :] * SLC_BLOCK <= qpos[:, None]
    forced = (blk[None, :] == 0) | (valid & (blk[None, :] > cur[:, None] - N_LOCAL))
    score = jnp.where(forced, jnp.inf, jnp.where(valid, imp_slc, -jnp.inf))
    _, idx = lax.top_k(score, min(N_SELECT, n_slc))
    b_i = jnp.arange(bsz)[:, None, None, None]
    g_i = jnp.arange(N_KV_HEADS)[None, :, None, None]
    k_sel = k_blk[b_i, idx, :, g_i, :]
    v_sel = v_blk[b_i, idx, :, g_i, :]
    kpos = idx[..., None] * SLC_BLOCK + jnp.arange(SLC_BLOCK, dtype=jnp.int32)
    mask_slc = (kpos <= qpos[:, None, None])[:, :, None]
    p_slc = masked_softmax(jnp.einsum('bgrqd,bgqkjd->bgrqkj', q, k_sel), mask_slc, (-2, -1))
    o_slc = jnp.einsum('bgrqkj,bgqkjd->bgrqd', p_slc, v_sel)
    kpos_w = band_pos0 + jnp.arange(k_band.shape[1], dtype=jnp.int32)
    dpos = qpos[:, None] - kpos_w[None, :]
    mask_w = (dpos >= 0) & (dpos < WINDOW) & (kpos_w[None, :] >= 0)
    p_win = masked_softmax(jnp.einsum('bgrqd,bkgd->bgrqk', q, k_band), mask_w, -1)
    o_win = jnp.einsum('bgrqk,bkgd->bgrqd', p_win, v_band)
    g = gates.transpose(0, 2, 3, 1).reshape(bsz, 3, N_KV_HEADS, GROUP_SIZE, nq)[..., None]
    o = g[:, 0] * o_cmp + g[:, 1] * o_slc + g[:, 2] * o_win
    return o.transpose(0, 3, 1, 2, 4).reshape(bsz, nq, D_ATTN).astype(q.dtype)


def attend_prompt(q, gates, kv_cmp, kv_slc, kv_win, lp):
    bsz, t = q.shape[:2]
    k_cmp = compress(kv_cmp[:, :, 0], lp['cmp_pe_k'], lp['cmp_w1_k'], lp['cmp_w2_k'])
    v_cmp = compress(kv_cmp[:, :, 1], lp['cmp_pe_v'], lp['cmp_w1_v'], lp['cmp_w2_v'])
    n_slc = -(-t // SLC_BLOCK)
    k_blk = to_blocks(kv_slc[:, :, 0], n_slc)
    v_blk = to_blocks(kv_slc[:, :, 1], n_slc)
    kw_pad = jnp.pad(kv_win, ((0, 0), (WINDOW, 0), (0, 0), (0, 0), (0, 0)))
    qb = Q_BLOCK if t % Q_BLOCK == 0 else t
    nqb = t // qb
    q_blocks = q.reshape(bsz, nqb, qb, N_KV_HEADS, GROUP_SIZE, HEAD_DIM).transpose(1, 0, 3, 4, 2, 5)
    g_blocks = gates.reshape(bsz, nqb, qb, 3, N_HEADS).swapaxes(0, 1)
    starts = jnp.arange(nqb, dtype=jnp.int32) * qb

    def one_block(args):
        qq, gg, s0 = args
        band = lax.dynamic_slice_in_dim(kw_pad, s0, WINDOW + qb, axis=1)
        return nsa_block(qq, gg, s0, k_cmp, v_cmp, k_blk, v_blk, band[:, :, 0], band[:, :, 1], s0 - WINDOW)

    o = lax.map(one_block, (q_blocks, g_blocks, starts))
    return o.swapaxes(0, 1).reshape(bsz, t, D_ATTN), kv_win[:, -min(WINDOW, t):]


def attend_sample(q, gates, kv_cmp, kv_slc, kv_win, cache_cmp, cache_slc, win_buf, page_table, layer, lp):
    bsz, t = q.shape[:2]
    past_len = page_table.shape[1] * PAGE_SIZE

    def full_rows(pool, new):
        past = pool[layer, page_table].reshape(bsz, past_len, 2, N_KV_HEADS, HEAD_DIM)
        return jnp.concatenate([past, new.astype(past.dtype)], axis=1)

    full_cmp = full_rows(cache_cmp, kv_cmp)
    full_slc = full_rows(cache_slc, kv_slc)
    k_cmp = compress(full_cmp[:, :, 0], lp['cmp_pe_k'], lp['cmp_w1_k'], lp['cmp_w2_k'])
    v_cmp = compress(full_cmp[:, :, 1], lp['cmp_pe_v'], lp['cmp_w1_v'], lp['cmp_w2_v'])
    n_slc = -(-full_slc.shape[1] // SLC_BLOCK)
    k_blk = to_blocks(full_slc[:, :, 0], n_slc)
    v_blk = to_blocks(full_slc[:, :, 1], n_slc)
    n_buf = win_buf.shape[1]
    band = jnp.concatenate([win_buf, kv_win.astype(win_buf.dtype)], axis=1)
    o = nsa_block(q.transpose(0, 2, 3, 1, 4), gates, PAST_LEN, k_cmp, v_cmp, k_blk, v_blk,
                  band[:, :, 0], band[:, :, 1], PAST_LEN - n_buf)
    return o, band[:, -n_buf:]


def hybrid_layer(x, c, ssm_s0, attend, lp):
    bsz, t, _ = x.shape
    mod = jnp.dot(jax.nn.silu(c), lp['w_ada']) + lp['b_ada']
    sh1, sc1, gt1, sh2, sc2, gt2 = jnp.split(mod, 6, axis=-1)
    h = modulate(x, lp['norm_attn'], sh1, sc1)
    proj = jnp.dot(h, lp['w_in'])
    sizes = [D_SSM, D_ATTN, D_KV, D_KV, D_KV, D_KV, D_KV, D_KV]
    offsets = [sum(sizes[:i + 1]) for i in range(len(sizes))]
    u, q, k_c, v_c, k_s, v_s, k_w, v_w, g = jnp.split(proj, offsets, axis=-1)
    kv_shape = (bsz, t, N_KV_HEADS, HEAD_DIM)
    kv_cmp = jnp.stack([k_c.reshape(kv_shape), v_c.reshape(kv_shape)], axis=2)
    kv_slc = jnp.stack([k_s.reshape(kv_shape), v_s.reshape(kv_shape)], axis=2)
    kv_win = jnp.stack([k_w.reshape(kv_shape), v_w.reshape(kv_shape)], axis=2)
    q = (q * HEAD_DIM ** -0.5).reshape(bsz, t, N_KV_HEADS, GROUP_SIZE, HEAD_DIM)
    gates = jax.nn.sigmoid(g.astype(jnp.float32)).reshape(bsz, t, 3, N_HEADS)
    o_ssm, s_last = ssm_mixer(u, ssm_s0, lp)
    o_attn, win_state = attend(q, gates, kv_cmp, kv_slc, kv_win)
    mix = jnp.concatenate([rmsnorm(o_ssm, lp['norm_out_ssm']), rmsnorm(o_attn, lp['norm_out_attn'])], axis=-1)
    x = x + gt1[:, None, :] * jnp.dot(mix, lp['w_out'])
    h2 = modulate(x, lp['norm_mlp'], sh2, sc2)
    x = x + gt2[:, None, :] * jnp.dot(jnp.square(jax.nn.relu(jnp.dot(h2, lp['w_up']))), lp['w_down'])
    return x, kv_cmp, kv_slc, win_state, s_last


def setup_inputs(seed: int = 0) -> dict:
    key = jax.random.key(seed)
    keys = iter(jax.random.split(key, 48))

    def nrm(shape, scale):
        return scale * jax.random.normal(next(keys), shape, jnp.float32)

    n_pages = PAST_LEN // PAGE_SIZE
    n_phys = (DEC_BATCH * n_pages * 5) // 4
    win_buf = min(WINDOW, PAST_LEN)
    kv_row = (2, N_KV_HEADS, HEAD_DIM)
    page_table = jax.random.permutation(next(keys), n_phys)[:DEC_BATCH * n_pages]
    page_table = page_table.reshape(DEC_BATCH, n_pages).astype(jnp.int32)
    lam_im = jnp.pi * jnp.arange(SSM_STATE, dtype=jnp.float32)
    ssm_shape = (DEPTH, N_SSM_GROUPS, SSM_STATE)
    return {
        'x_prompt': nrm((BATCH, SEQ, D_MODEL), 1.0),
        'x_sample': nrm((DEC_BATCH, DEC_SEQ, D_MODEL), 1.0),
        'cache_cmp': nrm((DEPTH, n_phys, PAGE_SIZE) + kv_row, 1.0),
        'cache_slc': nrm((DEPTH, n_phys, PAGE_SIZE) + kv_row, 1.0),
        'state_win': nrm((DEPTH, DEC_BATCH, win_buf) + kv_row, 1.0),
        'state_ssm_re': nrm((DEPTH, DEC_BATCH, N_SSM_GROUPS, SSM_STATE), 0.1),
        'state_ssm_im': nrm((DEPTH, DEC_BATCH, N_SSM_GROUPS, SSM_STATE), 0.1),
        'page_table': page_table,
        'c_prompt': nrm((BATCH, D_MODEL), 1.0),
        'c_sample': nrm((DEC_BATCH, D_MODEL), 1.0),
        'w_ada': nrm((DEPTH, D_MODEL, 6 * D_MODEL), 0.5 * D_MODEL ** -0.5),
        'b_ada': nrm((DEPTH, 6 * D_MODEL), 0.01),
        'norm_attn': 1.0 + nrm((DEPTH, D_MODEL), 0.01),
        'w_in': nrm((DEPTH, D_MODEL, D_IN), D_MODEL ** -0.5),
        'ssm_lambda_re': -0.5 + nrm(ssm_shape, 0.01),
        'ssm_lambda_im': lam_im + nrm(ssm_shape, 0.01),
        'ssm_log_dt': jax.random.uniform(next(keys), (DEPTH, N_SSM_GROUPS), jnp.float32, math.log(DT_MIN), math.log(DT_MAX)),
        'ssm_b_re': nrm((DEPTH, N_SSM_GROUPS, SSM_STATE, SSM_GROUP), (2 * SSM_GROUP) ** -0.5),
        'ssm_b_im': nrm((DEPTH, N_SSM_GROUPS, SSM_STATE, SSM_GROUP), (2 * SSM_GROUP) ** -0.5),
        'ssm_c_re': nrm((DEPTH, N_SSM_GROUPS, SSM_GROUP, SSM_STATE), SSM_STATE ** -0.5),
        'ssm_c_im': nrm((DEPTH, N_SSM_GROUPS, SSM_GROUP, SSM_STATE), SSM_STATE ** -0.5),
        'ssm_d': nrm((DEPTH, D_SSM), 0.5),
        'ssm_w_glu': nrm((DEPTH, D_SSM, D_SSM), D_SSM ** -0.5),
        'cmp_pe_k': nrm((DEPTH, CMP_BLOCK, HEAD_DIM), 0.02),
        'cmp_w1_k': nrm((DEPTH, CMP_BLOCK, HEAD_DIM, CMP_HIDDEN), (CMP_BLOCK * HEAD_DIM) ** -0.5),
        'cmp_w2_k': nrm((DEPTH, CMP_HIDDEN, HEAD_DIM), CMP_HIDDEN ** -0.5),
        'cmp_pe_v': nrm((DEPTH, CMP_BLOCK, HEAD_DIM), 0.02),
        'cmp_w1_v': nrm((DEPTH, CMP_BLOCK, HEAD_DIM, CMP_HIDDEN), (CMP_BLOCK * HEAD_DIM) ** -0.5),
        'cmp_w2_v': nrm((DEPTH, CMP_HIDDEN, HEAD_DIM), CMP_HIDDEN ** -0.5),
        'norm_out_ssm': 1.0 + nrm((DEPTH, D_SSM), 0.01),
        'norm_out_attn': 1.0 + nrm((DEPTH, D_ATTN), 0.01),
        'w_out': nrm((DEPTH, D_MODEL, D_MODEL), D_MODEL ** -0.5),
        'norm_mlp': 1.0 + nrm((DEPTH, D_MODEL), 0.01),
        'w_up': nrm((DEPTH, D_MODEL, D_FF), D_MODEL ** -0.5),
        'w_down': nrm((DEPTH, D_FF, D_MODEL), D_FF ** -0.5),
        'norm_final': 1.0 + nrm((D_MODEL,), 0.01),
    }


def reference(x_prompt, x_sample, cache_cmp, cache_slc, state_win, state_ssm_re, state_ssm_im, page_table,
              c_prompt, c_sample, w_ada, b_ada, norm_attn, w_in, ssm_lambda_re, ssm_lambda_im, ssm_log_dt,
              ssm_b_re, ssm_b_im, ssm_c_re, ssm_c_im, ssm_d, ssm_w_glu, cmp_pe_k, cmp_w1_k, cmp_w2_k,
              cmp_pe_v, cmp_w1_v, cmp_w2_v, norm_out_ssm, norm_out_attn, w_out, norm_mlp, w_up, w_down,
              norm_final):
    x_p = x_prompt
    x_s = x_sample
    cmp_p, slc_p, win_p, ssm_p = [], [], [], []
    cmp_s, slc_s, win_s, ssm_s = [], [], [], []
    for l in range(DEPTH):
        lp = {
            'w_ada': w_ada[l], 'b_ada': b_ada[l], 'norm_attn': norm_attn[l], 'w_in': w_in[l],
            'ssm_lambda_re': ssm_lambda_re[l], 'ssm_lambda_im': ssm_lambda_im[l], 'ssm_log_dt': ssm_log_dt[l],
            'ssm_b_re': ssm_b_re[l], 'ssm_b_im': ssm_b_im[l], 'ssm_c_re': ssm_c_re[l], 'ssm_c_im': ssm_c_im[l],
            'ssm_d': ssm_d[l], 'ssm_w_glu': ssm_w_glu[l],
            'cmp_pe_k': cmp_pe_k[l], 'cmp_w1_k': cmp_w1_k[l], 'cmp_w2_k': cmp_w2_k[l],
            'cmp_pe_v': cmp_pe_v[l], 'cmp_w1_v': cmp_w1_v[l], 'cmp_w2_v': cmp_w2_v[l],
            'norm_out_ssm': norm_out_ssm[l], 'norm_out_attn': norm_out_attn[l], 'w_out': w_out[l],
            'norm_mlp': norm_mlp[l], 'w_up': w_up[l], 'w_down': w_down[l],
        }
        s0_p = jnp.zeros((x_p.shape[0], N_SSM_GROUPS, SSM_STATE), jnp.complex64)
        x_p, kc, ks, kw, sl = hybrid_layer(x_p, c_prompt, s0_p, functools.partial(attend_prompt, lp=lp), lp)
        cmp_p.append(kc)
        slc_p.append(ks)
        win_p.append(kw)
        ssm_p.append(sl)
        s0_s = lax.complex(state_ssm_re[l].astype(jnp.float32), state_ssm_im[l].astype(jnp.float32))
        attend_s = functools.partial(attend_sample, cache_cmp=cache_cmp, cache_slc=cache_slc, win_buf=state_win[l],
                                     page_table=page_table, layer=l, lp=lp)
        x_s, kc, ks, kw, sl = hybrid_layer(x_s, c_sample, s0_s, attend_s, lp)
        cmp_s.append(kc)
        slc_s.append(ks)
        win_s.append(kw)
        ssm_s.append(sl)
    y_prompt = rmsnorm(x_p, norm_final)
    y_sample = rmsnorm(x_s, norm_final)
    ssm_p_all = jnp.stack(ssm_p, axis=0)
    ssm_s_all = jnp.stack(ssm_s, axis=0)
    return (y_prompt, y_sample,
            jnp.stack(cmp_p, axis=0), jnp.stack(slc_p, axis=0), jnp.stack(win_p, axis=0),
            ssm_p_all.real, ssm_p_all.imag,
            jnp.stack(cmp_s, axis=0), jnp.stack(slc_s, axis=0), jnp.stack(win_s, axis=0),
            ssm_s_all.real, ssm_s_all.imag)
```

```python
import math
from contextlib import ExitStack
import numpy as np
import concourse.bass as bass
import concourse.mybir as mybir
from concourse.bass_utils import run_bass_kernel_spmd

F32 = mybir.dt.float32
BF16 = mybir.dt.bfloat16
I32 = mybir.dt.int32
AF = mybir.ActivationFunctionType
ALU = mybir.AluOpType

NLANES = 10
INS_LINES = {}
NSW = 4
PE_NOP_AFTER_WAIT = False
BIG = 1.0e4
EPS = 1e-6
GC = 1.5957691216057308


class Buf:
    __slots__ = ("lw", "rd")

    def __init__(self):
        self.lw = None
        self.rd = {}


class T:
    def __init__(self, ap, buf=None):
        self.ap = ap
        self.buf = buf if buf is not None else Buf()

    def __getitem__(self, k):
        return self.ap[k]


def _bufs(xs):
    return [x.buf if isinstance(x, T) else x for x in xs]


class Prog:
    def __init__(self, nc):
        self.nc = nc
        self.units = ["pe", "act", "dve", "pool"] + ["L%d" % i for i in range(NLANES)] + ["G%d" % i for i in range(NSW)]
        self.q = {e: [] for e in ("pe", "act", "dve", "pool", "sp")}
        self.cnt = {u: 0 for u in self.units}
        self.seen = {e: {u: 0 for u in self.units} for e in self.q}
        self.lane_rr = 0
        self.sw_rr = 0
        self.n_ins = 0

    def _deps(self, reads, writes):
        deps = {}
        for b in reads:
            if b.lw is not None and deps.get(b.lw[0], 0) < b.lw[1]:
                deps[b.lw[0]] = b.lw[1]
        for b in writes:
            if b.lw is not None and deps.get(b.lw[0], 0) < b.lw[1]:
                deps[b.lw[0]] = b.lw[1]
            for u, n in b.rd.items():
                if deps.get(u, 0) < n:
                    deps[u] = n
        return deps

    def _waits(self, q, deps, me=None, raw=False):
        for u, n in deps.items():
            if u == me and not raw:
                continue
            if self.seen[q][u] >= n:
                continue
            self.seen[q][u] = n
            self.q[q].append(("w", u, n * 16 if u[0] in "LG" else n))

    def op(self, eng, fn, reads=(), writes=()):
        reads = _bufs(reads)
        writes = _bufs(writes)
        deps = self._deps(reads, writes)
        raw = eng != "pe"
        nq0 = len(self.q[eng])
        self._waits(eng, deps, me=eng, raw=raw)
        if eng == "pe" and len(self.q[eng]) > nq0 and PE_NOP_AFTER_WAIT:
            self.q[eng].append(("n",))
        self.cnt[eng] += 1
        n = self.cnt[eng]
        import sys as _s
        fr = _s._getframe(1)
        while fr.f_code.co_name in ('op','mm','tr','act','tt','ts','stt','cp','ms'):
            fr = fr.f_back
        self.q[eng].append(("i", fn, eng, fr.f_lineno))
        self.n_ins += 1
        for b in reads:
            if b.rd.get(eng, 0) < n:
                b.rd[eng] = n
        for b in writes:
            b.lw = (eng, n)
            b.rd = {}

    def dma(self, out, in_, reads=(), writes=(), q="sp", indirect=None, slow=False):
        reads = _bufs(reads)
        writes = _bufs(writes)
        if q == "pool":
            lane = "G%d" % self.sw_rr
            self.sw_rr = (self.sw_rr + 1) % NSW
        else:
            lane = "L%d" % self.lane_rr
            self.lane_rr = (self.lane_rr + 1) % NLANES
        deps = self._deps(reads, writes)
        if self.cnt[lane] > 0:
            deps[lane] = max(deps.get(lane, 0), self.cnt[lane])
        self._waits(q, deps)
        self.cnt[lane] += 1
        n = self.cnt[lane]
        if indirect is not None:
            fn = lambda e: e.indirect_dma_start(out=out, out_offset=None, in_=in_,
                                                in_offset=bass.IndirectOffsetOnAxis(ap=indirect, axis=0))
        else:
            fn = (lambda e: e.dma_start(out=out, in_=in_, allow_slow_non_contiguous=True)) if slow else (lambda e: e.dma_start(out=out, in_=in_))
        import sys as _s
        fr = _s._getframe(1)
        self.q[q].append(("i", fn, lane, fr.f_lineno))
        self.n_ins += 1
        for b in reads:
            if b.rd.get(lane, 0) < n:
                b.rd[lane] = n
        for b in writes:
            b.lw = (lane, n)
            b.rd = {}

    def barrier(self):
        deps = {u: n for u, n in self.cnt.items() if n > 0}
        for q in self.q:
            self._waits(q, dict(deps))

    def _pe_mode(self, mode):
        self.pe_mode = mode

    @staticmethod
    def _r(n):
        return 32 if n <= 32 else (64 if n <= 64 else 128)

    def mm(self, out, lhsT, rhs, start=True, stop=True, reads=(), writes=()):
        self._pe_mode(("mm", self._r(lhsT.shape[0]), self._r(int(np.prod(lhsT.shape[1:]))), str(lhsT.dtype)))
        self.op("pe", lambda e: e.matmul(out, lhsT=lhsT, rhs=rhs, start=start, stop=stop,
                                         skip_group_check=True), reads, writes)

    def tr(self, out, in_, ident, reads=(), writes=()):
        self._pe_mode(("tr", self._r(in_.shape[0]), self._r(int(np.prod(in_.shape[1:])))))
        self.op("pe", lambda e: e.transpose(out, in_, ident), reads, writes)

    def act(self, out, in_, func, scale=1.0, bias=0.0, accum=None, reads=(), writes=()):
        if accum is None:
            self.op("act", lambda e: e.activation(out=out, in_=in_, func=func, scale=scale, bias=bias),
                    reads, writes)
        else:
            self.op("act", lambda e: e.activation(out=out, in_=in_, func=func, scale=scale, bias=bias,
                                                  accum_out=accum), reads, writes)

    def tt(self, eng, out, in0, in1, op, reads=(), writes=()):
        self.op(eng, lambda e: e.tensor_tensor(out=out, in0=in0, in1=in1, op=op), reads, writes)

    def ts(self, eng, out, in0, s1, op0, s2=None, op1=None, reads=(), writes=()):
        if op1 is None:
            self.op(eng, lambda e: e.tensor_scalar(out=out, in0=in0, scalar1=s1, scalar2=None, op0=op0),
                    reads, writes)
        else:
            self.op(eng, lambda e: e.tensor_scalar(out=out, in0=in0, scalar1=s1, scalar2=s2, op0=op0, op1=op1),
                    reads, writes)

    def stt(self, eng, out, in0, scalar, in1, op0, op1, reads=(), writes=()):
        self.op(eng, lambda e: e.scalar_tensor_tensor(out=out, in0=in0, scalar=scalar, in1=in1, op0=op0, op1=op1),
                reads, writes)

    def cp(self, eng, out, in_, reads=(), writes=()):
        if eng == "act":
            self.op("act", lambda e: e.copy(out=out, in_=in_), reads, writes)
        else:
            self.op(eng, lambda e: e.tensor_copy(out=out, in_=in_), reads, writes)

    def ms(self, eng, ap, val, writes=()):
        self.op(eng, lambda e: e.memset(ap, val), (), writes)

    def emit(self, stack):
        nc = self.nc
        sems = {u: stack.enter_context(nc.semaphore("s_" + u)) for u in self.units}
        block = stack.enter_context(nc.Block())

        def run(eng, items):
            for it in items:
                if it[0] == "w":
                    eng.wait_ge(sems[it[1]], it[2])
                elif it[0] == "n":
                    eng.nop(nofuse=True)
                else:
                    bi = it[1](eng)
                    bi.then_inc(sems[it[2]], 16 if it[2][0] in "LG" else 1)
                    if len(it) > 3:
                        try:
                            INS_LINES[str(bi.ins.name)] = it[3]
                        except Exception:
                            pass

        @block.tensor
        def _(e):
            run(e, self.q["pe"])

        @block.scalar
        def _(e):
            run(e, self.q["act"])

        @block.vector
        def _(e):
            run(e, self.q["dve"])

        @block.gpsimd
        def _(e):
            run(e, self.q["pool"])

        @block.sync
        def _(e):
            run(e, self.q["sp"])


class _Stop(Exception):
    pass


class Ring:
    def __init__(self, items):
        self.items = items
        self.i = 0

    def get(self):
        t = self.items[self.i]
        self.i = (self.i + 1) % len(self.items)
        return t


def build(cfg):
    B, Tq, DB, NPG, NPHYS, NSEL = cfg["B"], cfg["T"], cfg["DB"], cfg["NPG"], cfg["NPHYS"], cfg["NSEL"]
    NT = Tq // 128
    PAST = NPG * 128
    NS = DB + B
    NBP = Tq // 64
    NBS = PAST // 64 + 1
    NBSm = NBS - 1
    NCS = PAST // 16 - 1
    assert NBP <= 128 and NBSm <= 128 and PAST >= 512 and Tq >= 512
    nc = bass.Bass("TRN2", target_bir_lowering=False)
    P = Prog(nc)
    global _LASTP
    _LASTP = P

    def din(name, shape, dt=F32):
        return T(nc.dram_tensor(name, shape, dt, kind="ExternalInput").ap())

    def dout(name, shape):
        return T(nc.dram_tensor(name, shape, F32, kind="ExternalOutput").ap())

    x_p = din("x_p", [B * Tq, 1024]); x_s = din("x_s", [DB, 1024]); c_all = din("c_all", [NS, 1024])
    cache_cmp = din("cache_cmp", [NPHYS * 128, 256]); cache_slc = din("cache_slc", [NPHYS * 128, 256])
    state_win = din("state_win", [DB * 512, 256])
    sre_in = din("ssm_re_in", [DB, 2048]); sim_in = din("ssm_im_in", [DB, 2048])
    ptab = din("ptab", [DB * NPG, 1], I32)
    w_ada = din("w_ada", [1024, 6144]); b_ada = din("b_ada", [6144]); norm_attn = din("norm_attn", [1024])
    w_in = din("w_in", [1024, 1816]); lam_re = din("lam_re", [2048]); lam_im = din("lam_im", [2048])
    log_dt = din("log_dt", [32]); b_re = din("b_re", [32 * 64 * 16]); b_im = din("b_im", [32 * 64 * 16])
    c_re = din("c_re", [32 * 16 * 64]); c_im = din("c_im", [32 * 16 * 64]); ssm_d = din("ssm_d", [512])
    w_glu = din("w_glu", [512, 512])
    pe_k = din("pe_k", [2048]); w1_k = din("w1_k", [2048, 128]); w2_k = din("w2_k", [128, 64])
    pe_v = din("pe_v", [2048]); w1_v = din("w1_v", [2048, 128]); w2_v = din("w2_v", [128, 64])
    n_ssm = din("n_ssm", [512]); n_attn = din("n_attn", [512]); w_out = din("w_out", [1024, 1024])
    norm_mlp = din("norm_mlp", [1024]); w_up = din("w_up", [1024, 4096]); w_down = din("w_down", [4096, 1024])
    norm_final = din("norm_final", [1024])

    y_p = dout("y_p", [B * Tq, 1024]); y_s = dout("y_s", [DB, 1024])
    cmp_p = dout("cmp_p", [B * Tq, 256]); slc_p = dout("slc_p", [B * Tq, 256]); win_p = dout("win_p", [B * 512, 256])
    sre_p = dout("sre_p", [B, 2048]); sim_p = dout("sim_p", [B, 2048])
    cmp_s = dout("cmp_s", [DB, 256]); slc_s = dout("slc_s", [DB, 256]); win_s = dout("win_s", [DB * 512, 256])
    sre_s = dout("sre_s", [DB, 2048]); sim_s = dout("sim_s", [DB, 2048])
    x1_d = T(nc.dram_tensor("x1_scratch", [B * Tq, 1024], F32, kind="ExternalOutput" if cfg.get("dbg") else "Internal").ap())
    outs_all = [y_p, y_s, cmp_p, slc_p, win_p, sre_p, sim_p, cmp_s, slc_s, win_s, sre_s, sim_s]

    st = ExitStack()
    ARENA_W = 53000
    arena = st.enter_context(nc.sbuf_tensor("arena", [128, ARENA_W], F32))
    off = [0]

    def A(shape, dt=F32, npart=128):
        n = int(np.prod(shape))
        w = n if dt != BF16 else (n + 1) // 2
        w = (w + 1) // 2 * 2
        assert off[0] + w <= ARENA_W, ("arena overflow", off[0], w)
        a = arena[0:npart, off[0]:off[0] + w]
        off[0] += w
        if dt != F32:
            a = a.bitcast(dt)
        a = a[:, 0:n]
        if len(shape) == 2:
            a = a.rearrange("p (a b) -> p a b", a=shape[0])
        elif len(shape) == 3:
            a = a.rearrange("p (a b c) -> p a b c", a=shape[0], b=shape[1])
        elif len(shape) == 4:
            a = a.rearrange("p (a b c d) -> p a b c d", a=shape[0], b=shape[1], c=shape[2])
        return T(a)

    def ring(n, shape, dt=F32, npart=128):
        return Ring([A(shape, dt, npart) for _ in range(n)])

    psb = [T(st.enter_context(nc.psum_tensor("psb%d" % i, [128, 512], F32))[:]) for i in range(8)]
    psg = Ring(psb[0:5])
    pso = Ring(psb[5:8])

    def finish_prog():
        P.barrier()
        deps = {}
        for o_ in outs_all:
            if o_.buf.lw is not None:
                deps[o_.buf.lw[0]] = max(deps.get(o_.buf.lw[0], 0), o_.buf.lw[1])
        P._waits('sp', deps)
        P.emit(st)
        st.close()
        return nc

    stop = cfg.get("stop", "")
    ident = A([128]); ones = A([128]); tri = A([128], BF16); anti = A([128], BF16)
    P.ms("pool", ones[:], 1.0, [ones])
    P.ms("pool", ident[:], 1.0, [ident])
    P.op("pool", lambda e: e.affine_select(out=ident[:], in_=ident[:], pattern=[[1, 128]], compare_op=ALU.is_equal,
                                           fill=0.0, base=0, channel_multiplier=-1), [ident], [ident])
    P.op("pool", lambda e: e.affine_select(out=tri[:], in_=ones[:], pattern=[[1, 128]], compare_op=ALU.is_ge,
                                           fill=0.0, base=0, channel_multiplier=-1), [ones], [tri])
    P.op("pool", lambda e: e.affine_select(out=anti[:], in_=ones[:], pattern=[[-1, 128]], compare_op=ALU.is_gt,
                                           fill=0.0, base=0, channel_multiplier=1), [ones], [anti])

    def loadT(vec, k, dst):
        tmp = A([128], F32)
        P.dma(tmp[0:k, :], vec.ap.rearrange("(k p) -> k p", p=128), [vec], [tmp])
        ps = psg.get()
        P.tr(ps[:, 0:k], tmp[0:k, :], ident[0:k, 0:k], [tmp, ident], [ps])
        return ps

    nattnT = A([8]); nmlpT = A([8]); dT = A([4]); gST = A([4]); gAT = A([4])
    for vec, k, dst in ((norm_attn, 8, nattnT), (norm_mlp, 8, nmlpT), (ssm_d, 4, dT), (n_ssm, 4, gST), (n_attn, 4, gAT)):
        ps = loadT(vec, k, dst)
        P.cp("dve", dst[:], ps[:, 0:k], [ps], [dst])
    nf_bc = A([1024])
    P.dma(nf_bc[:], norm_final.ap.partition_broadcast(128), [norm_final], [nf_bc])

    if stop == 'c0':
        return finish_prog()
    a1T = A([8, NS]); sh1T = A([8, NS]); a2T = A([8, NS]); sh2T = A([8, NS]); gt_tm = A([2, 1024], F32, NS); selr = A([128], F32, NS)
    mark_persist = off[0]
    csb = A([1024], F32, NS); csil = A([1024], F32, NS); cT = A([8, NS], BF16)
    P.dma(csb[:], c_all.ap, [c_all], [csb])
    P.act(csil[:], csb[:], AF.Silu, reads=[csb], writes=[csil])
    ps = psg.get()
    for kc in range(8):
        P.tr(ps[:, kc * NS:(kc + 1) * NS], csil[:, kc * 128:(kc + 1) * 128], ident[0:NS, 0:NS], [csil, ident], [ps])
    P.cp("dve", cT[:], ps[:, 0:8 * NS].rearrange("p (a b) -> p a b", a=8), [ps], [cT])
    wblk = ring(2, [8, 512], BF16); bbc = ring(2, [512], F32, NS); modb = ring(2, [512], F32, NS)
    featdst = {0: sh1T, 1: a1T, 3: sh2T, 4: a2T}
    for cb in range(12):
        wb = wblk.get(); bb = bbc.get(); mb = modb.get()
        P.dma(wb[:], w_ada.ap[:, cb * 512:(cb + 1) * 512].rearrange("(kc p) n -> p kc n", p=128), [w_ada], [wb], q="pool")
        P.dma(bb[:], b_ada.ap[cb * 512:(cb + 1) * 512].partition_broadcast(NS), [b_ada], [bb])
        ps = psg.get()
        for kc in range(8):
            P.mm(ps[0:NS, :], cT[:, kc, :], wb[:, kc, :], kc == 0, kc == 7, [cT, wb], [ps])
        which, half = cb // 2, cb % 2
        if which in (2, 5):
            P.tt("dve", gt_tm[:, 0 if which == 2 else 1, half * 512:(half + 1) * 512], ps[0:NS, :], bb[:], ALU.add,
                 [ps, bb], [gt_tm])
        else:
            P.tt("dve", mb[:], ps[0:NS, :], bb[:], ALU.add, [ps, bb], [mb])
            ps2 = psg.get()
            for j in range(4):
                P.tr(ps2[:, j * NS:(j + 1) * NS], mb[:, j * 128:(j + 1) * 128], ident[0:NS, 0:NS], [mb, ident], [ps2])
            dst = featdst[which]
            P.cp("dve", dst[:, half * 4:half * 4 + 4, :], ps2[:, 0:4 * NS].rearrange("p (a b) -> p a b", a=4), [ps2], [dst])
    for kc in range(8):
        P.ts("dve", a1T[:, kc, :], a1T[:, kc, :], 1.0, ALU.add, nattnT[:, kc:kc + 1], ALU.mult, [a1T, nattnT], [a1T])
        P.ts("dve", a2T[:, kc, :], a2T[:, kc, :], 1.0, ALU.add, nmlpT[:, kc:kc + 1], ALU.mult, [a2T, nmlpT], [a2T])
    P.barrier()
    off[0] = mark_persist

    if stop == 'mod':
        return finish_prog()
    w_in_bf = A([8, 1816], BF16)
    W1 = A([2, 32, 128], BF16, 64); W2 = A([2, 64], BF16)
    for kc in range(8):
        P.dma(w_in_bf[:, kc, :], w_in.ap[kc * 128:(kc + 1) * 128, :], [w_in], [w_in_bf], q="pool")
    for kv, (w1, w2) in enumerate(((w1_k, w2_k), (w1_v, w2_v))):
        P.dma(W1[:, kv, :, :], w1.ap.rearrange("(j d) h -> d j h", d=64), [w1], [W1], q="pool")
        P.dma(W2[:, kv, :], w2.ap, [w2], [W2], q="pool")
    peT = A([2, 32], BF16, 64); cbias = A([2])
    with nc.allow_non_contiguous_dma(reason="tiny pe transpose load"):
        for kv, pe in enumerate((pe_k, pe_v)):
            P.dma(peT[:, kv, :], pe.ap.rearrange("(j d) -> d j", d=64), [pe], [peT], q="pool", slow=True)
    ps = psg.get()
    for kv in range(2):
        for j in range(32):
            P.mm(ps[:, kv:kv + 1], W1[:, kv, j, :], peT[:, kv, j:j + 1], j == 0, j == 31, [W1, peT], [ps])
    P.cp("dve", cbias[:], ps[:, 0:2], [ps], [cbias])

    if stop == 'w':
        return finish_prog()
    NKTP = max(1, (Tq // 16 + 127) // 128)
    NKTS = (NCS + 127) // 128
    NKTC = max(NKTP, NKTS)
    NBm = max(NBP, NBSm)
    Mm = A([NKTC, NBm], BF16)
    NKE = max(NT, NPG)
    Eall = A([NKE, 128], BF16)
    Rtab = A([128]); sbias = A([NBS], F32, 1)
    mark_seltmp = off[0]
    mA = A([NKTC, NBm]); mB = A([NKTC, NBm]); etmp = A([NKE, 128]); rel = A([128]); r2 = A([128])
    P.ms("pool", mA[:], 1.0, [mA]); P.ms("pool", mB[:], 1.0, [mB])
    for (tm, lo, hi) in ((mA, -1, 3), (mB, 0, 2)):
        P.op("pool", lambda e, tm=tm, lo=lo: e.affine_select(out=tm[:], in_=tm[:], pattern=[[128, NKTC], [-4, NBm]],
                                                             compare_op=ALU.is_ge, fill=0.0, base=-lo, channel_multiplier=1), [tm], [tm])
        P.op("pool", lambda e, tm=tm, hi=hi: e.affine_select(out=tm[:], in_=tm[:], pattern=[[-128, NKTC], [4, NBm]],
                                                             compare_op=ALU.is_ge, fill=0.0, base=hi, channel_multiplier=-1), [tm], [tm])
    P.tt("dve", Mm[:], mA[:], mB[:], ALU.add, [mA, mB], [Mm])
    P.ms("pool", etmp[:], 1.0, [etmp])
    P.op("pool", lambda e: e.affine_select(out=etmp[:], in_=etmp[:], pattern=[[128, NKE], [1, 128]], compare_op=ALU.is_ge,
                                           fill=0.0, base=0, channel_multiplier=-64), [etmp], [etmp])
    P.op("pool", lambda e: e.affine_select(out=etmp[:], in_=etmp[:], pattern=[[-128, NKE], [-1, 128]], compare_op=ALU.is_ge,
                                           fill=0.0, base=63, channel_multiplier=64), [etmp], [etmp])
    P.cp("dve", Eall[:], etmp[:], [etmp], [Eall])
    P.op("pool", lambda e: e.iota(rel[:], pattern=[[1, 128]], base=-63, channel_multiplier=0,
                                  allow_small_or_imprecise_dtypes=True), [], [rel])
    P.op("pool", lambda e: e.affine_select(out=r2[:], in_=ones[:], pattern=[[0, 128]], compare_op=ALU.is_ge,
                                           fill=0.0, base=-64, channel_multiplier=1), [ones], [r2])
    P.tt("dve", rel[:], rel[:], r2[:], ALU.subtract, [rel, r2], [rel])
    P.ts("dve", Rtab[:], rel[:], -1.0, ALU.is_ge, BIG, ALU.mult, [rel], [Rtab])
    P.ts("dve", r2[:], rel[:], 1.0, ALU.is_ge, -2 * BIG, ALU.mult, [rel], [r2])
    P.tt("dve", Rtab[:], Rtab[:], r2[:], ALU.add, [Rtab, r2], [Rtab])
    P.ms("pool", sbias[:], 0.0, [sbias])
    for j in (0, NBS - 2, NBS - 1):
        P.ms("pool", sbias[:, j:j + 1], BIG, [sbias])
    P.barrier()
    off[0] = mark_seltmp
    mark_mixer = off[0]

    if stop == 'setup':
        return finish_prog()
    xt_r = ring(2, [1024]); xn_r = ring(1, [1024]); junk = A([1024]); st4 = ring(4, [4])
    hT_r = ring(2, [8, 128], BF16); pr_r = ring(1, [1816])

    def rstd_of(src_ap, src_t, ntok, width):
        s = st4.get()
        P.act(junk[0:ntok, 0:width], src_ap, AF.Square, accum=s[0:ntok, 0:1], reads=[src_t], writes=[junk, s])
        P.ts("dve", s[0:ntok, 1:2], s[0:ntok, 0:1], 1.0 / width, ALU.mult, EPS, ALU.add, [s], [s])
        P.act(s[0:ntok, 2:3], s[0:ntok, 1:2], AF.Sqrt, reads=[s], writes=[s])
        P.op("dve", lambda e: e.reciprocal(out=s[0:ntok, 3:4], in_=s[0:ntok, 2:3]), [s], [s])
        return s

    def norm_mod_T(xt, ntok, aT, shT, seq):
        s = rstd_of(xt[0:ntok, :], xt, ntok, 1024)
        xn = xn_r.get()
        P.ts("dve", xn[0:ntok, :], xt[0:ntok, :], s[0:ntok, 3:4], ALU.mult, reads=[xt, s], writes=[xn])
        hT = hT_r.get()
        if cfg.get("dummy", 0) == 3:
            for _ in range(3):
                P.cp("dve", junk[0:ntok, 0:1024], xn[0:ntok, :], [xn], [junk, xn])
        for half in range(2):
            ps = psg.get()
            if half == 0 and cfg.get("dummy", 0) == 1:
                P.tr(ps[:, 0:128], ident[:], ident[:], [ident], [ps])
            for j in range(4):
                kc = half * 4 + j
                P.tr(ps[:, j * ntok:(j + 1) * ntok], xn[0:ntok, kc * 128:(kc + 1) * 128], ident[0:ntok, 0:ntok], [xn, ident], [ps])
            for j in range(4):
                kc = half * 4 + j
                if seq is not None and cfg.get("dummy", 0) == 2:
                    P.ts("dve", hT[:, kc, 0:ntok], ps[:, j * ntok:(j + 1) * ntok], aT[:, kc, seq:seq + 1], ALU.mult,
                         shT[:, kc, seq:seq + 1], ALU.add, [ps, aT, shT], [hT])
                elif seq is not None:
                    P.act(hT[:, kc, 0:ntok], ps[:, j * ntok:(j + 1) * ntok], AF.Identity, scale=aT[:, kc, seq:seq + 1],
                          bias=shT[:, kc, seq:seq + 1], reads=[ps, aT, shT], writes=[hT])
                else:
                    P.tt("dve", hT[:, kc, 0:ntok], ps[:, j * ntok:(j + 1) * ntok], aT[:, kc, 0:ntok], ALU.mult, [ps, aT], [hT])
                    P.tt("dve", hT[:, kc, 0:ntok], hT[:, kc, 0:ntok], shT[:, kc, 0:ntok], ALU.add, [hT, shT], [hT])
        return hT

    def proj(hT, ntok):
        pr = pr_r.get()
        for cb, (c0, c1_) in enumerate(((0, 512), (512, 1024), (1024, 1536), (1536, 1816))):
            ps = psg.get()
            for kc in range(8):
                P.mm(ps[0:ntok, 0:c1_ - c0], hT[:, kc, 0:ntok], w_in_bf[:, kc, c0:c1_], kc == 0, kc == 7, [hT, w_in_bf], [ps])
            P.cp("act" if cb % 2 else "dve", pr[0:ntok, c0:c1_], ps[0:ntok, 0:c1_ - c0], [ps], [pr])
        return pr

    xg_r = ring(2, [4, 8 if True else 0]);

    def gelu_tanh(eng2, out_ap, out_t, x_ap, x_t, shape_tmp):
        t1, t2 = shape_tmp
        P.tt(eng2, t1, x_ap, x_ap, ALU.mult, [x_t], [t1.buf if isinstance(t1, T) else out_t])

    NQM = 128
    qT_r = ring(2, [8, 128], BF16, 64)
    Pt_r = ring(4, [4, 128], BF16)
    mk2_r = ring(2, [128], BF16)
    Ocat_r = ring(1, [3, 8, 65])
    o_r = ring(2, [512])
    imp_r = ring(2, [2, NBm]); sc_r = ring(2, [NBS]); wk_r = ring(2, [NBS]); m8_r = ring(2, [8]); sel_r = ring(2, [NBS])
    selT_r = ring(2, [128], BF16)
    rd_r = ring(2, [24]); cf_r = ring(2, [24]); tmpo_r = ring(1, [24, 64]); sg_r = ring(2, [24])
    VcA = A([NKTC, 2, 65], BF16)
    P.ms("pool", VcA[:], 1.0, [VcA])
    Hx_r = ring(2, [4, 8]); Hg_r = ring(2, [4, 8], BF16)
    Ht1_r = ring(2, [4, 8]); Ht2_r = ring(2, [4, 8])

    def compress(SC, XcT, col0, nb, n0):
        ps = psg.get()
        for kvg in range(4):
            for j in range(32):
                P.mm(ps[:, kvg * nb:(kvg + 1) * nb] if nb <= 128 else None, W1[:, kvg // 2, j, :],
                     XcT[:, kvg, col0 + j:col0 + j + 16 * (nb - 1) + 1:16], j == 0, j == 31, [W1, XcT], [ps]) if nb <= 128 else None
        return ps

    def compress_blocks(SC, XcT, col0, nb, n0):
        hx = Hx_r.get(); hg = Hg_r.get(); t1 = Ht1_r.get(); t2 = Ht2_r.get()
        if nb <= 128:
            ps = psg.get()
            for kvg in range(4):
                for j in range(32):
                    P.mm(ps[:, kvg * nb:(kvg + 1) * nb], W1[:, kvg // 2, j, :],
                         XcT[:, kvg, col0 + j:col0 + j + 16 * (nb - 1) + 1:16], j == 0, j == 31, [W1, XcT], [ps])
            for kv in range(2):
                P.act(hx[:, 2 * kv:2 * kv + 2, 0:nb], ps[:, 2 * kv * nb:(2 * kv + 2) * nb].rearrange("p (a b) -> p a b", a=2),
                      AF.Identity, bias=cbias[:, kv:kv + 1], reads=[ps, cbias], writes=[hx])
        else:
            for kvg in range(4):
                ps = psg.get()
                for j in range(32):
                    P.mm(ps[:, 0:nb], W1[:, kvg // 2, j, :],
                         XcT[:, kvg, col0 + j:col0 + j + 16 * (nb - 1) + 1:16], j == 0, j == 31, [W1, XcT], [ps])
                P.act(hx[:, kvg, 0:nb], ps[:, 0:nb], AF.Identity, bias=cbias[:, kvg // 2:kvg // 2 + 1], reads=[ps, cbias], writes=[hx])
        X = hx[:, :, 0:nb]
        P.tt("pool", t1[:, :, 0:nb], X, X, ALU.mult, [hx], [t1])
        P.ts("dve", t1[:, :, 0:nb], t1[:, :, 0:nb], 0.044715, ALU.mult, 1.0, ALU.add, [t1], [t1])
        P.tt("pool", t1[:, :, 0:nb], t1[:, :, 0:nb], X, ALU.mult, [t1, hx], [t1])
        P.act(t2[:, :, 0:nb], t1[:, :, 0:nb], AF.Sigmoid, scale=GC, reads=[t1], writes=[t2])
        P.tt("dve", hg[:, :, 0:nb], X, t2[:, :, 0:nb], ALU.mult, [hx, t2], [hg])
        for c in range(0, nb, 256):
            w = min(256, nb - c)
            ps = psg.get()
            for g in range(2):
                P.mm(ps[0:64, g * w:(g + 1) * w], W2[:, 0, :], hg[:, g, c:c + w], True, True, [W2, hg], [ps])
            P.cp("dve", SC["KcT"][:, :, n0 + c:n0 + c + w], ps[0:64, 0:2 * w].rearrange("p (a b) -> p a b", a=2), [ps], [SC["KcT"]])
        P.cp("pool", SC["GvT"][:, :, n0:n0 + nb], hg[:, 2:4, 0:nb], [hg], [SC["GvT"]])

    def attend(SC, qT, NQ, t, qpos0, nkt_c, slc_kts, win_kts, sig, prompt, getkv):
        Ocat = Ocat_r.get()
        NB = NBP if prompt else NBS
        NBmm = NBP if prompt else NBSm
        imp = imp_r.get()
        for ktc in range(nkt_c):
            ps = psg.get()
            for g in range(2):
                P.mm(ps[:, g * 64:(g + 1) * 64], SC["GvT"][:, g, ktc * 128:(ktc + 1) * 128], W2[:, 1, :], True, True,
                     [SC["GvT"], W2], [ps])
            P.cp("act", VcA[:, ktc, :, 0:64], ps[:, 0:128].rearrange("p (a b) -> p a b", a=2), [ps], [VcA])
        rd = rd_r.get()
        selTs = []
        for g in range(2):
            po1 = pso.get(); po2 = pso.get()
            for ktc in range(nkt_c):
                ps = psg.get()
                P.mm(ps[:, 0:4 * NQ], SC["KcT"][:, g, ktc * 128:(ktc + 1) * 128], qT[:, 4 * g:4 * g + 4, 0:NQ], True, True,
                     [SC["KcT"], qT], [ps])
                pt = Pt_r.get()
                P.act(pt[:, :, 0:NQ], ps[:, 0:4 * NQ].rearrange("p (a b) -> p a b", a=4), AF.Exp, reads=[ps], writes=[pt])
                base = qpos0 - 2048 * ktc - 31
                if base - 2032 < 0:
                    P.op("pool", lambda e, pt=pt, base=base: e.affine_select(
                        out=pt[:, :, 0:NQ], in_=pt[:, :, 0:NQ], pattern=[[0, 4], [1, NQ]], compare_op=ALU.is_ge, fill=0.0,
                        base=base, channel_multiplier=-16), [pt], [pt])
                for r in range(4):
                    P.mm(po1[0:NQ, r * 65:(r + 1) * 65], pt[:, r, 0:NQ], VcA[:, ktc, g, :], ktc == 0 and r == 0, ktc == nkt_c - 1, [pt, VcA], [po1])
                    P.mm(po2[0:NQ, r * NBmm:(r + 1) * NBmm], pt[:, r, 0:NQ], Mm[:, ktc, 0:NBmm], ktc == 0 and r == 0, ktc == nkt_c - 1, [pt, Mm], [po2])
            P.cp("act", Ocat[0:NQ, 0, 4 * g:4 * g + 4, :], po1[0:NQ, 0:260].rearrange("p (a b) -> p a b", a=4), [po1], [Ocat])
            P.ts("dve", rd[0:NQ, 4 * g:4 * g + 4], Ocat[0:NQ, 0, 4 * g:4 * g + 4, 64], 1e-30, ALU.max, reads=[Ocat], writes=[rd])
            P.op("dve", lambda e, g=g: e.reciprocal(out=rd[0:NQ, 4 * g:4 * g + 4], in_=rd[0:NQ, 4 * g:4 * g + 4]), [rd], [rd])
            for r in range(4):
                if r == 0:
                    P.ts("dve", imp[0:NQ, g, 0:NBmm], po2[0:NQ, 0:NBmm], rd[0:NQ, 4 * g:4 * g + 1], ALU.mult, reads=[po2, rd], writes=[imp])
                else:
                    P.stt("dve", imp[0:NQ, g, 0:NBmm], po2[0:NQ, r * NBmm:(r + 1) * NBmm], rd[0:NQ, 4 * g + r:4 * g + r + 1],
                          imp[0:NQ, g, 0:NBmm], ALU.mult, ALU.add, [po2, rd, imp], [imp])
            sc = sc_r.get(); wk = wk_r.get(); m8 = m8_r.get(); sel = sel_r.get()
            if prompt:
                P.tt("dve", sc[0:NQ, 0:NB], imp[0:NQ, g, 0:NB], Rtab[0:NQ, 63 - 2 * t:63 - 2 * t + NB], ALU.add, [imp, Rtab], [sc])
                P.ts("dve", sc[0:NQ, 0:1], sc[0:NQ, 0:1], BIG, ALU.add, reads=[sc], writes=[sc])
            else:
                P.tt("dve", sc[0:NQ, 0:NBmm], imp[0:NQ, g, 0:NBmm], sbias[0:NQ, 0:NBmm], ALU.add, [imp, sbias], [sc])
                P.cp("dve", sc[0:NQ, NBmm:NB], sbias[0:NQ, NBmm:NB], [sbias], [sc])
            cur = sc
            for rnd in range(NSEL // 8):
                P.op("dve", lambda e, cur=cur, m8=m8: e.max(out=m8[0:NQ, :], in_=cur[0:NQ, 0:NB]), [cur], [m8])
                if rnd < NSEL // 8 - 1:
                    P.op("dve", lambda e, cur=cur, m8=m8, wk=wk: e.match_replace(out=wk[0:NQ, 0:NB], in_to_replace=m8[0:NQ, :],
                                                                   in_values=cur[0:NQ, 0:NB], imm_value=-3.0e38), [cur, m8], [wk])
                    cur = wk
            P.ts("dve", sel[0:NQ, 0:NB], sc[0:NQ, 0:NB], m8[0:NQ, 7:8], ALU.is_ge, reads=[sc, m8], writes=[sel])
            ps = psg.get()
            P.tr(ps[0:NBmm, 0:NQ], sel[0:NQ, 0:NBmm], ident[0:NQ, 0:NQ], [sel, ident], [ps])
            selT = selT_r.get()
            P.cp("dve", selT[0:NBmm, 0:NQ], ps[0:NBmm, 0:NQ], [ps], [selT])
            selTs.append(selT)
        for br, kts in ((0, slc_kts), (1, win_kts)):
            pos = [pso.get(), pso.get()]
            steps = [(i, kt, mode, g) for i, (kt, mode) in enumerate(kts) for g in range(2)]
            kvc = {}

            def front(step, br=br, kvc=kvc):
                i, kt, mode, g = step
                if g == 0:
                    kvc[i] = getkv(br, kt)
                ktT, vaT = kvc[i]
                psm = None
                if mode in ("sel", "diag") and br == 0:
                    psm = psg.get()
                    P.mm(psm[:, 0:NQ], Eall[0:NBmm, kt, :], selTs[g][0:NBmm, 0:NQ], True, True, [Eall, selTs[g]], [psm])
                ps = psg.get()
                P.mm(ps[:, 0:4 * NQ], ktT[:, g, :], qT[:, 4 * g:4 * g + 4, 0:NQ], True, True, [ktT, qT], [ps])
                return ps, psm, vaT

            def back(step, fr, br=br, pos=pos, nsteps=len(kts)):
                i, kt, mode, g = step
                ps, psm, vaT = fr
                pt = Pt_r.get()
                P.act(pt[:, :, 0:NQ], ps[:, 0:4 * NQ].rearrange("p (a b) -> p a b", a=4), AF.Exp, reads=[ps], writes=[pt])
                if br == 0 and mode == "diag":
                    mk2 = mk2_r.get()
                    P.tt("dve", mk2[:, 0:NQ], psm[:, 0:NQ], tri[:, 0:NQ], ALU.mult, [psm, tri], [mk2])
                    P.tt("pool", pt[:, :, 0:NQ], pt[:, :, 0:NQ], mk2[:, 0:NQ].unsqueeze(1).to_broadcast([128, 4, NQ]), ALU.mult, [pt, mk2], [pt])
                elif br == 0 and mode == "sel":
                    P.tt("dve", pt[:, :, 0:NQ], pt[:, :, 0:NQ], psm[:, 0:NQ].unsqueeze(1).to_broadcast([128, 4, NQ]), ALU.mult, [pt, psm], [pt])
                elif br == 1 and mode in ("diag", "anti"):
                    m = tri if mode == "diag" else anti
                    P.tt("pool", pt[:, :, 0:NQ], pt[:, :, 0:NQ], m[:, 0:NQ].unsqueeze(1).to_broadcast([128, 4, NQ]), ALU.mult, [pt, m], [pt])
                for r in range(4):
                    P.mm(pos[g][0:NQ, r * 65:(r + 1) * 65], pt[:, r, 0:NQ], vaT[:, g, :], i == 0 and r == 0, i == nsteps - 1, [pt, vaT], [pos[g]])

            pending = front(steps[0])
            for n, step in enumerate(steps):
                nxt = front(steps[n + 1]) if n + 1 < len(steps) else None
                back(step, pending)
                pending = nxt
            for g in range(2):
                P.cp("act", Ocat[0:NQ, 1 + br, 4 * g:4 * g + 4, :], pos[g][0:NQ, 0:260].rearrange("p (a b) -> p a b", a=4), [pos[g]], [Ocat])
        rdd = rd_r.get(); cf = cf_r.get(); tmpo = tmpo_r.get(); o = o_r.get()
        P.ts("dve", rdd[0:NQ, :], Ocat[0:NQ, :, :, 64].rearrange("p a b -> p (a b)") if False else Ocat[0:NQ, :, :, 64],
             1e-30, ALU.max, reads=[Ocat], writes=[rdd]) if False else None
        oc_den = Ocat[0:NQ, :, :, 64]
        rdd3 = rdd[0:NQ, :].rearrange("p (a b) -> p a b", a=3)
        P.ts("dve", rdd3, oc_den, 1e-30, ALU.max, reads=[Ocat], writes=[rdd])
        P.op("dve", lambda e: e.reciprocal(out=rdd[0:NQ, :], in_=rdd[0:NQ, :]), [rdd], [rdd])
        P.tt("dve", cf[0:NQ, :], rdd[0:NQ, :], sig[0:NQ, :], ALU.mult, [rdd, sig], [cf])
        P.tt("dve", tmpo[0:NQ, :, :], Ocat[0:NQ, :, :, 0:64].rearrange("p a b c -> p (a b) c"),
             cf[0:NQ, :].unsqueeze(2).to_broadcast([NQ, 24, 64]), ALU.mult, [Ocat, cf], [tmpo])
        o3 = o[0:NQ, :].rearrange("p (a b) -> p a b", a=8)
        P.tt("pool", o3, tmpo[0:NQ, 0:8, :], tmpo[0:NQ, 8:16, :], ALU.add, [tmpo], [o])
        P.tt("pool", o3, o3, tmpo[0:NQ, 16:24, :], ALU.add, [o, tmpo], [o])
        return o

    def ssm(pr, ntok, sstate, mode):
        ps = psg.get()
        for kc in range(4):
            P.tr(ps[:, kc * ntok:(kc + 1) * ntok], pr[0:ntok, kc * 128:(kc + 1) * 128], ident[0:ntok, 0:ntok], [pr, ident], [ps])
        if cfg.get('cut') == 11:
            raise _Stop()
        uTf = uTf_r.get(); uTb = uTb_r.get()
        P.cp("act", uTf[:, :, 0:ntok], ps[:, 0:4 * ntok].rearrange("p (a b) -> p a b", a=4), [ps], [uTf])
        if cfg.get('cut') == 12:
            raise _Stop()
        P.cp("pool", uTb[:, :, 0:ntok], uTf[:, :, 0:ntok], [uTf], [uTb])
        if cfg.get('cut') == 1:
            raise _Stop()
        py = pso.get()
        if mode == "scan":
            for G in range(4):
                pre = psg.get(); pim = psg.get()
                for j in range(4):
                    gp = 4 * G + j
                    P.mm(pre[:, j * 128:(j + 1) * 128], Bw[:, gp, 0, :], uTb[:, G, 0:128], True, True, [Bw, uTb], [pre])
                    P.mm(pim[:, j * 128:(j + 1) * 128], Bw[:, gp, 1, :], uTb[:, G, 0:128], True, True, [Bw, uTb], [pim])
                pre3 = pre[:, 0:512].rearrange("p (a b) -> p a b", a=4); pim3 = pim[:, 0:512].rearrange("p (a b) -> p a b", a=4)
                ct4 = ctab[:, 4 * G:4 * G + 4, :]; st4 = stab2[:, 4 * G:4 * G + 4, 0, :]; mst4 = stab2[:, 4 * G:4 * G + 4, 1, :]
                TA = TA_r.get(); TB = TB_r.get(); W = W4_r.get(); R = R4_r.get(); SBF = SBF_r.get(); cin = cin_r.get()
                P.tt("dve", TA[:, 0, :, :], pre3, ct4, ALU.mult, [pre, ctab], [TA])
                P.tt("dve", TA[:, 1, :, :], pim3, ct4, ALU.mult, [pim, ctab], [TA])
                P.tt("dve", TB[:, 0, :, :], pim3, st4, ALU.mult, [pim, stab2], [TB])
                P.tt("dve", TB[:, 1, :, :], pre3, mst4, ALU.mult, [pre, stab2], [TB])
                P.tt("pool", W[:], TA[:], TB[:], ALU.add, [TA, TB], [W])
                ss4 = sstate[:, 4 * G:4 * G + 4, :, 0]
                P.tt("pool", cin[:], ss4, rho[:, 4 * G:4 * G + 4].unsqueeze(2).to_broadcast([128, 4, 2]), ALU.mult, [sstate, rho], [cin])
                W0 = W[:, :, :, 0].rearrange("p r j -> p j r")
                P.tt("pool", W0, W0, cin[:], ALU.add, [W, cin], [W])
                for ri in range(2):
                    P.op("dve", lambda e, ri=ri, G=G, W=W, R=R: e.tensor_tensor_scan(
                        out=R[:, ri, :, :].rearrange("p a b -> p (a b)"), data0=rhoB[:, 4 * G:4 * G + 4, :].rearrange("p a b -> p (a b)"),
                        data1=W[:, ri, :, :].rearrange("p a b -> p (a b)"), initial=0.0, op0=ALU.mult, op1=ALU.add), [rhoB, W], [R])
                P.tt("pool", TA[:], R[:], ct4.unsqueeze(1).to_broadcast([128, 2, 4, 128]), ALU.mult, [R, ctab], [TA])
                P.tt("dve", TB[:, 0, :, :], R[:, 1, :, :], mst4, ALU.mult, [R, stab2], [TB])
                P.tt("pool", TB[:, 1, :, :], R[:, 0, :, :], st4, ALU.mult, [R, stab2], [TB])
                P.tt("dve", SBF[:], TA[:], TB[:], ALU.add, [TA, TB], [SBF])
                P.tt("pool", ss4, TA[:, :, :, 127].rearrange("p r j -> p j r"), TB[:, :, :, 127].rearrange("p r j -> p j r"), ALU.add,
                     [TA, TB], [sstate])
                for j in range(4):
                    gp = 4 * G + j
                    for ri in range(2):
                        P.mm(py[:, G * 128:(G + 1) * 128], Cw[:, gp, ri, :], SBF[:, ri, j, :],
                             j == 0 and ri == 0, j == 3 and ri == 1, [Cw, SBF], [py])
        for gp in (range(16) if mode != "scan" else ()):
            pb = psg.get()
            for k3 in range(3):
                P.mm(pb[:, k3 * ntok:(k3 + 1) * ntok], Bw[:, gp, k3, :], uTb[:, gp // 4, 0:ntok], True, True, [Bw, uTb], [pb])
            b3 = pb[:, 0:3 * ntok].rearrange("p (a b) -> p a b", a=3)
            t1 = t1_r.get(); t2 = t2_r.get(); w = w_r.get(); r = r_r.get()
            if mode == "scan":
                ct_b = ctab[:, gp, 0:ntok].unsqueeze(1).to_broadcast([128, 2, ntok]); st_pre = stab2[:, gp, :, 0:ntok]
            else:
                ct_b = ctab[:, gp, 0:1].unsqueeze(1).to_broadcast([128, 2, ntok]); st_pre = stab2[:, gp, :, 0:1].to_broadcast([128, 2, ntok])
            P.tt("dve", t1[:, :, 0:ntok], b3[:, 0:2, :], ct_b, ALU.mult, [pb, ctab], [t1])
            P.tt("dve", t2[:, :, 0:ntok], b3[:, 1:3, :], st_pre, ALU.mult, [pb, stab2], [t2])
            P.tt("pool", w[:, :, 0:ntok], t1[:, :, 0:ntok], t2[:, :, 0:ntok], ALU.add, [t1, t2], [w])
            if cfg.get('cut') == 2 and gp == 0:
                raise _Stop()
            if mode == "scan":
                for ri in range(2):
                    P.op("dve", lambda e, ri=ri, gp=gp, w=w, r=r: e.tensor_tensor_scan(
                        out=r[:, ri, 0:ntok], data0=rhoB[:, gp, 0:ntok], data1=w[:, ri, 0:ntok],
                        initial=sstate[:, gp, ri, 0:1], op0=ALU.mult, op1=ALU.add), [rhoB, w, sstate], [r])
            else:
                P.stt("dve", r[:, :, 0:ntok], sstate[:, gp, :, 0:ntok], rho[:, gp:gp + 1], w[:, :, 0:ntok], ALU.mult, ALU.add,
                      [sstate, rho, w], [r])
            Ap = Apost_r.get(); Bp = Bpost_r.get(); sbf = sbf_r.get()
            if cfg.get('cut') == 3 and gp == 0:
                raise _Stop()
            if mode == "scan":
                s_sin = stab2[:, gp, 0, 0:ntok]; s_msin = stab2[:, gp, 1, 0:ntok]
            else:
                s_sin = stab2[:, gp, 0, 0:1].to_broadcast([128, ntok]); s_msin = stab2[:, gp, 1, 0:1].to_broadcast([128, ntok])
            P.tt("pool", Ap[:, :, 0:ntok], r[:, :, 0:ntok], ct_b, ALU.mult, [r, ctab], [Ap])
            P.tt("dve", Bp[:, 0, 0:ntok], r[:, 1, 0:ntok], s_msin, ALU.mult, [r, stab2], [Bp])
            P.tt("pool", Bp[:, 1, 0:ntok], r[:, 0, 0:ntok], s_sin, ALU.mult, [r, stab2], [Bp])
            P.tt("dve", sbf[:, :, 0:ntok], Ap[:, :, 0:ntok], Bp[:, :, 0:ntok], ALU.add, [Ap, Bp], [sbf])
            if cfg.get('cut') == 4 and gp == 0:
                raise _Stop()
            if mode == "scan":
                P.tt("pool", sstate[:, gp, :, 0:1], Ap[:, :, ntok - 1:ntok], Bp[:, :, ntok - 1:ntok], ALU.add, [Ap, Bp], [sstate])
            else:
                P.tt("pool", sstate[:, gp, :, 0:ntok], Ap[:, :, 0:ntok], Bp[:, :, 0:ntok], ALU.add, [Ap, Bp], [sstate])
            for ri in range(2):
                P.mm(py[:, (gp // 4) * ntok:(gp // 4 + 1) * ntok], Cw[:, gp, ri, :], sbf[:, ri, 0:ntok],
                     gp % 4 == 0 and ri == 0, gp % 4 == 3 and ri == 1, [Cw, sbf], [py])
        if cfg.get('cut') == 5:
            raise _Stop()
        yv = yv_r.get(); z = zz_r.get(); zb = zb_r.get(); g1 = g1_r.get(); g2 = g2_r.get()
        for kc in range(4):
            P.stt("dve", yv[:, kc, 0:ntok], uTf[:, kc, 0:ntok], dT[:, kc:kc + 1], py[:, kc * ntok:(kc + 1) * ntok], ALU.mult, ALU.add,
                  [uTf, dT, py], [yv])
        Y = yv[:, :, 0:ntok]
        P.tt("pool", g1[:, :, 0:ntok], Y, Y, ALU.mult, [yv], [g1])
        P.ts("dve", g1[:, :, 0:ntok], g1[:, :, 0:ntok], 0.044715, ALU.mult, 1.0, ALU.add, [g1], [g1])
        P.tt("pool", g1[:, :, 0:ntok], g1[:, :, 0:ntok], Y, ALU.mult, [g1, yv], [g1])
        P.act(g2[:, :, 0:ntok], g1[:, :, 0:ntok], AF.Sigmoid, scale=GC, reads=[g1], writes=[g2])
        P.tt("dve", z[:, :, 0:ntok], Y, g2[:, :, 0:ntok], ALU.mult, [yv, g2], [z])
        P.cp("pool", zb[:, :, 0:ntok], z[:, :, 0:ntok], [z], [zb])
        if cfg.get('cut') == 6:
            raise _Stop()
        pg = psg.get()
        for oc in range(4):
            for kc in range(4):
                P.mm(pg[:, oc * ntok:(oc + 1) * ntok], w_glu_bf[:, kc, oc * 128:(oc + 1) * 128], zb[:, kc, 0:ntok], kc == 0, kc == 3, [w_glu_bf, zb], [pg])
        P.act(g2[:, :, 0:ntok], pg[:, 0:4 * ntok].rearrange("p (a b) -> p a b", a=4), AF.Sigmoid, reads=[pg], writes=[g2])
        osm = os_r.get()
        P.tt("dve", osm[:, :, 0:ntok], z[:, :, 0:ntok], g2[:, :, 0:ntok], ALU.mult, [z, g2], [osm])
        return osm

    def out_proj(osm, o, xt, ntok, gt1_ap, gt_t):
        sq = sq_r.get(); mixS = mixS_r.get()
        P.tt("pool", sq[:, :, 0:ntok], osm[:, :, 0:ntok], osm[:, :, 0:ntok], ALU.mult, [osm], [sq])
        pss = psg.get()
        for kc in range(4):
            P.mm(pss[0:ntok, 0:1], sq[:, kc, 0:ntok], ones[:, 0:1], kc == 0, kc == 3, [sq, ones], [pss])
        s = st4.get()
        P.ts("dve", s[0:ntok, 1:2], pss[0:ntok, 0:1], 1.0 / 512, ALU.mult, EPS, ALU.add, [pss], [s])
        P.act(s[0:ntok, 2:3], s[0:ntok, 1:2], AF.Sqrt, reads=[s], writes=[s])
        P.op("dve", lambda e: e.reciprocal(out=s[0:ntok, 3:4], in_=s[0:ntok, 2:3]), [s], [s])
        for kc in range(4):
            P.ts("dve", mixS[:, kc, 0:ntok], osm[:, kc, 0:ntok], gST[:, kc:kc + 1], ALU.mult, reads=[osm, gST], writes=[mixS])
        acc = acc_r.get()
        for half in range(2):
            ps = psg.get()
            for kc in range(4):
                P.mm(ps[0:ntok, :], mixS[:, kc, 0:ntok], w_out_bf[:, kc, half * 512:(half + 1) * 512], kc == 0, kc == 3, [mixS, w_out_bf], [ps])
            P.ts("dve", acc[0:ntok, half * 512:(half + 1) * 512], ps[0:ntok, :], s[0:ntok, 3:4], ALU.mult, reads=[ps, s], writes=[acc])
        sa = rstd_of(o[0:ntok, :], o, ntok, 512)
        on = on_r.get()
        P.ts("dve", on[0:ntok, :], o[0:ntok, :], sa[0:ntok, 3:4], ALU.mult, reads=[o, sa], writes=[on])
        ps = psg.get()
        for kc in range(4):
            P.tr(ps[:, kc * ntok:(kc + 1) * ntok], on[0:ntok, kc * 128:(kc + 1) * 128], ident[0:ntok, 0:ntok], [on, ident], [ps])
        mixA = mixA_r.get()
        for kc in range(4):
            P.ts("dve", mixA[:, kc, 0:ntok], ps[:, kc * ntok:(kc + 1) * ntok], gAT[:, kc:kc + 1], ALU.mult, reads=[ps, gAT], writes=[mixA])
        x1 = x1_r.get()
        for half in range(2):
            ps = psg.get()
            for kc in range(4):
                P.mm(ps[0:ntok, :], mixA[:, kc, 0:ntok], w_out_bf[:, 4 + kc, half * 512:(half + 1) * 512], kc == 0, kc == 3, [mixA, w_out_bf], [ps])
            sl = slice(half * 512, (half + 1) * 512)
            P.tt("dve", acc[0:ntok, sl], acc[0:ntok, sl], ps[0:ntok, :], ALU.add, [acc, ps], [acc])
            P.tt("pool", acc[0:ntok, sl], acc[0:ntok, sl], gt1_ap[:, sl], ALU.mult, [acc, gt_t], [acc])
            P.tt("pool", x1[0:ntok, sl], acc[0:ntok, sl], xt[0:ntok, sl], ALU.add, [acc, xt], [x1])
        return x1

    def bcast_rows(dst, row, which):
        P.ms("pool", selr[:], 0.0, [selr])
        P.op("pool", lambda e: e.affine_select(out=selr[:], in_=ones[0:NS, :], pattern=[[0, 128]], compare_op=ALU.is_equal,
                                               fill=0.0, base=-row, channel_multiplier=1), [ones], [selr])
        for half in range(2):
            ps = psg.get()
            P.mm(ps[:, :], selr[:], gt_tm[:, which, half * 512:(half + 1) * 512], True, True, [selr, gt_tm], [ps])
            P.cp("dve", dst[:, half * 512:(half + 1) * 512], ps[:, :], [ps], [dst])

    sig_r = ring(2, [24])

    def make_qT(pr, ntok):
        ps = psg.get(); ps2 = psg.get()
        for h in range(8):
            pp = ps if h < 4 else ps2
            P.tr(pp[0:64, (h % 4) * ntok:(h % 4 + 1) * ntok], pr[0:ntok, 512 + h * 64:512 + (h + 1) * 64], ident[0:ntok, 0:ntok], [pr, ident], [pp])
        qT = qT_r.get()
        P.act(qT[:, 0:4, 0:ntok], ps[0:64, 0:4 * ntok].rearrange("p (a b) -> p a b", a=4), AF.Copy, scale=0.125, reads=[ps], writes=[qT])
        P.act(qT[:, 4:8, 0:ntok], ps2[0:64, 0:4 * ntok].rearrange("p (a b) -> p a b", a=4), AF.Copy, scale=0.125, reads=[ps2], writes=[qT])
        sig = sig_r.get()
        P.act(sig[0:ntok, :], pr[0:ntok, 1792:1816], AF.Sigmoid, reads=[pr], writes=[sig])
        return qT, sig

    _k = "ExternalOutput" if cfg.get("dbg") else "Internal"
    o_d = T(nc.dram_tensor("o_scratch", [B * Tq + DB, 512], F32, kind=_k).ap())
    u_d = T(nc.dram_tensor("u_scratch", [B * Tq + DB, 512], F32, kind=_k).ap())
    x1s_d = T(nc.dram_tensor("x1s_scratch", [DB, 1024], F32, kind="Internal").ap())
    mark_state = off[0]
    KT = A([2, 2, NT * 128], BF16, 64); VA = A([2, NT, 2, 65], BF16)
    KcT = A([2, NKTP * 128], BF16, 64); GvT = A([2, NKTP * 128], BF16)
    XcT = A([4, 144], BF16, 64)
    SCp = dict(KT=KT, VA=VA, KcT=KcT, GvT=GvT)

    def getkv_p(br, kt):
        return T(KT[:, br, :, kt * 128:(kt + 1) * 128], KT.buf), T(VA[:, br, kt, :, :], VA.buf)

    for b in range(B):
        P.ms("pool", KcT[:], 0.0, [KcT]); P.ms("pool", GvT[:], 0.0, [GvT]); P.ms("pool", XcT[:], 0.0, [XcT])
        P.ms("pool", VA[:], 1.0, [VA])
        for t in range(NT):
            r0 = b * Tq + t * 128
            xt = xt_r.get()
            P.dma(xt[:], x_p.ap[r0:r0 + 128, :], [x_p], [xt])
            hT = norm_mod_T(xt, 128, a1T, sh1T, DB + b)
            pr = proj(hT, 128)
            P.dma(cmp_p.ap[r0:r0 + 128, :], pr[:, 1024:1280], [pr], [cmp_p])
            P.dma(slc_p.ap[r0:r0 + 128, :], pr[:, 1280:1536], [pr], [slc_p])
            P.dma(u_d.ap[r0:r0 + 128, :], pr[:, 0:512], [pr], [u_d])
            if t >= NT - 4:
                w0 = b * 512 + (t - (NT - 4)) * 128
                P.dma(win_p.ap[w0:w0 + 128, :], pr[:, 1536:1792], [pr], [win_p])
            P.cp("pool", XcT[:, :, 0:16], XcT[:, :, 128:144], [XcT], [XcT])
            ps = psg.get()
            for kvg in range(4):
                P.tr(ps[0:64, kvg * 128:(kvg + 1) * 128], pr[:, 1024 + kvg * 64:1024 + (kvg + 1) * 64], ident[:], [pr, ident], [ps])
            P.cp("dve", XcT[:, :, 16:144], ps[0:64, :].rearrange("p (a b) -> p a b", a=4), [ps], [XcT])
            ps = psg.get()
            for br in range(2):
                for g in range(2):
                    c0 = 1280 + br * 256 + g * 64
                    P.tr(ps[0:64, (br * 2 + g) * 128:(br * 2 + g + 1) * 128], pr[:, c0:c0 + 64], ident[:], [pr, ident], [ps])
            P.cp("act", KT[:, :, :, t * 128:(t + 1) * 128], ps[0:64, :].rearrange("p (a b c) -> p a b c", a=2, b=2), [ps], [KT])
            for br in range(2):
                c0 = 1280 + br * 256 + 128
                P.cp("pool", VA[:, br, t, :, 0:64], pr[:, c0:c0 + 128].rearrange("p (a b) -> p a b", a=2), [pr], [VA])
            if t == 0:
                compress_blocks(SCp, XcT, 16, 7, 0)
            else:
                compress_blocks(SCp, XcT, 0, 8, 8 * t - 1)
            qT, sig = make_qT(pr, 128)
            nkt_c = (8 * t + 7 + 127) // 128
            slc_kts = [(kt, "diag" if kt == t else "sel") for kt in range(t + 1)]
            win_kts = []
            for kt in range(max(0, t - 4), t + 1):
                win_kts.append((kt, "diag" if kt == t else ("anti" if kt == t - 4 else "none")))
            o = attend(SCp, qT, 128, t, 128 * t, nkt_c, slc_kts, win_kts, sig, True, getkv_p)
            P.dma(o_d.ap[r0:r0 + 128, :], o[:], [o], [o_d])
    P.barrier()
    if stop == 'A':
        return finish_prog()
    P.barrier()
    off[0] = mark_state
    GPG = min(16, NPG); NGRP = NPG // GPG
    Hx_r = ring(1, [4, GPG * 8]); Hg_r = ring(1, [4, GPG * 8], BF16); Ht1_r = ring(1, [4, GPG * 8]); Ht2_r = ring(1, [4, GPG * 8])
    KcT_s = A([2, NKTS * 128], BF16, 64); GvT_s = A([2, NKTS * 128], BF16)
    XcW = A([4, 16 + GPG * 128], BF16, 64)
    SCs = dict(KcT=KcT_s, GvT=GvT_s)
    pg_r = ring(3, [256]); ktile_r = ring(3, [2, 128], BF16, 64); vtile_r = ring(3, [2, 65], BF16)
    knew = A([2, 2, 128], BF16, 64); vnew = A([2, 2, 65], BF16)
    kTn = A([4, DB], BF16, 64)
    vrow = A([DB, 2, 128], F32, 1); sigrow = A([DB, 24], F32, 1)
    pti = A([DB * NPG], I32); ptf = A([DB * NPG]); idx_all = A([DB * NPG], I32); pcol = A([1])
    P.dma(pti[:], ptab.ap.rearrange("n o -> (n o)").partition_broadcast(128), [ptab], [pti])
    P.op("pool", lambda e: e.iota(pcol[:], pattern=[[0, 1]], base=0, channel_multiplier=1,
                                  allow_small_or_imprecise_dtypes=True), [], [pcol])
    P.cp("dve", ptf[:], pti[:], [pti], [ptf])
    P.ts("dve", ptf[:], ptf[:], 128.0, ALU.mult, pcol[:, 0:1], ALU.add, [ptf, pcol], [ptf])
    P.cp("dve", idx_all[:], ptf[:], [ptf], [idx_all])
    xs = xt_r.get()
    P.dma(xs[0:DB, :], x_s.ap, [x_s], [xs])
    hTs = norm_mod_T(xs, DB, a1T, sh1T, None)
    prs = proj(hTs, DB)
    P.dma(cmp_s.ap, prs[0:DB, 1024:1280], [prs], [cmp_s])
    P.dma(slc_s.ap, prs[0:DB, 1280:1536], [prs], [slc_s])
    P.dma(u_d.ap[B * Tq:B * Tq + DB, :], prs[0:DB, 0:512], [prs], [u_d])
    P.dma(win_s.ap.rearrange("(i r) c -> i r c", r=512)[:, 511, :], prs[0:DB, 1536:1792], [prs], [win_s])
    P.dma(win_s.ap.rearrange("(i r) c -> i r c", r=512)[:, 0:511, :], state_win.ap.rearrange("(i r) c -> i r c", r=512)[:, 1:512, :],
          [state_win], [win_s])
    qTs, sigs = make_qT(prs, DB)
    ps = psg.get()
    for br in range(2):
        for g in range(2):
            c0 = 1280 + br * 256 + g * 64
            P.tr(ps[0:64, (br * 2 + g) * DB:(br * 2 + g + 1) * DB], prs[0:DB, c0:c0 + 64], ident[0:DB, 0:DB], [prs, ident], [ps])
    P.cp("dve", kTn[:], ps[0:64, 0:4 * DB].rearrange("p (a b) -> p a b", a=4), [ps], [kTn])
    for i in range(DB):
        P.dma(sigrow[0:1, i, :], sigs[i:i + 1, :], [sigs], [sigrow])
        for br in range(2):
            c0 = 1280 + br * 256 + 128
            P.dma(vrow[0:1, i, br, :], prs[i:i + 1, c0:c0 + 128], [prs], [vrow])
    for tl in vtile_r.items:
        P.ms("pool", tl[:], 1.0, [tl])
    for i in range(DB):
        P.ms("pool", KcT_s[:], 0.0, [KcT_s]); P.ms("pool", GvT_s[:], 0.0, [GvT_s]); P.ms("pool", XcW[:], 0.0, [XcW])
        for G in range(NGRP):
            if G > 0:
                P.cp("pool", XcW[:, :, 0:16], XcW[:, :, GPG * 128:GPG * 128 + 16], [XcW], [XcW])
            for jp in range(GPG):
                j = G * GPG + jp
                pg = pg_r.get()
                P.dma(pg[:], cache_cmp.ap, [cache_cmp, idx_all], [pg], q="pool", indirect=idx_all[:, i * NPG + j:i * NPG + j + 1])
                ps = psg.get()
                for kvg in range(4):
                    P.tr(ps[0:64, kvg * 128:(kvg + 1) * 128], pg[:, kvg * 64:(kvg + 1) * 64], ident[:], [pg, ident], [ps])
                P.cp("dve" if jp % 2 else "act", XcW[:, :, 16 + jp * 128:16 + (jp + 1) * 128], ps[0:64, :].rearrange("p (a b) -> p a b", a=4), [ps], [XcW])
            if G == 0:
                compress_blocks(SCs, XcW, 16, GPG * 8 - 1, 0)
            else:
                compress_blocks(SCs, XcW, 0, GPG * 8, G * GPG * 8 - 1)
        P.ms("pool", knew[:], 0.0, [knew]); P.ms("pool", vnew[:], 0.0, [vnew])
        for br in range(2):
            P.cp("dve", knew[:, br, :, 0], kTn[:, 2 * br:2 * br + 2, i], [kTn], [knew])
            P.cp("dve", vnew[0:1, br, :, 0:64], vrow[0:1, i, br, :].rearrange("p (a b) -> p a b", a=2), [vrow], [vnew])
            P.ms("pool", vnew[0:1, br, :, 64:65], 1.0, [vnew])

        def getkv_s(br, kt, i=i):
            if (br == 0 and kt == NPG) or (br == 1 and kt == 4):
                return T(knew[:, br, :, :], knew.buf), T(vnew[:, br, :, :], vnew.buf)
            pg = pg_r.get()
            if br == 0:
                P.dma(pg[:], cache_slc.ap, [cache_slc, idx_all], [pg], q="pool", indirect=idx_all[:, i * NPG + kt:i * NPG + kt + 1])
            else:
                r0 = i * 512 + kt * 128
                P.dma(pg[:], state_win.ap[r0:r0 + 128, :], [state_win], [pg])
            ps = psg.get()
            for g in range(2):
                P.tr(ps[0:64, g * 128:(g + 1) * 128], pg[:, g * 64:(g + 1) * 64], ident[:], [pg, ident], [ps])
            ktl = ktile_r.get(); vtl = vtile_r.get()
            P.cp("act", ktl[:], ps[0:64, 0:256].rearrange("p (a b) -> p a b", a=2), [ps], [ktl])
            P.cp("pool", vtl[:, :, 0:64], pg[:, 128:256].rearrange("p (a b) -> p a b", a=2), [pg], [vtl])
            if br == 1 and kt == 0:
                P.ms("pool", vtl[0:1, :, :], 0.0, [vtl])
            elif br == 1 and kt == 1:
                P.ms("pool", vtl[0:1, :, 64:65], 1.0, [vtl])
            return ktl, vtl

        qTi = T(qTs[:, :, i:i + 1], qTs.buf)
        sigi = T(sigrow[0:1, i, :], sigrow.buf)
        slc_kts = [(kt, "sel") for kt in range(NPG)] + [(NPG, "none")]
        win_kts = [(kt, "none") for kt in range(5)]
        o = attend(SCs, qTi, 1, 0, PAST, NKTS, slc_kts, win_kts, sigi, False, getkv_s)
        P.dma(o_d.ap[B * Tq + i:B * Tq + i + 1, :], o[0:1, :], [o], [o_d])
    P.barrier()
    P.barrier()
    off[0] = mark_persist
    w_out_bf = A([8, 1024], BF16); w_glu_bf = A([4, 512], BF16)
    P.dma(w_out_bf[:], w_out.ap.rearrange("(kc p) n -> p kc n", p=128), [w_out], [w_out_bf], q="pool")
    P.dma(w_glu_bf[:], w_glu.ap.rearrange("(kc p) n -> p kc n", p=128), [w_glu], [w_glu_bf], q="pool")
    lreT = A([16]); limT = A([16]); dtT = A([16]); rho = A([16]); th = A([16])
    for vec, dst in ((lam_re, lreT), (lam_im, limT)):
        ps = loadT(vec, 16, dst)
        P.cp("dve", dst[:], ps[:, 0:16], [ps], [dst])
    with nc.allow_non_contiguous_dma(reason="tiny log_dt broadcast"):
        for g2 in range(2):
            P.dma(dtT[g2 * 64:(g2 + 1) * 64, :], log_dt.ap.rearrange("(gp g2) -> g2 gp", g2=2)[g2:g2 + 1, :].partition_broadcast(64)
                  if False else bass.AP(tensor=log_dt.ap.tensor, offset=g2, ap=[[0, 64], [2, 16]]), [log_dt], [dtT], slow=True)
    P.act(dtT[:], dtT[:], AF.Exp, reads=[dtT], writes=[dtT])
    P.tt("dve", rho[:], lreT[:], dtT[:], ALU.mult, [lreT, dtT], [rho])
    P.tt("dve", th[:], limT[:], dtT[:], ALU.mult, [limT, dtT], [th])
    P.act(rho[:], rho[:], AF.Exp, reads=[rho], writes=[rho])
    ctab = A([16, 128]); stab2 = A([16, 2, 128])
    Bw = A([16, 3, 128], BF16); Cw = A([16, 2, 128], BF16)
    mark_tabs = off[0]
    iot = A([128]); ph = A([16, 128]); phf = A([16, 128]); phi = A([16, 128], I32)
    P.op("pool", lambda e: e.iota(iot[:], pattern=[[1, 128]], base=1, channel_multiplier=0,
                                  allow_small_or_imprecise_dtypes=True), [], [iot])

    def sin_table(dst_ap, dst_t, phase_turns):
        for gp in range(16):
            P.ts("dve", ph[:, gp, :], iot[:], th[:, gp:gp + 1], ALU.mult, 1.0 / (2 * math.pi), ALU.mult, [iot, th], [ph])
        P.ts("dve", ph[:], ph[:], phase_turns, ALU.add, reads=[ph], writes=[ph])
        P.cp("dve", phi[:], ph[:], [ph], [phi])
        P.cp("dve", phf[:], phi[:], [phi], [phf])
        P.tt("dve", ph[:], ph[:], phf[:], ALU.subtract, [ph, phf], [ph])
        P.ts("dve", phf[:], ph[:], 0.5, ALU.is_gt, reads=[ph], writes=[phf])
        P.tt("dve", ph[:], ph[:], phf[:], ALU.subtract, [ph, phf], [ph])
        P.ts("dve", phf[:], ph[:], -0.5, ALU.is_lt, reads=[ph], writes=[phf])
        P.tt("dve", ph[:], ph[:], phf[:], ALU.add, [ph, phf], [ph])
        P.act(dst_ap, ph[:], AF.Sin, scale=2 * math.pi, reads=[ph], writes=[dst_t])

    sin_table(ctab[:], ctab, 0.25)
    sin_table(stab2[:, :, 0, :], stab2, 0.0)
    P.ts("dve", stab2[:, :, 1, :], stab2[:, :, 0, :], -1.0, ALU.mult, reads=[stab2], writes=[stab2])
    c1 = A([16]); s1 = A([16]); nre = A([16]); nim = A([16]); l2 = A([16]); kre = A([16]); kim = A([16]); tmpk = A([16])
    P.cp("dve", c1[:], ctab[:, :, 0], [ctab], [c1]); P.cp("dve", s1[:], stab2[:, :, 0, 0], [stab2], [s1])
    P.tt("dve", nre[:], rho[:], c1[:], ALU.mult, [rho, c1], [nre]); P.ts("dve", nre[:], nre[:], -1.0, ALU.add, reads=[nre], writes=[nre])
    P.tt("dve", nim[:], rho[:], s1[:], ALU.mult, [rho, s1], [nim])
    P.tt("dve", l2[:], lreT[:], lreT[:], ALU.mult, [lreT], [l2]); P.tt("dve", tmpk[:], limT[:], limT[:], ALU.mult, [limT], [tmpk])
    P.tt("dve", l2[:], l2[:], tmpk[:], ALU.add, [l2, tmpk], [l2]); P.op("dve", lambda e: e.reciprocal(out=l2[:], in_=l2[:]), [l2], [l2])
    P.tt("dve", kre[:], nre[:], lreT[:], ALU.mult, [nre, lreT], [kre]); P.tt("dve", tmpk[:], nim[:], limT[:], ALU.mult, [nim, limT], [tmpk])
    P.tt("dve", kre[:], kre[:], tmpk[:], ALU.add, [kre, tmpk], [kre]); P.tt("dve", kre[:], kre[:], l2[:], ALU.mult, [kre, l2], [kre])
    P.tt("dve", kim[:], nim[:], lreT[:], ALU.mult, [nim, lreT], [kim]); P.tt("dve", tmpk[:], nre[:], limT[:], ALU.mult, [nre, limT], [tmpk])
    P.tt("dve", kim[:], kim[:], tmpk[:], ALU.subtract, [kim, tmpk], [kim]); P.tt("dve", kim[:], kim[:], l2[:], ALU.mult, [kim, l2], [kim])
    Bre = A([16, 16]); Bim = A([16, 16]); Bbr = A([16, 16]); Bbi = A([16, 16]); tB = A([16, 16])
    for src, dst in ((b_re, Bre), (b_im, Bim)):
        for g2 in range(2):
            P.dma(dst[g2 * 64:(g2 + 1) * 64, :, :],
                  bass.AP(tensor=src.ap.tensor, offset=g2 * 1024, ap=[[16, 64], [2048, 16], [1, 16]]), [src], [dst])
    kre_b = kre[:].unsqueeze(2).to_broadcast([128, 16, 16]); kim_b = kim[:].unsqueeze(2).to_broadcast([128, 16, 16])
    P.tt("dve", Bbr[:], Bre[:], kre_b, ALU.mult, [Bre, kre], [Bbr]); P.tt("dve", tB[:], Bim[:], kim_b, ALU.mult, [Bim, kim], [tB])
    P.tt("dve", Bbr[:], Bbr[:], tB[:], ALU.subtract, [Bbr, tB], [Bbr])
    P.tt("dve", Bbi[:], Bim[:], kre_b, ALU.mult, [Bim, kre], [Bbi]); P.tt("dve", tB[:], Bre[:], kim_b, ALU.mult, [Bre, kim], [tB])
    P.tt("dve", Bbi[:], Bbi[:], tB[:], ALU.add, [Bbi, tB], [Bbi])
    Bpad = A([128]);
    for ri, Bb in enumerate((Bbr, Bbi)):
        for gp in range(16):
            c0 = 32 * (gp % 4)
            P.ms("pool", Bpad[:], 0.0, [Bpad])
            for g2 in range(2):
                P.cp("pool", Bpad[g2 * 64:(g2 + 1) * 64, c0 + 16 * g2:c0 + 16 * g2 + 16], Bb[g2 * 64:(g2 + 1) * 64, gp, :], [Bb], [Bpad])
            ps = psg.get()
            P.tr(ps[:, 0:128], Bpad[:], ident[:], [Bpad, ident], [ps])
            P.cp("dve", Bw[:, gp, ri, :], ps[:, 0:128], [ps], [Bw])
            if ri == 0:
                P.cp("act", Bw[:, gp, 2, :], ps[:, 0:128], [ps], [Bw])
    Cre = A([16, 16]); Cim = A([16, 16])
    with nc.allow_non_contiguous_dma(reason="small C transpose load"):
        for src, dst in ((c_re, Cre), (c_im, Cim)):
            for g2 in range(2):
                for gp in range(16):
                    P.dma(dst[g2 * 64:(g2 + 1) * 64, gp, :],
                          bass.AP(tensor=src.ap.tensor, offset=g2 * 1024 + gp * 2048, ap=[[1, 64], [64, 16]]), [src], [dst], q="pool", slow=True)
    P.ms("pool", Cw[:], 0.0, [Cw])
    for gp in range(16):
        c0 = 32 * (gp % 4)
        for g2 in range(2):
            sl = slice(g2 * 64, (g2 + 1) * 64)
            P.cp("dve", Cw[sl, gp, 0, c0 + 16 * g2:c0 + 16 * g2 + 16], Cre[sl, gp, :], [Cre], [Cw])
            P.ts("dve", Cw[sl, gp, 1, c0 + 16 * g2:c0 + 16 * g2 + 16], Cim[sl, gp, :], -1.0, ALU.mult, reads=[Cim], writes=[Cw])
    P.barrier()
    off[0] = mark_tabs

    uTf_r = ring(1, [4, 128]); uTb_r = ring(2, [4, 128], BF16)
    t1_r = ring(1, [2, 128]); t2_r = ring(1, [2, 128]); w_r = ring(1, [2, 128]); r_r = ring(1, [2, 128])
    Apost_r = ring(1, [2, 128]); Bpost_r = ring(1, [2, 128]); sbf_r = ring(2, [2, 128], BF16)
    TA_r = ring(1, [2, 4, 128]); TB_r = ring(1, [2, 4, 128]); W4_r = ring(2, [2, 4, 128]); R4_r = ring(2, [2, 4, 128])
    SBF_r = ring(2, [2, 4, 128], BF16); cin_r = ring(2, [4, 2])
    yv_r = ring(1, [4, 128]); zz_r = ring(1, [4, 128]); zb_r = ring(2, [4, 128], BF16); g1_r = ring(1, [4, 128]); g2_r = ring(1, [4, 128])
    os_r = ring(1, [4, 128]); sq_r = ring(1, [4, 128]); mixS_r = ring(2, [4, 128], BF16); mixA_r = ring(2, [4, 128], BF16)
    acc_r = ring(1, [1024]); on_r = ring(1, [512]); x1_r = ring(1, [1024])

    rhoB = A([16, 128])
    P.cp("dve", rhoB[:], rho[:].unsqueeze(2).to_broadcast([128, 16, 128]), [rho], [rhoB])
    P.ms("dve", rhoB[:, :, 0:1], 0.0, [rhoB])
    if stop == 'B0':
        return finish_prog()
    xt_r = ring(2, [1024]); junk = A([1024]); st4 = ring(4, [4]); ub_r = ring(2, [512]); ob_r = ring(2, [512])
    sst = A([16, 2, 1]); gtbc = A([2, 1024]); stmp = A([16]); srow = A([128], F32, 16)
    for b in range(B):
        P.ms("pool", sst[:], 0.0, [sst])
        for which in range(1):
            bcast_rows(T(gtbc[:, which, :], gtbc.buf), DB + b, which)
        if stop == 'B1':
            return finish_prog()
        for t in range(NT):
            r0 = b * Tq + t * 128
            xt = xt_r.get(); ub = ub_r.get(); ob = ob_r.get()
            P.dma(xt[:], x_p.ap[r0:r0 + 128, :], [x_p], [xt])
            if cfg.get("dummy") == 5:
                P.dma(ub[:], x_p.ap[r0:r0 + 128, 0:512], [x_p], [ub])
                P.dma(ob[:], x_p.ap[r0:r0 + 128, 512:1024], [x_p], [ob])
            else:
                P.dma(ub[:], u_d.ap[r0:r0 + 128, :], [u_d], [ub])
                P.dma(ob[:], o_d.ap[r0:r0 + 128, :], [o_d], [ob])
            if cfg.get('cut') == 10:
                return finish_prog()
            try:
                osm = ssm(ub, 128, sst, "scan")
            except _Stop:
                return finish_prog()
            if stop == 'B2':
                return finish_prog()
            x1 = out_proj(osm, ob, xt, 128, gtbc[:, 0, :], gtbc)
            if stop == 'B3':
                return finish_prog()
            P.dma(x1_d.ap[r0:r0 + 128, :], x1[:], [x1], [x1_d])
        for ri, dst in enumerate((sre_p, sim_p)):
            ps = psg.get()
            P.cp("dve", stmp[:], sst[:, :, ri, 0], [sst], [stmp])
            P.tr(ps[0:16, 0:128], stmp[:], ident[:], [stmp, ident], [ps])
            P.cp("dve", srow[:], ps[0:16, 0:128], [ps], [srow])
            P.dma(dst.ap[b:b + 1, :].rearrange("o (gp q) -> (o gp) q", q=128), srow[:], [srow], [dst])
    if stop == 'B':
        return finish_prog()
    sst_s = A([16, 2, DB]); sin_t = A([2, 2048], F32, DB); sout_t = sin_t
    P.dma(sin_t[:, 0, :], sre_in.ap, [sre_in], [sin_t])
    P.dma(sin_t[:, 1, :], sim_in.ap, [sim_in], [sin_t])
    for ri in range(2):
        for gq in range(4):
            ps = psg.get()
            for j in range(4):
                gp = gq * 4 + j
                P.tr(ps[:, j * DB:(j + 1) * DB], sin_t[:, ri, gp * 128:(gp + 1) * 128], ident[0:DB, 0:DB], [sin_t, ident], [ps])
            P.cp("dve", sst_s[:, gq * 4:gq * 4 + 4, ri, :], ps[:, 0:4 * DB].rearrange("p (a b) -> p a b", a=4), [ps], [sst_s])
    xs = xt_r.get(); ub = ub_r.get(); ob = ob_r.get()
    P.dma(xs[0:DB, :], x_s.ap, [x_s], [xs])
    P.dma(ub[0:DB, :], u_d.ap[B * Tq:B * Tq + DB, :], [u_d], [ub])
    P.dma(ob[0:DB, :], o_d.ap[B * Tq:B * Tq + DB, :], [o_d], [ob])
    osm = ssm(ub, DB, sst_s, "step")
    for ri, dst in enumerate((sre_s, sim_s)):
        for gq in range(4):
            ps = psg.get()
            for j in range(4):
                gp = gq * 4 + j
                P.tr(ps[0:DB, j * 128:(j + 1) * 128], sst_s[:, gp, ri, :], ident[:], [sst_s, ident], [ps])
            P.cp("dve", sout_t[:, ri, gq * 512:(gq + 1) * 512], ps[0:DB, :], [ps], [sout_t])
        P.dma(dst.ap, sout_t[:, ri, :], [sout_t], [dst])
    x1s = out_proj(osm, ob, xs, DB, gt_tm[0:DB, 0, :], gt_tm)
    P.dma(x1s_d.ap, x1s[0:DB, :], [x1s], [x1s_d])
    P.barrier()
    off[0] = mark_persist
    w_up_bf = A([8, 4096], BF16); w_dn_bf = A([32, 1024], BF16)
    for c in range(4):
        P.dma(w_up_bf[:, :, c * 1024:(c + 1) * 1024], w_up.ap[:, c * 1024:(c + 1) * 1024].rearrange("(kc p) n -> p kc n", p=128), [w_up], [w_up_bf], q="pool")
        P.dma(w_dn_bf[:, c * 8:(c + 1) * 8, :], w_down.ap[c * 1024:(c + 1) * 1024, :].rearrange("(kc p) n -> p kc n", p=128), [w_down], [w_dn_bf], q="pool")
    nf_bc = A([1024]); gtbc = A([1024])
    P.dma(nf_bc[:], norm_final.ap.partition_broadcast(128), [norm_final], [nf_bc])
    xt_r = ring(2, [1024]); xn_r = ring(1, [1024]); junk = A([1024]); st4 = ring(4, [4]); hT_r = ring(2, [8, 128], BF16)
    aT_r = ring(2, [32, 128], BF16); rl_r = ring(2, [128]); x2_r = ring(2, [1024])

    def mlp_tile(xt, ntok, seq, gt2_ap, gt_t, dst_ap, dst_t):
        hT = norm_mod_T(xt, ntok, a2T, sh2T, seq)
        aT = aT_r.get()
        for fc in range(32):
            ps = psg.get()
            for kc in range(8):
                P.mm(ps[:, 0:ntok], w_up_bf[:, kc, fc * 128:(fc + 1) * 128], hT[:, kc, 0:ntok], kc == 0, kc == 7, [w_up_bf, hT], [ps])
            rl = rl_r.get()
            P.act(rl[:, 0:ntok], ps[:, 0:ntok], AF.Relu, reads=[ps], writes=[rl])
            P.tt("pool" if fc % 2 else "dve", aT[:, fc, 0:ntok], rl[:, 0:ntok], rl[:, 0:ntok], ALU.mult, [rl], [aT])
        x2 = x2_r.get()
        for half in range(2):
            ps = psg.get()
            sl = slice(half * 512, (half + 1) * 512)
            for fc in range(32):
                P.mm(ps[0:ntok, :], aT[:, fc, 0:ntok], w_dn_bf[:, fc, sl], fc == 0, fc == 31, [aT, w_dn_bf], [ps])
            P.tt("dve", x2[0:ntok, sl], ps[0:ntok, :], gt2_ap[:, sl], ALU.mult, [ps, gt_t], [x2])
            P.tt("pool", x2[0:ntok, sl], x2[0:ntok, sl], xt[0:ntok, sl], ALU.add, [x2, xt], [x2])
        s = rstd_of(x2[0:ntok, :], x2, ntok, 1024)
        P.stt("dve", x2[0:ntok, :], x2[0:ntok, :], s[0:ntok, 3:4], nf_bc[0:ntok, :], ALU.mult, ALU.mult, [x2, s, nf_bc], [x2])
        P.dma(dst_ap, x2[0:ntok, :], [x2], [dst_t])

    for b in range(B):
        bcast_rows(gtbc, DB + b, 1)
        for t in range(NT):
            r0 = b * Tq + t * 128
            xt = xt_r.get()
            P.dma(xt[:], x1_d.ap[r0:r0 + 128, :], [x1_d], [xt])
            mlp_tile(xt, 128, DB + b, gtbc[:], gtbc, y_p.ap[r0:r0 + 128, :], y_p)
    xs = xt_r.get()
    P.dma(xs[0:DB, :], x1s_d.ap, [x1s_d], [xs])
    mlp_tile(xs, DB, None, gt_tm[0:DB, 1, :], gt_tm, y_s.ap, y_s)
    P.barrier()
    deps = {}
    for o_ in outs_all:
        if o_.buf.lw is not None:
            deps[o_.buf.lw[0]] = max(deps.get(o_.buf.lw[0], 0), o_.buf.lw[1])
    P._waits('sp', deps)
    P.emit(st)
    st.close()
    return nc


def _run(inp, cfg, n_cores):
    B, Tq, DB, NPG = cfg["B"], cfg["T"], cfg["DB"], cfg["NPG"]
    f = lambda a: np.ascontiguousarray(np.asarray(a, dtype=np.float32))
    nc = build(cfg)
    shared = {
        "cache_cmp": f(inp["cache_cmp"][0]).reshape(-1, 256), "cache_slc": f(inp["cache_slc"][0]).reshape(-1, 256),
        "w_ada": f(inp["w_ada"][0]), "b_ada": f(inp["b_ada"][0]), "norm_attn": f(inp["norm_attn"][0]), "w_in": f(inp["w_in"][0]),
        "lam_re": f(inp["ssm_lambda_re"][0]).reshape(-1), "lam_im": f(inp["ssm_lambda_im"][0]).reshape(-1),
        "log_dt": f(inp["ssm_log_dt"][0]), "b_re": f(inp["ssm_b_re"][0]).reshape(-1), "b_im": f(inp["ssm_b_im"][0]).reshape(-1),
        "c_re": f(inp["ssm_c_re"][0]).reshape(-1), "c_im": f(inp["ssm_c_im"][0]).reshape(-1), "ssm_d": f(inp["ssm_d"][0]),
        "w_glu": f(inp["ssm_w_glu"][0]), "pe_k": f(inp["cmp_pe_k"][0]).reshape(-1), "w1_k": f(inp["cmp_w1_k"][0]).reshape(2048, 128),
        "w2_k": f(inp["cmp_w2_k"][0]), "pe_v": f(inp["cmp_pe_v"][0]).reshape(-1), "w1_v": f(inp["cmp_w1_v"][0]).reshape(2048, 128),
        "w2_v": f(inp["cmp_w2_v"][0]), "n_ssm": f(inp["norm_out_ssm"][0]), "n_attn": f(inp["norm_out_attn"][0]),
        "w_out": f(inp["w_out"][0]), "norm_mlp": f(inp["norm_mlp"][0]), "w_up": f(inp["w_up"][0]), "w_down": f(inp["w_down"][0]),
        "norm_final": f(inp["norm_final"]),
    }
    used = set(a.memorylocations[0].name for a in nc.allocations if getattr(a, "kind", None) == "ExternalInput")
    maps = []
    for c in range(n_cores):
        m = dict(shared)
        m["x_p"] = f(inp["x_prompt"][c * B:(c + 1) * B]).reshape(B * Tq, 1024)
        m["x_s"] = f(inp["x_sample"][c * DB:(c + 1) * DB]).reshape(DB, 1024)
        m["c_all"] = np.concatenate([f(inp["c_sample"][c * DB:(c + 1) * DB]), f(inp["c_prompt"][c * B:(c + 1) * B])], axis=0)
        m["state_win"] = f(inp["state_win"][0, c * DB:(c + 1) * DB]).reshape(DB * 512, 256)
        m["ssm_re_in"] = f(inp["state_ssm_re"][0, c * DB:(c + 1) * DB]).reshape(DB, 2048)
        m["ssm_im_in"] = f(inp["state_ssm_im"][0, c * DB:(c + 1) * DB]).reshape(DB, 2048)
        m["ptab"] = np.ascontiguousarray(np.asarray(inp["page_table"][c * DB:(c + 1) * DB], dtype=np.int32)).reshape(DB * NPG, 1)
        maps.append({k: v for k, v in m.items() if k in used})
    res = run_bass_kernel_spmd(nc, maps, core_ids=list(range(n_cores))).results
    cat = lambda k: np.concatenate([r[k] for r in res], axis=0)
    NB = n_cores * B
    ND = n_cores * DB
    return (cat("y_p").reshape(NB, Tq, 1024), cat("y_s").reshape(ND, 1, 1024),
            cat("cmp_p").reshape(1, NB, Tq, 2, 2, 64), cat("slc_p").reshape(1, NB, Tq, 2, 2, 64),
            cat("win_p").reshape(1, NB, 512, 2, 2, 64),
            cat("sre_p").reshape(1, NB, 32, 64), cat("sim_p").reshape(1, NB, 32, 64),
            cat("cmp_s").reshape(1, ND, 1, 2, 2, 64), cat("slc_s").reshape(1, ND, 1, 2, 2, 64),
            cat("win_s").reshape(1, ND, 512, 2, 2, 64),
            cat("sre_s").reshape(1, ND, 32, 64), cat("sim_s").reshape(1, ND, 32, 64))


def kernel(**inputs):
    cfg = dict(B=2, T=4096, DB=16, NPG=64, NPHYS=10240, NSEL=16)
    return _run(inputs, cfg, 8)
```

```python
import math
from contextlib import ExitStack
import numpy as np
import concourse.bass as bass
import concourse.mybir as mybir
from concourse.bass_utils import run_bass_kernel_spmd

F32 = mybir.dt.float32
BF16 = mybir.dt.bfloat16
I32 = mybir.dt.int32
AF = mybir.ActivationFunctionType
ALU = mybir.AluOpType

NLANES = 10
INS_LINES = {}
NSW = 4
PE_NOP_AFTER_WAIT = False
BIG = 1.0e4
EPS = 1e-6
GC = 1.5957691216057308


class Buf:
    __slots__ = ("lw", "rd")

    def __init__(self):
        self.lw = None
        self.rd = {}


class T:
    def __init__(self, ap, buf=None):
        self.ap = ap
        self.buf = buf if buf is not None else Buf()

    def __getitem__(self, k):
        return self.ap[k]


def _bufs(xs):
    return [x.buf if isinstance(x, T) else x for x in xs]


class Prog:
    def __init__(self, nc):
        self.nc = nc
        self.units = ["pe", "act", "dve", "pool"] + ["L%d" % i for i in range(NLANES)] + ["G%d" % i for i in range(NSW)]
        self.q = {e: [] for e in ("pe", "act", "dve", "pool", "sp")}
        self.cnt = {u: 0 for u in self.units}
        self.seen = {e: {u: 0 for u in self.units} for e in self.q}
        self.lane_rr = 0
        self.sw_rr = 0
        self.n_ins = 0

    def _deps(self, reads, writes):
        deps = {}
        for b in reads:
            if b.lw is not None and deps.get(b.lw[0], 0) < b.lw[1]:
                deps[b.lw[0]] = b.lw[1]
        for b in writes:
            if b.lw is not None and deps.get(b.lw[0], 0) < b.lw[1]:
                deps[b.lw[0]] = b.lw[1]
            for u, n in b.rd.items():
                if deps.get(u, 0) < n:
                    deps[u] = n
        return deps

    def _waits(self, q, deps, me=None, raw=False):
        for u, n in deps.items():
            if u == me and not raw:
                continue
            if self.seen[q][u] >= n:
                continue
            self.seen[q][u] = n
            self.q[q].append(("w", u, n * 16 if u[0] in "LG" else n))

    def op(self, eng, fn, reads=(), writes=()):
        reads = _bufs(reads)
        writes = _bufs(writes)
        deps = self._deps(reads, writes)
        raw = eng != "pe"
        nq0 = len(self.q[eng])
        self._waits(eng, deps, me=eng, raw=raw)
        if eng == "pe" and len(self.q[eng]) > nq0 and PE_NOP_AFTER_WAIT:
            self.q[eng].append(("n",))
        self.cnt[eng] += 1
        n = self.cnt[eng]
        import sys as _s
        fr = _s._getframe(1)
        while fr.f_code.co_name in ('op','mm','tr','act','tt','ts','stt','cp','ms'):
            fr = fr.f_back
        self.q[eng].append(("i", fn, eng, fr.f_lineno))
        self.n_ins += 1
        for b in reads:
            if b.rd.get(eng, 0) < n:
                b.rd[eng] = n
        for b in writes:
            b.lw = (eng, n)
            b.rd = {}

    def dma(self, out, in_, reads=(), writes=(), q="sp", indirect=None, slow=False):
        reads = _bufs(reads)
        writes = _bufs(writes)
        if q == "pool":
            lane = "G%d" % self.sw_rr
            self.sw_rr = (self.sw_rr + 1) % NSW
        else:
            lane = "L%d" % self.lane_rr
            self.lane_rr = (self.lane_rr + 1) % NLANES
        deps = self._deps(reads, writes)
        if self.cnt[lane] > 0:
            deps[lane] = max(deps.get(lane, 0), self.cnt[lane])
        self._waits(q, deps)
        self.cnt[lane] += 1
        n = self.cnt[lane]
        if indirect is not None:
            fn = lambda e: e.indirect_dma_start(out=out, out_offset=None, in_=in_,
                                                in_offset=bass.IndirectOffsetOnAxis(ap=indirect, axis=0))
        else:
            fn = (lambda e: e.dma_start(out=out, in_=in_, allow_slow_non_contiguous=True)) if slow else (lambda e: e.dma_start(out=out, in_=in_))
        import sys as _s
        fr = _s._getframe(1)
        self.q[q].append(("i", fn, lane, fr.f_lineno))
        self.n_ins += 1
        for b in reads:
            if b.rd.get(lane, 0) < n:
                b.rd[lane] = n
        for b in writes:
            b.lw = (lane, n)
            b.rd = {}

    def barrier(self):
        deps = {u: n for u, n in self.cnt.items() if n > 0}
        for q in self.q:
            self._waits(q, dict(deps))

    def _pe_mode(self, mode):
        self.pe_mode = mode

    @staticmethod
    def _r(n):
        return 32 if n <= 32 else (64 if n <= 64 else 128)

    def mm(self, out, lhsT, rhs, start=True, stop=True, reads=(), writes=()):
        self._pe_mode(("mm", self._r(lhsT.shape[0]), self._r(int(np.prod(lhsT.shape[1:]))), str(lhsT.dtype)))
        self.op("pe", lambda e: e.matmul(out, lhsT=lhsT, rhs=rhs, start=start, stop=stop,
                                         skip_group_check=True), reads, writes)

    def tr(self, out, in_, ident, reads=(), writes=()):
        self._pe_mode(("tr", self._r(in_.shape[0]), self._r(int(np.prod(in_.shape[1:])))))
        self.op("pe", lambda e: e.transpose(out, in_, ident), reads, writes)

    def act(self, out, in_, func, scale=1.0, bias=0.0, accum=None, reads=(), writes=()):
        if accum is None:
            self.op("act", lambda e: e.activation(out=out, in_=in_, func=func, scale=scale, bias=bias),
                    reads, writes)
        else:
            self.op("act", lambda e: e.activation(out=out, in_=in_, func=func, scale=scale, bias=bias,
                                                  accum_out=accum), reads, writes)

    def tt(self, eng, out, in0, in1, op, reads=(), writes=()):
        self.op(eng, lambda e: e.tensor_tensor(out=out, in0=in0, in1=in1, op=op), reads, writes)

    def ts(self, eng, out, in0, s1, op0, s2=None, op1=None, reads=(), writes=()):
        if op1 is None:
            self.op(eng, lambda e: e.tensor_scalar(out=out, in0=in0, scalar1=s1, scalar2=None, op0=op0),
                    reads, writes)
        else:
            self.op(eng, lambda e: e.tensor_scalar(out=out, in0=in0, scalar1=s1, scalar2=s2, op0=op0, op1=op1),
                    reads, writes)

    def stt(self, eng, out, in0, scalar, in1, op0, op1, reads=(), writes=()):
        self.op(eng, lambda e: e.scalar_tensor_tensor(out=out, in0=in0, scalar=scalar, in1=in1, op0=op0, op1=op1),
                reads, writes)

    def cp(self, eng, out, in_, reads=(), writes=()):
        if eng == "act":
            self.op("act", lambda e: e.copy(out=out, in_=in_), reads, writes)
        else:
            self.op(eng, lambda e: e.tensor_copy(out=out, in_=in_), reads, writes)

    def ms(self, eng, ap, val, writes=()):
        self.op(eng, lambda e: e.memset(ap, val), (), writes)

    def emit(self, stack):
        nc = self.nc
        sems = {u: stack.enter_context(nc.semaphore("s_" + u)) for u in self.units}
        block = stack.enter_context(nc.Block())

        def run(eng, items):
            for it in items:
                if it[0] == "w":
                    eng.wait_ge(sems[it[1]], it[2])
                elif it[0] == "n":
                    eng.nop(nofuse=True)
                else:
                    bi = it[1](eng)
                    bi.then_inc(sems[it[2]], 16 if it[2][0] in "LG" else 1)
                    if len(it) > 3:
                        try:
                            INS_LINES[str(bi.ins.name)] = it[3]
                        except Exception:
                            pass

        @block.tensor
        def _(e):
            run(e, self.q["pe"])

        @block.scalar
        def _(e):
            run(e, self.q["act"])

        @block.vector
        def _(e):
            run(e, self.q["dve"])

        @block.gpsimd
        def _(e):
            run(e, self.q["pool"])

        @block.sync
        def _(e):
            run(e, self.q["sp"])


class _Stop(Exception):
    pass


class Ring:
    def __init__(self, items):
        self.items = items
        self.i = 0

    def get(self):
        t = self.items[self.i]
        self.i = (self.i + 1) % len(self.items)
        return t


def build(cfg):
    B, Tq, DB, NPG, NPHYS, NSEL = cfg["B"], cfg["T"], cfg["DB"], cfg["NPG"], cfg["NPHYS"], cfg["NSEL"]
    NT = Tq // 128
    PAST = NPG * 128
    NS = DB + B
    NBP = Tq // 64
    NBS = PAST // 64 + 1
    NBSm = NBS - 1
    NCS = PAST // 16 - 1
    assert NBP <= 128 and NBSm <= 128 and PAST >= 512 and Tq >= 512
    nc = bass.Bass("TRN2", target_bir_lowering=False)
    P = Prog(nc)
    global _LASTP
    _LASTP = P

    def din(name, shape, dt=F32):
        return T(nc.dram_tensor(name, shape, dt, kind="ExternalInput").ap())

    def dout(name, shape):
        return T(nc.dram_tensor(name, shape, F32, kind="ExternalOutput").ap())

    x_p = din("x_p", [B * Tq, 1024]); x_s = din("x_s", [DB, 1024]); c_all = din("c_all", [NS, 1024])
    cache_cmp = din("cache_cmp", [NPHYS * 128, 256]); cache_slc = din("cache_slc", [NPHYS * 128, 256])
    state_win = din("state_win", [DB * 512, 256])
    sre_in = din("ssm_re_in", [DB, 2048]); sim_in = din("ssm_im_in", [DB, 2048])
    ptab = din("ptab", [DB * NPG, 1], I32)
    w_ada = din("w_ada", [1024, 6144]); b_ada = din("b_ada", [6144]); norm_attn = din("norm_attn", [1024])
    w_in = din("w_in", [1024, 1816]); lam_re = din("lam_re", [2048]); lam_im = din("lam_im", [2048])
    log_dt = din("log_dt", [32]); b_re = din("b_re", [32 * 64 * 16]); b_im = din("b_im", [32 * 64 * 16])
    c_re = din("c_re", [32 * 16 * 64]); c_im = din("c_im", [32 * 16 * 64]); ssm_d = din("ssm_d", [512])
    w_glu = din("w_glu", [512, 512])
    pe_k = din("pe_k", [2048]); w1_k = din("w1_k", [2048, 128]); w2_k = din("w2_k", [128, 64])
    pe_v = din("pe_v", [2048]); w1_v = din("w1_v", [2048, 128]); w2_v = din("w2_v", [128, 64])
    n_ssm = din("n_ssm", [512]); n_attn = din("n_attn", [512]); w_out = din("w_out", [1024, 1024])
    norm_mlp = din("norm_mlp", [1024]); w_up = din("w_up", [1024, 4096]); w_down = din("w_down", [4096, 1024])
    norm_final = din("norm_final", [1024])

    y_p = dout("y_p", [B * Tq, 1024]); y_s = dout("y_s", [DB, 1024])
    cmp_p = dout("cmp_p", [B * Tq, 256]); slc_p = dout("slc_p", [B * Tq, 256]); win_p = dout("win_p", [B * 512, 256])
    sre_p = dout("sre_p", [B, 2048]); sim_p = dout("sim_p", [B, 2048])
    cmp_s = dout("cmp_s", [DB, 256]); slc_s = dout("slc_s", [DB, 256]); win_s = dout("win_s", [DB * 512, 256])
    sre_s = dout("sre_s", [DB, 2048]); sim_s = dout("sim_s", [DB, 2048])
    x1_d = T(nc.dram_tensor("x1_scratch", [B * Tq, 1024], F32, kind="ExternalOutput" if cfg.get("dbg") else "Internal").ap())
    outs_all = [y_p, y_s, cmp_p, slc_p, win_p, sre_p, sim_p, cmp_s, slc_s, win_s, sre_s, sim_s]

    st = ExitStack()
    ARENA_W = 53000
    arena = st.enter_context(nc.sbuf_tensor("arena", [128, ARENA_W], F32))
    off = [0]

    def A(shape, dt=F32, npart=128):
        n = int(np.prod(shape))
        w = n if dt != BF16 else (n + 1) // 2
        w = (w + 1) // 2 * 2
        assert off[0] + w <= ARENA_W, ("arena overflow", off[0], w)
        a = arena[0:npart, off[0]:off[0] + w]
        off[0] += w
        if dt != F32:
            a = a.bitcast(dt)
        a = a[:, 0:n]
        if len(shape) == 2:
            a = a.rearrange("p (a b) -> p a b", a=shape[0])
        elif len(shape) == 3:
            a = a.rearrange("p (a b c) -> p a b c", a=shape[0], b=shape[1])
        elif len(shape) == 4:
            a = a.rearrange("p (a b c d) -> p a b c d", a=shape[0], b=shape[1], c=shape[2])
        return T(a)

    def ring(n, shape, dt=F32, npart=128):
        return Ring([A(shape, dt, npart) for _ in range(n)])

    psb = [T(st.enter_context(nc.psum_tensor("psb%d" % i, [128, 512], F32))[:]) for i in range(8)]
    psg = Ring(psb[0:2])
    psa = Ring(psb[2:6])
    pso = Ring(psb[6:8])

    def finish_prog():
        P.barrier()
        deps = {}
        for o_ in outs_all:
            if o_.buf.lw is not None:
                deps[o_.buf.lw[0]] = max(deps.get(o_.buf.lw[0], 0), o_.buf.lw[1])
        P._waits('sp', deps)
        P.emit(st)
        st.close()
        return nc

    stop = cfg.get("stop", "")
    ident = A([128]); ones = A([128]); tri = A([128], BF16); anti = A([128], BF16)
    P.ms("pool", ones[:], 1.0, [ones])
    P.ms("pool", ident[:], 1.0, [ident])
    P.op("pool", lambda e: e.affine_select(out=ident[:], in_=ident[:], pattern=[[1, 128]], compare_op=ALU.is_equal,
                                           fill=0.0, base=0, channel_multiplier=-1), [ident], [ident])
    P.op("pool", lambda e: e.affine_select(out=tri[:], in_=ones[:], pattern=[[1, 128]], compare_op=ALU.is_ge,
                                           fill=0.0, base=0, channel_multiplier=-1), [ones], [tri])
    P.op("pool", lambda e: e.affine_select(out=anti[:], in_=ones[:], pattern=[[-1, 128]], compare_op=ALU.is_gt,
                                           fill=0.0, base=0, channel_multiplier=1), [ones], [anti])

    def loadT(vec, k, dst):
        tmp = A([128], F32)
        P.dma(tmp[0:k, :], vec.ap.rearrange("(k p) -> k p", p=128), [vec], [tmp])
        ps = psg.get()
        P.tr(ps[:, 0:k], tmp[0:k, :], ident[0:k, 0:k], [tmp, ident], [ps])
        return ps

    nattnT = A([8]); nmlpT = A([8]); dT = A([4]); gST = A([4]); gAT = A([4])
    for vec, k, dst in ((norm_attn, 8, nattnT), (norm_mlp, 8, nmlpT), (ssm_d, 4, dT), (n_ssm, 4, gST), (n_attn, 4, gAT)):
        ps = loadT(vec, k, dst)
        P.cp("dve", dst[:], ps[:, 0:k], [ps], [dst])
    nf_bc = A([1024])
    P.dma(nf_bc[:], norm_final.ap.partition_broadcast(128), [norm_final], [nf_bc])

    if stop == 'c0':
        return finish_prog()
    a1T = A([8, NS]); sh1T = A([8, NS]); a2T = A([8, NS]); sh2T = A([8, NS]); gt_tm = A([2, 1024], F32, NS); selr = A([128], F32, NS)
    mark_persist = off[0]
    csb = A([1024], F32, NS); csil = A([1024], F32, NS); cT = A([8, NS], BF16)
    P.dma(csb[:], c_all.ap, [c_all], [csb])
    P.act(csil[:], csb[:], AF.Silu, reads=[csb], writes=[csil])
    ps = psg.get()
    for kc in range(8):
        P.tr(ps[:, kc * NS:(kc + 1) * NS], csil[:, kc * 128:(kc + 1) * 128], ident[0:NS, 0:NS], [csil, ident], [ps])
    P.cp("dve", cT[:], ps[:, 0:8 * NS].rearrange("p (a b) -> p a b", a=8), [ps], [cT])
    wblk = ring(2, [8, 512], BF16); bbc = ring(2, [512], F32, NS); modb = ring(2, [512], F32, NS)
    featdst = {0: sh1T, 1: a1T, 3: sh2T, 4: a2T}
    for cb in range(12):
        wb = wblk.get(); bb = bbc.get(); mb = modb.get()
        P.dma(wb[:], w_ada.ap[:, cb * 512:(cb + 1) * 512].rearrange("(kc p) n -> p kc n", p=128), [w_ada], [wb], q="pool")
        P.dma(bb[:], b_ada.ap[cb * 512:(cb + 1) * 512].partition_broadcast(NS), [b_ada], [bb])
        ps = psg.get()
        for kc in range(8):
            P.mm(ps[0:NS, :], cT[:, kc, :], wb[:, kc, :], kc == 0, kc == 7, [cT, wb], [ps])
        which, half = cb // 2, cb % 2
        if which in (2, 5):
            P.tt("dve", gt_tm[:, 0 if which == 2 else 1, half * 512:(half + 1) * 512], ps[0:NS, :], bb[:], ALU.add,
                 [ps, bb], [gt_tm])
        else:
            P.tt("dve", mb[:], ps[0:NS, :], bb[:], ALU.add, [ps, bb], [mb])
            ps2 = psg.get()
            for j in range(4):
                P.tr(ps2[:, j * NS:(j + 1) * NS], mb[:, j * 128:(j + 1) * 128], ident[0:NS, 0:NS], [mb, ident], [ps2])
            dst = featdst[which]
            P.cp("dve", dst[:, half * 4:half * 4 + 4, :], ps2[:, 0:4 * NS].rearrange("p (a b) -> p a b", a=4), [ps2], [dst])
    for kc in range(8):
        P.ts("dve", a1T[:, kc, :], a1T[:, kc, :], 1.0, ALU.add, nattnT[:, kc:kc + 1], ALU.mult, [a1T, nattnT], [a1T])
        P.ts("dve", a2T[:, kc, :], a2T[:, kc, :], 1.0, ALU.add, nmlpT[:, kc:kc + 1], ALU.mult, [a2T, nmlpT], [a2T])
    P.barrier()
    off[0] = mark_persist

    if stop == 'mod':
        return finish_prog()
    w_in_bf = A([8, 1816], BF16)
    W1 = A([2, 32, 128], BF16, 64); W2 = A([2, 64], BF16)
    for kc in range(8):
        P.dma(w_in_bf[:, kc, :], w_in.ap[kc * 128:(kc + 1) * 128, :], [w_in], [w_in_bf], q="pool")
    for kv, (w1, w2) in enumerate(((w1_k, w2_k), (w1_v, w2_v))):
        P.dma(W1[:, kv, :, :], w1.ap.rearrange("(j d) h -> d j h", d=64), [w1], [W1], q="pool")
        P.dma(W2[:, kv, :], w2.ap, [w2], [W2], q="pool")
    peT = A([2, 32], BF16, 64); cbias = A([2])
    with nc.allow_non_contiguous_dma(reason="tiny pe transpose load"):
        for kv, pe in enumerate((pe_k, pe_v)):
            P.dma(peT[:, kv, :], pe.ap.rearrange("(j d) -> d j", d=64), [pe], [peT], q="pool", slow=True)
    ps = psg.get()
    for kv in range(2):
        for j in range(32):
            P.mm(ps[:, kv:kv + 1], W1[:, kv, j, :], peT[:, kv, j:j + 1], j == 0, j == 31, [W1, peT], [ps])
    P.cp("dve", cbias[:], ps[:, 0:2], [ps], [cbias])

    if stop == 'w':
        return finish_prog()
    NKTP = max(1, (Tq // 16 + 127) // 128)
    NKTS = (NCS + 127) // 128
    NKTC = max(NKTP, NKTS)
    NBm = max(NBP, NBSm)
    Mm = A([NKTC, NBm], BF16)
    NKE = max(NT, NPG)
    Eall = A([NKE, 128], BF16)
    Rtab = A([128]); sbias = A([NBS], F32, 1)
    mark_seltmp = off[0]
    mA = A([NKTC, NBm]); mB = A([NKTC, NBm]); etmp = A([NKE, 128]); rel = A([128]); r2 = A([128])
    P.ms("pool", mA[:], 1.0, [mA]); P.ms("pool", mB[:], 1.0, [mB])
    for (tm, lo, hi) in ((mA, -1, 3), (mB, 0, 2)):
        P.op("pool", lambda e, tm=tm, lo=lo: e.affine_select(out=tm[:], in_=tm[:], pattern=[[128, NKTC], [-4, NBm]],
                                                             compare_op=ALU.is_ge, fill=0.0, base=-lo, channel_multiplier=1), [tm], [tm])
        P.op("pool", lambda e, tm=tm, hi=hi: e.affine_select(out=tm[:], in_=tm[:], pattern=[[-128, NKTC], [4, NBm]],
                                                             compare_op=ALU.is_ge, fill=0.0, base=hi, channel_multiplier=-1), [tm], [tm])
    P.tt("dve", Mm[:], mA[:], mB[:], ALU.add, [mA, mB], [Mm])
    P.ms("pool", etmp[:], 1.0, [etmp])
    P.op("pool", lambda e: e.affine_select(out=etmp[:], in_=etmp[:], pattern=[[128, NKE], [1, 128]], compare_op=ALU.is_ge,
                                           fill=0.0, base=0, channel_multiplier=-64), [etmp], [etmp])
    P.op("pool", lambda e: e.affine_select(out=etmp[:], in_=etmp[:], pattern=[[-128, NKE], [-1, 128]], compare_op=ALU.is_ge,
                                           fill=0.0, base=63, channel_multiplier=64), [etmp], [etmp])
    P.cp("dve", Eall[:], etmp[:], [etmp], [Eall])
    P.op("pool", lambda e: e.iota(rel[:], pattern=[[1, 128]], base=-63, channel_multiplier=0,
                                  allow_small_or_imprecise_dtypes=True), [], [rel])
    P.op("pool", lambda e: e.affine_select(out=r2[:], in_=ones[:], pattern=[[0, 128]], compare_op=ALU.is_ge,
                                           fill=0.0, base=-64, channel_multiplier=1), [ones], [r2])
    P.tt("dve", rel[:], rel[:], r2[:], ALU.subtract, [rel, r2], [rel])
    P.ts("dve", Rtab[:], rel[:], -1.0, ALU.is_ge, BIG, ALU.mult, [rel], [Rtab])
    P.ts("dve", r2[:], rel[:], 1.0, ALU.is_ge, -2 * BIG, ALU.mult, [rel], [r2])
    P.tt("dve", Rtab[:], Rtab[:], r2[:], ALU.add, [Rtab, r2], [Rtab])
    P.ms("pool", sbias[:], 0.0, [sbias])
    for j in (0, NBS - 2, NBS - 1):
        P.ms("pool", sbias[:, j:j + 1], BIG, [sbias])
    P.barrier()
    off[0] = mark_seltmp
    mark_mixer = off[0]

    if stop == 'setup':
        return finish_prog()
    xt_r = ring(2, [1024]); xn_r = ring(1, [1024]); junk = A([1024]); st4 = ring(4, [4])
    hT_r = ring(2, [8, 128], BF16); pr_r = ring(2, [1816])

    def rstd_of(src_ap, src_t, ntok, width):
        s = st4.get()
        P.act(junk[0:ntok, 0:width], src_ap, AF.Square, accum=s[0:ntok, 0:1], reads=[src_t], writes=[junk, s])
        P.ts("dve", s[0:ntok, 1:2], s[0:ntok, 0:1], 1.0 / width, ALU.mult, EPS, ALU.add, [s], [s])
        P.act(s[0:ntok, 2:3], s[0:ntok, 1:2], AF.Sqrt, reads=[s], writes=[s])
        P.op("dve", lambda e: e.reciprocal(out=s[0:ntok, 3:4], in_=s[0:ntok, 2:3]), [s], [s])
        return s

    def norm_mod_T(xt, ntok, aT, shT, seq):
        s = rstd_of(xt[0:ntok, :], xt, ntok, 1024)
        xn = xn_r.get()
        P.ts("dve", xn[0:ntok, :], xt[0:ntok, :], s[0:ntok, 3:4], ALU.mult, reads=[xt, s], writes=[xn])
        hT = hT_r.get()
        if cfg.get("dummy", 0) == 3:
            for _ in range(3):
                P.cp("dve", junk[0:ntok, 0:1024], xn[0:ntok, :], [xn], [junk, xn])
        for half in range(2):
            ps = psg.get()
            if half == 0 and cfg.get("dummy", 0) == 1:
                P.tr(ps[:, 0:128], ident[:], ident[:], [ident], [ps])
            for j in range(4):
                kc = half * 4 + j
                P.tr(ps[:, j * ntok:(j + 1) * ntok], xn[0:ntok, kc * 128:(kc + 1) * 128], ident[0:ntok, 0:ntok], [xn, ident], [ps])
            for j in range(4):
                kc = half * 4 + j
                if seq is not None and cfg.get("dummy", 0) == 2:
                    P.ts("dve", hT[:, kc, 0:ntok], ps[:, j * ntok:(j + 1) * ntok], aT[:, kc, seq:seq + 1], ALU.mult,
                         shT[:, kc, seq:seq + 1], ALU.add, [ps, aT, shT], [hT])
                elif seq is not None:
                    P.act(hT[:, kc, 0:ntok], ps[:, j * ntok:(j + 1) * ntok], AF.Identity, scale=aT[:, kc, seq:seq + 1],
                          bias=shT[:, kc, seq:seq + 1], reads=[ps, aT, shT], writes=[hT])
                else:
                    P.tt("dve", hT[:, kc, 0:ntok], ps[:, j * ntok:(j + 1) * ntok], aT[:, kc, 0:ntok], ALU.mult, [ps, aT], [hT])
                    P.tt("dve", hT[:, kc, 0:ntok], hT[:, kc, 0:ntok], shT[:, kc, 0:ntok], ALU.add, [hT, shT], [hT])
        return hT

    def proj(hT, ntok):
        pr = pr_r.get()
        for cb, (c0, c1_) in enumerate(((0, 512), (512, 1024), (1024, 1536), (1536, 1816))):
            ps = psg.get()
            for kc in range(8):
                P.mm(ps[0:ntok, 0:c1_ - c0], hT[:, kc, 0:ntok], w_in_bf[:, kc, c0:c1_], kc == 0, kc == 7, [hT, w_in_bf], [ps])
            P.cp("act" if cb % 2 else "dve", pr[0:ntok, c0:c1_], ps[0:ntok, 0:c1_ - c0], [ps], [pr])
        return pr

    xg_r = ring(2, [4, 8 if True else 0]);

    def gelu_tanh(eng2, out_ap, out_t, x_ap, x_t, shape_tmp):
        t1, t2 = shape_tmp
        P.tt(eng2, t1, x_ap, x_ap, ALU.mult, [x_t], [t1.buf if isinstance(t1, T) else out_t])

    NQM = 128
    qT_r = ring(2, [8, 128], BF16, 64)
    Pt_r = ring(4, [4, 128], BF16)
    mk2_r = ring(2, [128], BF16)
    Ocat_r = ring(1, [3, 8, 65])
    o_r = ring(2, [512])
    imp_r = ring(2, [2, NBm]); sc_r = ring(2, [NBS]); wk_r = ring(2, [NBS]); m8_r = ring(2, [8]); sel_r = ring(2, [NBS])
    selT_r = ring(2, [128], BF16)
    rd_r = ring(2, [24]); cf_r = ring(2, [24]); tmpo_r = ring(1, [24, 64]); sg_r = ring(2, [24])
    VcA = A([NKTC, 2, 65], BF16)
    P.ms("pool", VcA[:], 1.0, [VcA])
    Hx_r = ring(2, [4, 8]); Hg_r = ring(2, [4, 8], BF16)
    Ht1_r = ring(2, [4, 8]); Ht2_r = ring(2, [4, 8])

    def compress(SC, XcT, col0, nb, n0):
        ps = psg.get()
        for kvg in range(4):
            for j in range(32):
                P.mm(ps[:, kvg * nb:(kvg + 1) * nb] if nb <= 128 else None, W1[:, kvg // 2, j, :],
                     XcT[:, kvg, col0 + j:col0 + j + 16 * (nb - 1) + 1:16], j == 0, j == 31, [W1, XcT], [ps]) if nb <= 128 else None
        return ps

    def compress_blocks(SC, XcT, col0, nb, n0):
        hx = Hx_r.get(); hg = Hg_r.get(); t1 = Ht1_r.get(); t2 = Ht2_r.get()
        for kvg in range(4):
            ps = psg.get()
            for j in range(32):
                P.mm(ps[:, 0:nb], W1[:, kvg // 2, j, :],
                     XcT[:, kvg, col0 + j:col0 + j + 16 * (nb - 1) + 1:16], j == 0, j == 31, [W1, XcT], [ps])
            P.act(hx[:, kvg, 0:nb], ps[:, 0:nb], AF.Identity, bias=cbias[:, kvg // 2:kvg // 2 + 1], reads=[ps, cbias], writes=[hx])
            yield
        X = hx[:, :, 0:nb]
        P.tt("pool", t1[:, :, 0:nb], X, X, ALU.mult, [hx], [t1])
        P.ts("dve", t1[:, :, 0:nb], t1[:, :, 0:nb], 0.044715, ALU.mult, 1.0, ALU.add, [t1], [t1])
        P.tt("pool", t1[:, :, 0:nb], t1[:, :, 0:nb], X, ALU.mult, [t1, hx], [t1])
        P.act(t2[:, :, 0:nb], t1[:, :, 0:nb], AF.Sigmoid, scale=GC, reads=[t1], writes=[t2])
        P.tt("dve", hg[:, :, 0:nb], X, t2[:, :, 0:nb], ALU.mult, [hx, t2], [hg])
        yield
        for c in range(0, nb, 256):
            w = min(256, nb - c)
            ps = psg.get()
            for g in range(2):
                P.mm(ps[0:64, g * w:(g + 1) * w], W2[:, 0, :], hg[:, g, c:c + w], True, True, [W2, hg], [ps])
            P.cp("dve", SC["KcT"][:, :, n0 + c:n0 + c + w], ps[0:64, 0:2 * w].rearrange("p (a b) -> p a b", a=2), [ps], [SC["KcT"]])
        P.cp("pool", SC["GvT"][:, :, n0:n0 + nb], hg[:, 2:4, 0:nb], [hg], [SC["GvT"]])

    def attend(SC, qT, NQ, t, qpos0, nkt_c, slc_kts, win_kts, sig, prompt, getkv, tick=None):
        Ocat = Ocat_r.get()
        NB = NBP if prompt else NBS
        NBmm = NBP if prompt else NBSm
        imp = imp_r.get()
        for ktc in range(nkt_c):
            ps = psg.get()
            for g in range(2):
                P.mm(ps[:, g * 64:(g + 1) * 64], SC["GvT"][:, g, ktc * 128:(ktc + 1) * 128], W2[:, 1, :], True, True,
                     [SC["GvT"], W2], [ps])
            P.cp("act", VcA[:, ktc, :, 0:64], ps[:, 0:128].rearrange("p (a b) -> p a b", a=2), [ps], [VcA])
        rd = rd_r.get()
        selTs = []
        for g in range(2):
            po1 = pso.get(); po2 = pso.get()
            for ktc in range(nkt_c):
                ps = psg.get()
                P.mm(ps[:, 0:4 * NQ], SC["KcT"][:, g, ktc * 128:(ktc + 1) * 128], qT[:, 4 * g:4 * g + 4, 0:NQ], True, True,
                     [SC["KcT"], qT], [ps])
                pt = Pt_r.get()
                P.act(pt[:, :, 0:NQ], ps[:, 0:4 * NQ].rearrange("p (a b) -> p a b", a=4), AF.Exp, reads=[ps], writes=[pt])
                base = qpos0 - 2048 * ktc - 31
                if base - 2032 < 0:
                    P.op("pool", lambda e, pt=pt, base=base: e.affine_select(
                        out=pt[:, :, 0:NQ], in_=pt[:, :, 0:NQ], pattern=[[0, 4], [1, NQ]], compare_op=ALU.is_ge, fill=0.0,
                        base=base, channel_multiplier=-16), [pt], [pt])
                for r in range(4):
                    P.mm(po1[0:NQ, r * 65:(r + 1) * 65], pt[:, r, 0:NQ], VcA[:, ktc, g, :], ktc == 0 and r == 0, ktc == nkt_c - 1, [pt, VcA], [po1])
                    P.mm(po2[0:NQ, r * NBmm:(r + 1) * NBmm], pt[:, r, 0:NQ], Mm[:, ktc, 0:NBmm], ktc == 0 and r == 0, ktc == nkt_c - 1, [pt, Mm], [po2])
            P.cp("act", Ocat[0:NQ, 0, 4 * g:4 * g + 4, :], po1[0:NQ, 0:260].rearrange("p (a b) -> p a b", a=4), [po1], [Ocat])
            P.ts("dve", rd[0:NQ, 4 * g:4 * g + 4], Ocat[0:NQ, 0, 4 * g:4 * g + 4, 64], 1e-30, ALU.max, reads=[Ocat], writes=[rd])
            P.op("dve", lambda e, g=g: e.reciprocal(out=rd[0:NQ, 4 * g:4 * g + 4], in_=rd[0:NQ, 4 * g:4 * g + 4]), [rd], [rd])
            for r in range(4):
                if r == 0:
                    P.ts("dve", imp[0:NQ, g, 0:NBmm], po2[0:NQ, 0:NBmm], rd[0:NQ, 4 * g:4 * g + 1], ALU.mult, reads=[po2, rd], writes=[imp])
                else:
                    P.stt("dve", imp[0:NQ, g, 0:NBmm], po2[0:NQ, r * NBmm:(r + 1) * NBmm], rd[0:NQ, 4 * g + r:4 * g + r + 1],
                          imp[0:NQ, g, 0:NBmm], ALU.mult, ALU.add, [po2, rd, imp], [imp])
            sc = sc_r.get(); wk = wk_r.get(); m8 = m8_r.get(); sel = sel_r.get()
            if prompt:
                P.tt("dve", sc[0:NQ, 0:NB], imp[0:NQ, g, 0:NB], Rtab[0:NQ, 63 - 2 * t:63 - 2 * t + NB], ALU.add, [imp, Rtab], [sc])
                P.ts("dve", sc[0:NQ, 0:1], sc[0:NQ, 0:1], BIG, ALU.add, reads=[sc], writes=[sc])
            else:
                P.tt("dve", sc[0:NQ, 0:NBmm], imp[0:NQ, g, 0:NBmm], sbias[0:NQ, 0:NBmm], ALU.add, [imp, sbias], [sc])
                P.cp("dve", sc[0:NQ, NBmm:NB], sbias[0:NQ, NBmm:NB], [sbias], [sc])
            cur = sc
            for rnd in range(NSEL // 8):
                P.op("dve", lambda e, cur=cur, m8=m8: e.max(out=m8[0:NQ, :], in_=cur[0:NQ, 0:NB]), [cur], [m8])
                if rnd < NSEL // 8 - 1:
                    P.op("dve", lambda e, cur=cur, m8=m8, wk=wk: e.match_replace(out=wk[0:NQ, 0:NB], in_to_replace=m8[0:NQ, :],
                                                                   in_values=cur[0:NQ, 0:NB], imm_value=-3.0e38), [cur, m8], [wk])
                    cur = wk
            P.ts("dve", sel[0:NQ, 0:NB], sc[0:NQ, 0:NB], m8[0:NQ, 7:8], ALU.is_ge, reads=[sc, m8], writes=[sel])
            ps = psg.get()
            P.tr(ps[0:NBmm, 0:NQ], sel[0:NQ, 0:NBmm], ident[0:NQ, 0:NQ], [sel, ident], [ps])
            selT = selT_r.get()
            P.cp("dve", selT[0:NBmm, 0:NQ], ps[0:NBmm, 0:NQ], [ps], [selT])
            selTs.append(selT)
        for br, kts in ((0, slc_kts), (1, win_kts)):
            pos = [pso.get(), pso.get()]
            steps = [(i, kt, mode, g) for i, (kt, mode) in enumerate(kts) for g in range(2)]
            kvc = {}

            def front(step, br=br, kvc=kvc):
                i, kt, mode, g = step
                if g == 0:
                    kvc[i] = getkv(br, kt)
                ktT, vaT = kvc[i]
                psm = None
                if mode in ("sel", "diag") and br == 0:
                    psm = psa.get()
                    P.mm(psm[:, 0:NQ], Eall[0:NBmm, kt, :], selTs[g][0:NBmm, 0:NQ], True, True, [Eall, selTs[g]], [psm])
                ps = psa.get()
                P.mm(ps[:, 0:4 * NQ], ktT[:, g, :], qT[:, 4 * g:4 * g + 4, 0:NQ], True, True, [ktT, qT], [ps])
                return ps, psm, vaT

            def back(step, fr, br=br, pos=pos, nsteps=len(kts)):
                i, kt, mode, g = step
                ps, psm, vaT = fr
                pt = Pt_r.get()
                P.act(pt[:, :, 0:NQ], ps[:, 0:4 * NQ].rearrange("p (a b) -> p a b", a=4), AF.Exp, reads=[ps], writes=[pt])
                if br == 0 and mode == "diag":
                    mk2 = mk2_r.get()
                    P.tt("dve", mk2[:, 0:NQ], psm[:, 0:NQ], tri[:, 0:NQ], ALU.mult, [psm, tri], [mk2])
                    P.tt("pool", pt[:, :, 0:NQ], pt[:, :, 0:NQ], mk2[:, 0:NQ].unsqueeze(1).to_broadcast([128, 4, NQ]), ALU.mult, [pt, mk2], [pt])
                elif br == 0 and mode == "sel":
                    P.tt("dve", pt[:, :, 0:NQ], pt[:, :, 0:NQ], psm[:, 0:NQ].unsqueeze(1).to_broadcast([128, 4, NQ]), ALU.mult, [pt, psm], [pt])
                elif br == 1 and mode in ("diag", "anti"):
                    m = tri if mode == "diag" else anti
                    P.tt("pool", pt[:, :, 0:NQ], pt[:, :, 0:NQ], m[:, 0:NQ].unsqueeze(1).to_broadcast([128, 4, NQ]), ALU.mult, [pt, m], [pt])
                for r in range(4):
                    P.mm(pos[g][0:NQ, r * 65:(r + 1) * 65], pt[:, r, 0:NQ], vaT[:, g, :], i == 0 and r == 0, i == nsteps - 1, [pt, vaT], [pos[g]])

            pending = front(steps[0])
            for n, step in enumerate(steps):
                nxt = front(steps[n + 1]) if n + 1 < len(steps) else None
                back(step, pending)
                pending = nxt
                if tick is not None:
                    tick()
            for g in range(2):
                P.cp("act", Ocat[0:NQ, 1 + br, 4 * g:4 * g + 4, :], pos[g][0:NQ, 0:260].rearrange("p (a b) -> p a b", a=4), [pos[g]], [Ocat])
        rdd = rd_r.get(); cf = cf_r.get(); tmpo = tmpo_r.get(); o = o_r.get()
        P.ts("dve", rdd[0:NQ, :], Ocat[0:NQ, :, :, 64].rearrange("p a b -> p (a b)") if False else Ocat[0:NQ, :, :, 64],
             1e-30, ALU.max, reads=[Ocat], writes=[rdd]) if False else None
        oc_den = Ocat[0:NQ, :, :, 64]
        rdd3 = rdd[0:NQ, :].rearrange("p (a b) -> p a b", a=3)
        P.ts("dve", rdd3, oc_den, 1e-30, ALU.max, reads=[Ocat], writes=[rdd])
        P.op("dve", lambda e: e.reciprocal(out=rdd[0:NQ, :], in_=rdd[0:NQ, :]), [rdd], [rdd])
        P.tt("dve", cf[0:NQ, :], rdd[0:NQ, :], sig[0:NQ, :], ALU.mult, [rdd, sig], [cf])
        P.tt("dve", tmpo[0:NQ, :, :], Ocat[0:NQ, :, :, 0:64].rearrange("p a b c -> p (a b) c"),
             cf[0:NQ, :].unsqueeze(2).to_broadcast([NQ, 24, 64]), ALU.mult, [Ocat, cf], [tmpo])
        o3 = o[0:NQ, :].rearrange("p (a b) -> p a b", a=8)
        P.tt("pool", o3, tmpo[0:NQ, 0:8, :], tmpo[0:NQ, 8:16, :], ALU.add, [tmpo], [o])
        P.tt("pool", o3, o3, tmpo[0:NQ, 16:24, :], ALU.add, [o, tmpo], [o])
        return o

    def ssm(pr, ntok, sstate, mode):
        ps = psg.get()
        for kc in range(4):
            P.tr(ps[:, kc * ntok:(kc + 1) * ntok], pr[0:ntok, kc * 128:(kc + 1) * 128], ident[0:ntok, 0:ntok], [pr, ident], [ps])
        if cfg.get('cut') == 11:
            raise _Stop()
        uTf = uTf_r.get(); uTb = uTb_r.get()
        P.cp("act", uTf[:, :, 0:ntok], ps[:, 0:4 * ntok].rearrange("p (a b) -> p a b", a=4), [ps], [uTf])
        if cfg.get('cut') == 12:
            raise _Stop()
        P.cp("pool", uTb[:, :, 0:ntok], uTf[:, :, 0:ntok], [uTf], [uTb])
        if cfg.get('cut') == 1:
            raise _Stop()
        py = pso.get()
        if mode == "scan":
            for G in range(4):
                pre = psg.get(); pim = psg.get()
                for j in range(4):
                    gp = 4 * G + j
                    P.mm(pre[:, j * 128:(j + 1) * 128], Bw[:, gp, 0, :], uTb[:, G, 0:128], True, True, [Bw, uTb], [pre])
                    P.mm(pim[:, j * 128:(j + 1) * 128], Bw[:, gp, 1, :], uTb[:, G, 0:128], True, True, [Bw, uTb], [pim])
                pre3 = pre[:, 0:512].rearrange("p (a b) -> p a b", a=4); pim3 = pim[:, 0:512].rearrange("p (a b) -> p a b", a=4)
                ct4 = ctab[:, 4 * G:4 * G + 4, :]; st4 = stab2[:, 4 * G:4 * G + 4, 0, :]; mst4 = stab2[:, 4 * G:4 * G + 4, 1, :]
                TA = TA_r.get(); TB = TB_r.get(); W = W4_r.get(); R = R4_r.get(); SBF = SBF_r.get(); cin = cin_r.get()
                P.tt("dve", TA[:, 0, :, :], pre3, ct4, ALU.mult, [pre, ctab], [TA])
                P.tt("dve", TA[:, 1, :, :], pim3, ct4, ALU.mult, [pim, ctab], [TA])
                P.tt("dve", TB[:, 0, :, :], pim3, st4, ALU.mult, [pim, stab2], [TB])
                P.tt("dve", TB[:, 1, :, :], pre3, mst4, ALU.mult, [pre, stab2], [TB])
                P.tt("pool", W[:], TA[:], TB[:], ALU.add, [TA, TB], [W])
                ss4 = sstate[:, 4 * G:4 * G + 4, :, 0]
                P.tt("pool", cin[:], ss4, rho[:, 4 * G:4 * G + 4].unsqueeze(2).to_broadcast([128, 4, 2]), ALU.mult, [sstate, rho], [cin])
                W0 = W[:, :, :, 0].rearrange("p r j -> p j r")
                P.tt("pool", W0, W0, cin[:], ALU.add, [W, cin], [W])
                for ri in range(2):
                    P.op("dve", lambda e, ri=ri, G=G, W=W, R=R: e.tensor_tensor_scan(
                        out=R[:, ri, :, :].rearrange("p a b -> p (a b)"), data0=rhoB[:, 4 * G:4 * G + 4, :].rearrange("p a b -> p (a b)"),
                        data1=W[:, ri, :, :].rearrange("p a b -> p (a b)"), initial=0.0, op0=ALU.mult, op1=ALU.add), [rhoB, W], [R])
                P.tt("pool", TA[:], R[:], ct4.unsqueeze(1).to_broadcast([128, 2, 4, 128]), ALU.mult, [R, ctab], [TA])
                P.tt("dve", TB[:, 0, :, :], R[:, 1, :, :], mst4, ALU.mult, [R, stab2], [TB])
                P.tt("pool", TB[:, 1, :, :], R[:, 0, :, :], st4, ALU.mult, [R, stab2], [TB])
                P.tt("dve", SBF[:], TA[:], TB[:], ALU.add, [TA, TB], [SBF])
                P.tt("pool", ss4, TA[:, :, :, 127].rearrange("p r j -> p j r"), TB[:, :, :, 127].rearrange("p r j -> p j r"), ALU.add,
                     [TA, TB], [sstate])
                for j in range(4):
                    gp = 4 * G + j
                    for ri in range(2):
                        P.mm(py[:, G * 128:(G + 1) * 128], Cw[:, gp, ri, :], SBF[:, ri, j, :],
                             j == 0 and ri == 0, j == 3 and ri == 1, [Cw, SBF], [py])
        for gp in (range(16) if mode != "scan" else ()):
            pb = psg.get()
            for k3 in range(3):
                P.mm(pb[:, k3 * ntok:(k3 + 1) * ntok], Bw[:, gp, k3, :], uTb[:, gp // 4, 0:ntok], True, True, [Bw, uTb], [pb])
            b3 = pb[:, 0:3 * ntok].rearrange("p (a b) -> p a b", a=3)
            t1 = t1_r.get(); t2 = t2_r.get(); w = w_r.get(); r = r_r.get()
            if mode == "scan":
                ct_b = ctab[:, gp, 0:ntok].unsqueeze(1).to_broadcast([128, 2, ntok]); st_pre = stab2[:, gp, :, 0:ntok]
            else:
                ct_b = ctab[:, gp, 0:1].unsqueeze(1).to_broadcast([128, 2, ntok]); st_pre = stab2[:, gp, :, 0:1].to_broadcast([128, 2, ntok])
            P.tt("dve", t1[:, :, 0:ntok], b3[:, 0:2, :], ct_b, ALU.mult, [pb, ctab], [t1])
            P.tt("dve", t2[:, :, 0:ntok], b3[:, 1:3, :], st_pre, ALU.mult, [pb, stab2], [t2])
            P.tt("pool", w[:, :, 0:ntok], t1[:, :, 0:ntok], t2[:, :, 0:ntok], ALU.add, [t1, t2], [w])
            if cfg.get('cut') == 2 and gp == 0:
                raise _Stop()
            if mode == "scan":
                for ri in range(2):
                    P.op("dve", lambda e, ri=ri, gp=gp, w=w, r=r: e.tensor_tensor_scan(
                        out=r[:, ri, 0:ntok], data0=rhoB[:, gp, 0:ntok], data1=w[:, ri, 0:ntok],
                        initial=sstate[:, gp, ri, 0:1], op0=ALU.mult, op1=ALU.add), [rhoB, w, sstate], [r])
            else:
                P.stt("dve", r[:, :, 0:ntok], sstate[:, gp, :, 0:ntok], rho[:, gp:gp + 1], w[:, :, 0:ntok], ALU.mult, ALU.add,
                      [sstate, rho, w], [r])
            Ap = Apost_r.get(); Bp = Bpost_r.get(); sbf = sbf_r.get()
            if cfg.get('cut') == 3 and gp == 0:
                raise _Stop()
            if mode == "scan":
                s_sin = stab2[:, gp, 0, 0:ntok]; s_msin = stab2[:, gp, 1, 0:ntok]
            else:
                s_sin = stab2[:, gp, 0, 0:1].to_broadcast([128, ntok]); s_msin = stab2[:, gp, 1, 0:1].to_broadcast([128, ntok])
            P.tt("pool", Ap[:, :, 0:ntok], r[:, :, 0:ntok], ct_b, ALU.mult, [r, ctab], [Ap])
            P.tt("dve", Bp[:, 0, 0:ntok], r[:, 1, 0:ntok], s_msin, ALU.mult, [r, stab2], [Bp])
            P.tt("pool", Bp[:, 1, 0:ntok], r[:, 0, 0:ntok], s_sin, ALU.mult, [r, stab2], [Bp])
            P.tt("dve", sbf[:, :, 0:ntok], Ap[:, :, 0:ntok], Bp[:, :, 0:ntok], ALU.add, [Ap, Bp], [sbf])
            if cfg.get('cut') == 4 and gp == 0:
                raise _Stop()
            if mode == "scan":
                P.tt("pool", sstate[:, gp, :, 0:1], Ap[:, :, ntok - 1:ntok], Bp[:, :, ntok - 1:ntok], ALU.add, [Ap, Bp], [sstate])
            else:
                P.tt("pool", sstate[:, gp, :, 0:ntok], Ap[:, :, 0:ntok], Bp[:, :, 0:ntok], ALU.add, [Ap, Bp], [sstate])
            for ri in range(2):
                P.mm(py[:, (gp // 4) * ntok:(gp // 4 + 1) * ntok], Cw[:, gp, ri, :], sbf[:, ri, 0:ntok],
                     gp % 4 == 0 and ri == 0, gp % 4 == 3 and ri == 1, [Cw, sbf], [py])
        if cfg.get('cut') == 5:
            raise _Stop()
        yv = yv_r.get(); z = zz_r.get(); zb = zb_r.get(); g1 = g1_r.get(); g2 = g2_r.get()
        for kc in range(4):
            P.stt("dve", yv[:, kc, 0:ntok], uTf[:, kc, 0:ntok], dT[:, kc:kc + 1], py[:, kc * ntok:(kc + 1) * ntok], ALU.mult, ALU.add,
                  [uTf, dT, py], [yv])
        Y = yv[:, :, 0:ntok]
        P.tt("pool", g1[:, :, 0:ntok], Y, Y, ALU.mult, [yv], [g1])
        P.ts("dve", g1[:, :, 0:ntok], g1[:, :, 0:ntok], 0.044715, ALU.mult, 1.0, ALU.add, [g1], [g1])
        P.tt("pool", g1[:, :, 0:ntok], g1[:, :, 0:ntok], Y, ALU.mult, [g1, yv], [g1])
        P.act(g2[:, :, 0:ntok], g1[:, :, 0:ntok], AF.Sigmoid, scale=GC, reads=[g1], writes=[g2])
        P.tt("dve", z[:, :, 0:ntok], Y, g2[:, :, 0:ntok], ALU.mult, [yv, g2], [z])
        P.cp("pool", zb[:, :, 0:ntok], z[:, :, 0:ntok], [z], [zb])
        if cfg.get('cut') == 6:
            raise _Stop()
        pg = psg.get()
        for oc in range(4):
            for kc in range(4):
                P.mm(pg[:, oc * ntok:(oc + 1) * ntok], w_glu_bf[:, kc, oc * 128:(oc + 1) * 128], zb[:, kc, 0:ntok], kc == 0, kc == 3, [w_glu_bf, zb], [pg])
        P.act(g2[:, :, 0:ntok], pg[:, 0:4 * ntok].rearrange("p (a b) -> p a b", a=4), AF.Sigmoid, reads=[pg], writes=[g2])
        osm = os_r.get()
        P.tt("dve", osm[:, :, 0:ntok], z[:, :, 0:ntok], g2[:, :, 0:ntok], ALU.mult, [z, g2], [osm])
        return osm

    def out_proj(osm, o, xt, ntok, gt1_ap, gt_t):
        sq = sq_r.get(); mixS = mixS_r.get()
        P.tt("pool", sq[:, :, 0:ntok], osm[:, :, 0:ntok], osm[:, :, 0:ntok], ALU.mult, [osm], [sq])
        pss = psg.get()
        for kc in range(4):
            P.mm(pss[0:ntok, 0:1], sq[:, kc, 0:ntok], ones[:, 0:1], kc == 0, kc == 3, [sq, ones], [pss])
        s = st4.get()
        P.ts("dve", s[0:ntok, 1:2], pss[0:ntok, 0:1], 1.0 / 512, ALU.mult, EPS, ALU.add, [pss], [s])
        P.act(s[0:ntok, 2:3], s[0:ntok, 1:2], AF.Sqrt, reads=[s], writes=[s])
        P.op("dve", lambda e: e.reciprocal(out=s[0:ntok, 3:4], in_=s[0:ntok, 2:3]), [s], [s])
        for kc in range(4):
            P.ts("dve", mixS[:, kc, 0:ntok], osm[:, kc, 0:ntok], gST[:, kc:kc + 1], ALU.mult, reads=[osm, gST], writes=[mixS])
        acc = acc_r.get()
        for half in range(2):
            ps = psg.get()
            for kc in range(4):
                P.mm(ps[0:ntok, :], mixS[:, kc, 0:ntok], w_out_bf[:, kc, half * 512:(half + 1) * 512], kc == 0, kc == 3, [mixS, w_out_bf], [ps])
            P.ts("dve", acc[0:ntok, half * 512:(half + 1) * 512], ps[0:ntok, :], s[0:ntok, 3:4], ALU.mult, reads=[ps, s], writes=[acc])
        sa = rstd_of(o[0:ntok, :], o, ntok, 512)
        on = on_r.get()
        P.ts("dve", on[0:ntok, :], o[0:ntok, :], sa[0:ntok, 3:4], ALU.mult, reads=[o, sa], writes=[on])
        ps = psg.get()
        for kc in range(4):
            P.tr(ps[:, kc * ntok:(kc + 1) * ntok], on[0:ntok, kc * 128:(kc + 1) * 128], ident[0:ntok, 0:ntok], [on, ident], [ps])
        mixA = mixA_r.get()
        for kc in range(4):
            P.ts("dve", mixA[:, kc, 0:ntok], ps[:, kc * ntok:(kc + 1) * ntok], gAT[:, kc:kc + 1], ALU.mult, reads=[ps, gAT], writes=[mixA])
        x1 = x1_r.get()
        for half in range(2):
            ps = psg.get()
            for kc in range(4):
                P.mm(ps[0:ntok, :], mixA[:, kc, 0:ntok], w_out_bf[:, 4 + kc, half * 512:(half + 1) * 512], kc == 0, kc == 3, [mixA, w_out_bf], [ps])
            sl = slice(half * 512, (half + 1) * 512)
            P.tt("dve", acc[0:ntok, sl], acc[0:ntok, sl], ps[0:ntok, :], ALU.add, [acc, ps], [acc])
            P.tt("pool", acc[0:ntok, sl], acc[0:ntok, sl], gt1_ap[:, sl], ALU.mult, [acc, gt_t], [acc])
            P.tt("pool", x1[0:ntok, sl], acc[0:ntok, sl], xt[0:ntok, sl], ALU.add, [acc, xt], [x1])
        return x1

    def bcast_rows(dst, row, which):
        P.ms("pool", selr[:], 0.0, [selr])
        P.op("pool", lambda e: e.affine_select(out=selr[:], in_=ones[0:NS, :], pattern=[[0, 128]], compare_op=ALU.is_equal,
                                               fill=0.0, base=-row, channel_multiplier=1), [ones], [selr])
        for half in range(2):
            ps = psg.get()
            P.mm(ps[:, :], selr[:], gt_tm[:, which, half * 512:(half + 1) * 512], True, True, [selr, gt_tm], [ps])
            P.cp("dve", dst[:, half * 512:(half + 1) * 512], ps[:, :], [ps], [dst])

    sig_r = ring(2, [24])

    def make_qT(pr, ntok):
        ps = psg.get(); ps2 = psg.get()
        for h in range(8):
            pp = ps if h < 4 else ps2
            P.tr(pp[0:64, (h % 4) * ntok:(h % 4 + 1) * ntok], pr[0:ntok, 512 + h * 64:512 + (h + 1) * 64], ident[0:ntok, 0:ntok], [pr, ident], [pp])
        qT = qT_r.get()
        P.act(qT[:, 0:4, 0:ntok], ps[0:64, 0:4 * ntok].rearrange("p (a b) -> p a b", a=4), AF.Copy, scale=0.125, reads=[ps], writes=[qT])
        P.act(qT[:, 4:8, 0:ntok], ps2[0:64, 0:4 * ntok].rearrange("p (a b) -> p a b", a=4), AF.Copy, scale=0.125, reads=[ps2], writes=[qT])
        sig = sig_r.get()
        P.act(sig[0:ntok, :], pr[0:ntok, 1792:1816], AF.Sigmoid, reads=[pr], writes=[sig])
        return qT, sig

    _k = "ExternalOutput" if cfg.get("dbg") else "Internal"
    o_d = T(nc.dram_tensor("o_scratch", [B * Tq + DB, 512], F32, kind=_k).ap())
    u_d = T(nc.dram_tensor("u_scratch", [B * Tq + DB, 512], F32, kind=_k).ap())
    x1s_d = T(nc.dram_tensor("x1s_scratch", [DB, 1024], F32, kind="Internal").ap())
    mark_state = off[0]
    KT = A([2, 2, NT * 128], BF16, 64); VA = A([2, NT, 2, 65], BF16)
    KcT = A([2, NKTP * 128], BF16, 64); GvT = A([2, NKTP * 128], BF16)
    XcT = A([4, 144], BF16, 64)
    SCp = dict(KT=KT, VA=VA, KcT=KcT, GvT=GvT)

    def getkv_p(br, kt):
        return T(KT[:, br, :, kt * 128:(kt + 1) * 128], KT.buf), T(VA[:, br, kt, :, :], VA.buf)

    for b in range(B):
        P.ms("pool", KcT[:], 0.0, [KcT]); P.ms("pool", GvT[:], 0.0, [GvT]); P.ms("pool", XcT[:], 0.0, [XcT])
        P.ms("pool", VA[:], 1.0, [VA])
        def pre_tile(t, b=b):
            r0 = b * Tq + t * 128
            xt = xt_r.get()
            P.dma(xt[:], x_p.ap[r0:r0 + 128, :], [x_p], [xt])
            hT = norm_mod_T(xt, 128, a1T, sh1T, DB + b)
            pr = proj(hT, 128)
            P.dma(cmp_p.ap[r0:r0 + 128, :], pr[:, 1024:1280], [pr], [cmp_p])
            P.dma(slc_p.ap[r0:r0 + 128, :], pr[:, 1280:1536], [pr], [slc_p])
            P.dma(u_d.ap[r0:r0 + 128, :], pr[:, 0:512], [pr], [u_d])
            if t >= NT - 4:
                w0 = b * 512 + (t - (NT - 4)) * 128
                P.dma(win_p.ap[w0:w0 + 128, :], pr[:, 1536:1792], [pr], [win_p])
            return pr

        pr_next = pre_tile(0)
        for t in range(NT):
            r0 = b * Tq + t * 128
            pr = pr_next
            if t + 1 < NT:
                pr_next = pre_tile(t + 1)
            P.cp("pool", XcT[:, :, 0:16], XcT[:, :, 128:144], [XcT], [XcT])
            ps = psg.get()
            for kvg in range(4):
                P.tr(ps[0:64, kvg * 128:(kvg + 1) * 128], pr[:, 1024 + kvg * 64:1024 + (kvg + 1) * 64], ident[:], [pr, ident], [ps])
            P.cp("dve", XcT[:, :, 16:144], ps[0:64, :].rearrange("p (a b) -> p a b", a=4), [ps], [XcT])
            ps = psg.get()
            for br in range(2):
                for g in range(2):
                    c0 = 1280 + br * 256 + g * 64
                    P.tr(ps[0:64, (br * 2 + g) * 128:(br * 2 + g + 1) * 128], pr[:, c0:c0 + 64], ident[:], [pr, ident], [ps])
            P.cp("act", KT[:, :, :, t * 128:(t + 1) * 128], ps[0:64, :].rearrange("p (a b c) -> p a b c", a=2, b=2), [ps], [KT])
            for br in range(2):
                c0 = 1280 + br * 256 + 128
                P.cp("pool", VA[:, br, t, :, 0:64], pr[:, c0:c0 + 128].rearrange("p (a b) -> p a b", a=2), [pr], [VA])
            if t == 0:
                for _ in compress_blocks(SCp, XcT, 16, 7, 0):
                    pass
            else:
                for _ in compress_blocks(SCp, XcT, 0, 8, 8 * t - 1):
                    pass
            qT, sig = make_qT(pr, 128)
            nkt_c = (8 * t + 7 + 127) // 128
            slc_kts = [(kt, "diag" if kt == t else "sel") for kt in range(t + 1)]
            win_kts = []
            for kt in range(max(0, t - 4), t + 1):
                win_kts.append((kt, "diag" if kt == t else ("anti" if kt == t - 4 else "none")))
            o = attend(SCp, qT, 128, t, 128 * t, nkt_c, slc_kts, win_kts, sig, True, getkv_p)
            P.dma(o_d.ap[r0:r0 + 128, :], o[:], [o], [o_d])
    P.barrier()
    if stop == 'A':
        return finish_prog()
    P.barrier()
    off[0] = mark_state
    GPG = min(16, NPG); NGRP = NPG // GPG
    Hx_r = ring(1, [4, GPG * 8]); Hg_r = ring(1, [4, GPG * 8], BF16); Ht1_r = ring(1, [4, GPG * 8]); Ht2_r = ring(1, [4, GPG * 8])
    KcT_s = A([2, NKTS * 128], BF16, 64); GvT_s = A([2, NKTS * 128], BF16)
    KcT_s2 = A([2, NKTS * 128], BF16, 64); GvT_s2 = A([2, NKTS * 128], BF16)
    XcW = A([4, 16 + GPG * 128], BF16, 64)
    SCs2 = [dict(KcT=KcT_s, GvT=GvT_s), dict(KcT=KcT_s2, GvT=GvT_s2)]
    pg_r = ring(6, [256]); ktile_r = ring(3, [2, 128], BF16, 64); vtile_r = ring(3, [2, 65], BF16)
    knew = A([2, 2, 128], BF16, 64); vnew = A([2, 2, 65], BF16)
    kTn = A([4, DB], BF16, 64)
    vrow = A([2, 2, 128], F32, 1); sigrow = A([2, 24], F32, 1)
    pti = A([DB * NPG], I32); ptf = A([DB * NPG]); idx_all = pti; pcol = A([1])
    P.dma(pti[:], ptab.ap.rearrange("n o -> (n o)").partition_broadcast(128), [ptab], [pti])
    P.op("pool", lambda e: e.iota(pcol[:], pattern=[[0, 1]], base=0, channel_multiplier=1,
                                  allow_small_or_imprecise_dtypes=True), [], [pcol])
    P.cp("dve", ptf[:], pti[:], [pti], [ptf])
    P.ts("dve", ptf[:], ptf[:], 128.0, ALU.mult, pcol[:, 0:1], ALU.add, [ptf, pcol], [ptf])
    P.cp("dve", idx_all[:], ptf[:], [ptf], [idx_all])
    xs = xt_r.get()
    P.dma(xs[0:DB, :], x_s.ap, [x_s], [xs])
    hTs = norm_mod_T(xs, DB, a1T, sh1T, None)
    prs = proj(hTs, DB)
    P.dma(cmp_s.ap, prs[0:DB, 1024:1280], [prs], [cmp_s])
    P.dma(slc_s.ap, prs[0:DB, 1280:1536], [prs], [slc_s])
    P.dma(u_d.ap[B * Tq:B * Tq + DB, :], prs[0:DB, 0:512], [prs], [u_d])
    P.dma(win_s.ap.rearrange("(i r) c -> i r c", r=512)[:, 511, :], prs[0:DB, 1536:1792], [prs], [win_s])
    P.dma(win_s.ap.rearrange("(i r) c -> i r c", r=512)[:, 0:511, :], state_win.ap.rearrange("(i r) c -> i r c", r=512)[:, 1:512, :],
          [state_win], [win_s])
    qTs, sigs = make_qT(prs, DB)
    ps = psg.get()
    for br in range(2):
        for g in range(2):
            c0 = 1280 + br * 256 + g * 64
            P.tr(ps[0:64, (br * 2 + g) * DB:(br * 2 + g + 1) * DB], prs[0:DB, c0:c0 + 64], ident[0:DB, 0:DB], [prs, ident], [ps])
    P.cp("dve", kTn[:], ps[0:64, 0:4 * DB].rearrange("p (a b) -> p a b", a=4), [ps], [kTn])
    for tl in vtile_r.items:
        P.ms("pool", tl[:], 1.0, [tl])
    def compress_seq(i):
        SC = SCs2[i % 2]
        P.ms("pool", SC["KcT"][:], 0.0, [SC["KcT"]]); P.ms("pool", SC["GvT"][:], 0.0, [SC["GvT"]]); P.ms("pool", XcW[:], 0.0, [XcW])
        yield
        for G in range(NGRP):
            if G > 0:
                P.cp("pool", XcW[:, :, 0:16], XcW[:, :, GPG * 128:GPG * 128 + 16], [XcW], [XcW])
            for jp in range(GPG):
                j = G * GPG + jp
                pg = pg_r.get()
                P.dma(pg[:], cache_cmp.ap, [cache_cmp, idx_all], [pg], q="pool", indirect=idx_all[:, i * NPG + j:i * NPG + j + 1])
                ps = psg.get()
                for kvg in range(4):
                    P.tr(ps[0:64, kvg * 128:(kvg + 1) * 128], pg[:, kvg * 64:(kvg + 1) * 64], ident[:], [pg, ident], [ps])
                P.cp("dve" if jp % 2 else "act", XcW[:, :, 16 + jp * 128:16 + (jp + 1) * 128], ps[0:64, :].rearrange("p (a b) -> p a b", a=4), [ps], [XcW])
                yield
            if G == 0:
                yield from compress_blocks(SC, XcW, 16, GPG * 8 - 1, 0)
            else:
                yield from compress_blocks(SC, XcW, 0, GPG * 8, G * GPG * 8 - 1)
            yield

    gen_cur = compress_seq(0)
    for _ in gen_cur:
        pass
    for i in range(DB):
        SCi = SCs2[i % 2]
        gen_next = compress_seq(i + 1) if i + 1 < DB else iter(())
        P.dma(sigrow[0:1, i % 2, :], sigs[i:i + 1, :], [sigs], [sigrow])
        for br in range(2):
            c0 = 1280 + br * 256 + 128
            P.dma(vrow[0:1, i % 2, br, :], prs[i:i + 1, c0:c0 + 128], [prs], [vrow])
        P.ms("pool", knew[:], 0.0, [knew]); P.ms("pool", vnew[:], 0.0, [vnew])
        for br in range(2):
            P.cp("dve", knew[:, br, :, 0], kTn[:, 2 * br:2 * br + 2, i], [kTn], [knew])
            P.cp("dve", vnew[0:1, br, :, 0:64], vrow[0:1, i % 2, br, :].rearrange("p (a b) -> p a b", a=2), [vrow], [vnew])
            P.ms("pool", vnew[0:1, br, :, 64:65], 1.0, [vnew])

        def getkv_s(br, kt, i=i):
            if (br == 0 and kt == NPG) or (br == 1 and kt == 4):
                return T(knew[:, br, :, :], knew.buf), T(vnew[:, br, :, :], vnew.buf)
            pg = pg_r.get()
            if br == 0:
                P.dma(pg[:], cache_slc.ap, [cache_slc, idx_all], [pg], q="pool", indirect=idx_all[:, i * NPG + kt:i * NPG + kt + 1])
            else:
                r0 = i * 512 + kt * 128
                P.dma(pg[:], state_win.ap[r0:r0 + 128, :], [state_win], [pg])
            ps = psg.get()
            for g in range(2):
                P.tr(ps[0:64, g * 128:(g + 1) * 128], pg[:, g * 64:(g + 1) * 64], ident[:], [pg, ident], [ps])
            ktl = ktile_r.get(); vtl = vtile_r.get()
            P.cp("act", ktl[:], ps[0:64, 0:256].rearrange("p (a b) -> p a b", a=2), [ps], [ktl])
            P.cp("pool", vtl[:, :, 0:64], pg[:, 128:256].rearrange("p (a b) -> p a b", a=2), [pg], [vtl])
            if br == 1 and kt == 0:
                P.ms("pool", vtl[0:1, :, :], 0.0, [vtl])
            elif br == 1 and kt == 1:
                P.ms("pool", vtl[0:1, :, 64:65], 1.0, [vtl])
            return ktl, vtl

        qTi = T(qTs[:, :, i:i + 1], qTs.buf)
        sigi = T(sigrow[0:1, i % 2, :], sigrow.buf)
        slc_kts = [(kt, "sel") for kt in range(NPG)] + [(NPG, "none")]
        win_kts = [(kt, "none") for kt in range(5)]
        o = attend(SCi, qTi, 1, 0, PAST, NKTS, slc_kts, win_kts, sigi, False, getkv_s, tick=lambda g_=gen_next: next(g_, None))
        P.dma(o_d.ap[B * Tq + i:B * Tq + i + 1, :], o[0:1, :], [o], [o_d])
        for _ in gen_next:
            pass
    P.barrier()
    P.barrier()
    psg.items = psb[0:6]; psg.i = 0
    off[0] = mark_persist
    w_out_bf = A([8, 1024], BF16); w_glu_bf = A([4, 512], BF16)
    P.dma(w_out_bf[:], w_out.ap.rearrange("(kc p) n -> p kc n", p=128), [w_out], [w_out_bf], q="pool")
    P.dma(w_glu_bf[:], w_glu.ap.rearrange("(kc p) n -> p kc n", p=128), [w_glu], [w_glu_bf], q="pool")
    lreT = A([16]); limT = A([16]); dtT = A([16]); rho = A([16]); th = A([16])
    for vec, dst in ((lam_re, lreT), (lam_im, limT)):
        ps = loadT(vec, 16, dst)
        P.cp("dve", dst[:], ps[:, 0:16], [ps], [dst])
    with nc.allow_non_contiguous_dma(reason="tiny log_dt broadcast"):
        for g2 in range(2):
            P.dma(dtT[g2 * 64:(g2 + 1) * 64, :], log_dt.ap.rearrange("(gp g2) -> g2 gp", g2=2)[g2:g2 + 1, :].partition_broadcast(64)
                  if False else bass.AP(tensor=log_dt.ap.tensor, offset=g2, ap=[[0, 64], [2, 16]]), [log_dt], [dtT], slow=True)
    P.act(dtT[:], dtT[:], AF.Exp, reads=[dtT], writes=[dtT])
    P.tt("dve", rho[:], lreT[:], dtT[:], ALU.mult, [lreT, dtT], [rho])
    P.tt("dve", th[:], limT[:], dtT[:], ALU.mult, [limT, dtT], [th])
    P.act(rho[:], rho[:], AF.Exp, reads=[rho], writes=[rho])
    ctab = A([16, 128]); stab2 = A([16, 2, 128])
    Bw = A([16, 3, 128], BF16); Cw = A([16, 2, 128], BF16)
    mark_tabs = off[0]
    iot = A([128]); ph = A([16, 128]); phf = A([16, 128]); phi = A([16, 128], I32)
    P.op("pool", lambda e: e.iota(iot[:], pattern=[[1, 128]], base=1, channel_multiplier=0,
                                  allow_small_or_imprecise_dtypes=True), [], [iot])

    def sin_table(dst_ap, dst_t, phase_turns):
        for gp in range(16):
            P.ts("dve", ph[:, gp, :], iot[:], th[:, gp:gp + 1], ALU.mult, 1.0 / (2 * math.pi), ALU.mult, [iot, th], [ph])
        P.ts("dve", ph[:], ph[:], phase_turns, ALU.add, reads=[ph], writes=[ph])
        P.cp("dve", phi[:], ph[:], [ph], [phi])
        P.cp("dve", phf[:], phi[:], [phi], [phf])
        P.tt("dve", ph[:], ph[:], phf[:], ALU.subtract, [ph, phf], [ph])
        P.ts("dve", phf[:], ph[:], 0.5, ALU.is_gt, reads=[ph], writes=[phf])
        P.tt("dve", ph[:], ph[:], phf[:], ALU.subtract, [ph, phf], [ph])
        P.ts("dve", phf[:], ph[:], -0.5, ALU.is_lt, reads=[ph], writes=[phf])
        P.tt("dve", ph[:], ph[:], phf[:], ALU.add, [ph, phf], [ph])
        P.act(dst_ap, ph[:], AF.Sin, scale=2 * math.pi, reads=[ph], writes=[dst_t])

    sin_table(ctab[:], ctab, 0.25)
    sin_table(stab2[:, :, 0, :], stab2, 0.0)
    P.ts("dve", stab2[:, :, 1, :], stab2[:, :, 0, :], -1.0, ALU.mult, reads=[stab2], writes=[stab2])
    c1 = A([16]); s1 = A([16]); nre = A([16]); nim = A([16]); l2 = A([16]); kre = A([16]); kim = A([16]); tmpk = A([16])
    P.cp("dve", c1[:], ctab[:, :, 0], [ctab], [c1]); P.cp("dve", s1[:], stab2[:, :, 0, 0], [stab2], [s1])
    P.tt("dve", nre[:], rho[:], c1[:], ALU.mult, [rho, c1], [nre]); P.ts("dve", nre[:], nre[:], -1.0, ALU.add, reads=[nre], writes=[nre])
    P.tt("dve", nim[:], rho[:], s1[:], ALU.mult, [rho, s1], [nim])
    P.tt("dve", l2[:], lreT[:], lreT[:], ALU.mult, [lreT], [l2]); P.tt("dve", tmpk[:], limT[:], limT[:], ALU.mult, [limT], [tmpk])
    P.tt("dve", l2[:], l2[:], tmpk[:], ALU.add, [l2, tmpk], [l2]); P.op("dve", lambda e: e.reciprocal(out=l2[:], in_=l2[:]), [l2], [l2])
    P.tt("dve", kre[:], nre[:], lreT[:], ALU.mult, [nre, lreT], [kre]); P.tt("dve", tmpk[:], nim[:], limT[:], ALU.mult, [nim, limT], [tmpk])
    P.tt("dve", kre[:], kre[:], tmpk[:], ALU.add, [kre, tmpk], [kre]); P.tt("dve", kre[:], kre[:], l2[:], ALU.mult, [kre, l2], [kre])
    P.tt("dve", kim[:], nim[:], lreT[:], ALU.mult, [nim, lreT], [kim]); P.tt("dve", tmpk[:], nre[:], limT[:], ALU.mult, [nre, limT], [tmpk])
    P.tt("dve", kim[:], kim[:], tmpk[:], ALU.subtract, [kim, tmpk], [kim]); P.tt("dve", kim[:], kim[:], l2[:], ALU.mult, [kim, l2], [kim])
    Bre = A([16, 16]); Bim = A([16, 16]); Bbr = A([16, 16]); Bbi = A([16, 16]); tB = A([16, 16])
    for src, dst in ((b_re, Bre), (b_im, Bim)):
        for g2 in range(2):
            P.dma(dst[g2 * 64:(g2 + 1) * 64, :, :],
                  bass.AP(tensor=src.ap.tensor, offset=g2 * 1024, ap=[[16, 64], [2048, 16], [1, 16]]), [src], [dst])
    kre_b = kre[:].unsqueeze(2).to_broadcast([128, 16, 16]); kim_b = kim[:].unsqueeze(2).to_broadcast([128, 16, 16])
    P.tt("dve", Bbr[:], Bre[:], kre_b, ALU.mult, [Bre, kre], [Bbr]); P.tt("dve", tB[:], Bim[:], kim_b, ALU.mult, [Bim, kim], [tB])
    P.tt("dve", Bbr[:], Bbr[:], tB[:], ALU.subtract, [Bbr, tB], [Bbr])
    P.tt("dve", Bbi[:], Bim[:], kre_b, ALU.mult, [Bim, kre], [Bbi]); P.tt("dve", tB[:], Bre[:], kim_b, ALU.mult, [Bre, kim], [tB])
    P.tt("dve", Bbi[:], Bbi[:], tB[:], ALU.add, [Bbi, tB], [Bbi])
    Bpad = A([128]);
    for ri, Bb in enumerate((Bbr, Bbi)):
        for gp in range(16):
            c0 = 32 * (gp % 4)
            P.ms("pool", Bpad[:], 0.0, [Bpad])
            for g2 in range(2):
                P.cp("pool", Bpad[g2 * 64:(g2 + 1) * 64, c0 + 16 * g2:c0 + 16 * g2 + 16], Bb[g2 * 64:(g2 + 1) * 64, gp, :], [Bb], [Bpad])
            ps = psg.get()
            P.tr(ps[:, 0:128], Bpad[:], ident[:], [Bpad, ident], [ps])
            P.cp("dve", Bw[:, gp, ri, :], ps[:, 0:128], [ps], [Bw])
            if ri == 0:
                P.cp("act", Bw[:, gp, 2, :], ps[:, 0:128], [ps], [Bw])
    Cre = A([16, 16]); Cim = A([16, 16])
    with nc.allow_non_contiguous_dma(reason="small C transpose load"):
        for src, dst in ((c_re, Cre), (c_im, Cim)):
            for g2 in range(2):
                for gp in range(16):
                    P.dma(dst[g2 * 64:(g2 + 1) * 64, gp, :],
                          bass.AP(tensor=src.ap.tensor, offset=g2 * 1024 + gp * 2048, ap=[[1, 64], [64, 16]]), [src], [dst], q="pool", slow=True)
    P.ms("pool", Cw[:], 0.0, [Cw])
    for gp in range(16):
        c0 = 32 * (gp % 4)
        for g2 in range(2):
            sl = slice(g2 * 64, (g2 + 1) * 64)
            P.cp("dve", Cw[sl, gp, 0, c0 + 16 * g2:c0 + 16 * g2 + 16], Cre[sl, gp, :], [Cre], [Cw])
            P.ts("dve", Cw[sl, gp, 1, c0 + 16 * g2:c0 + 16 * g2 + 16], Cim[sl, gp, :], -1.0, ALU.mult, reads=[Cim], writes=[Cw])
    P.barrier()
    off[0] = mark_tabs

    uTf_r = ring(1, [4, 128]); uTb_r = ring(2, [4, 128], BF16)
    t1_r = ring(1, [2, 128]); t2_r = ring(1, [2, 128]); w_r = ring(1, [2, 128]); r_r = ring(1, [2, 128])
    Apost_r = ring(1, [2, 128]); Bpost_r = ring(1, [2, 128]); sbf_r = ring(2, [2, 128], BF16)
    TA_r = ring(2, [2, 4, 128]); TB_r = ring(2, [2, 4, 128]); W4_r = ring(2, [2, 4, 128]); R4_r = ring(2, [2, 4, 128])
    SBF_r = ring(2, [2, 4, 128], BF16); cin_r = ring(2, [4, 2])
    yv_r = ring(1, [4, 128]); zz_r = ring(1, [4, 128]); zb_r = ring(2, [4, 128], BF16); g1_r = ring(1, [4, 128]); g2_r = ring(1, [4, 128])
    os_r = ring(2, [4, 128]); sq_r = g1_r; mixS_r = ring(2, [4, 128], BF16); mixA_r = ring(2, [4, 128], BF16)
    acc_r = ring(1, [1024]); on_r = ring(1, [512]); x1_r = ring(1, [1024])

    rhoB = A([16, 128])
    P.cp("dve", rhoB[:], rho[:].unsqueeze(2).to_broadcast([128, 16, 128]), [rho], [rhoB])
    P.ms("dve", rhoB[:, :, 0:1], 0.0, [rhoB])
    if stop == 'B0':
        return finish_prog()
    xt_r = ring(2, [1024]); junk = on_r.items[0]; st4 = ring(4, [4]); ub_r = ring(2, [512]); ob_r = ring(2, [512])
    sst = A([16, 2, 1]); gtbc = A([1, 1024]); stmp = A([16]); srow = A([128], F32, 16)
    for b in range(B):
        P.ms("pool", sst[:], 0.0, [sst])
        for which in range(1):
            bcast_rows(T(gtbc[:, which, :], gtbc.buf), DB + b, which)
        if stop == 'B1':
            return finish_prog()
        prev = None
        for t in range(NT + 1):
            if t < NT:
                r0 = b * Tq + t * 128
                xt = xt_r.get(); ub = ub_r.get(); ob = ob_r.get()
                P.dma(xt[:], x_p.ap[r0:r0 + 128, :], [x_p], [xt])
                P.dma(ub[:], u_d.ap[r0:r0 + 128, :], [u_d], [ub])
                P.dma(ob[:], o_d.ap[r0:r0 + 128, :], [o_d], [ob])
                osm = ssm(ub, 128, sst, "scan")
            if prev is not None:
                posm, pob, pxt, pr0 = prev
                x1 = out_proj(posm, pob, pxt, 128, gtbc[:, 0, :], gtbc)
                P.dma(x1_d.ap[pr0:pr0 + 128, :], x1[:], [x1], [x1_d])
            prev = (osm, ob, xt, r0) if t < NT else None
        for ri, dst in enumerate((sre_p, sim_p)):
            ps = psg.get()
            P.cp("dve", stmp[:], sst[:, :, ri, 0], [sst], [stmp])
            P.tr(ps[0:16, 0:128], stmp[:], ident[:], [stmp, ident], [ps])
            P.cp("dve", srow[:], ps[0:16, 0:128], [ps], [srow])
            P.dma(dst.ap[b:b + 1, :].rearrange("o (gp q) -> (o gp) q", q=128), srow[:], [srow], [dst])
    if stop == 'B':
        return finish_prog()
    sst_s = A([16, 2, DB]); sin_t = A([2, 2048], F32, DB); sout_t = sin_t
    P.dma(sin_t[:, 0, :], sre_in.ap, [sre_in], [sin_t])
    P.dma(sin_t[:, 1, :], sim_in.ap, [sim_in], [sin_t])
    for ri in range(2):
        for gq in range(4):
            ps = psg.get()
            for j in range(4):
                gp = gq * 4 + j
                P.tr(ps[:, j * DB:(j + 1) * DB], sin_t[:, ri, gp * 128:(gp + 1) * 128], ident[0:DB, 0:DB], [sin_t, ident], [ps])
            P.cp("dve", sst_s[:, gq * 4:gq * 4 + 4, ri, :], ps[:, 0:4 * DB].rearrange("p (a b) -> p a b", a=4), [ps], [sst_s])
    xs = xt_r.get(); ub = ub_r.get(); ob = ob_r.get()
    P.dma(xs[0:DB, :], x_s.ap, [x_s], [xs])
    P.dma(ub[0:DB, :], u_d.ap[B * Tq:B * Tq + DB, :], [u_d], [ub])
    P.dma(ob[0:DB, :], o_d.ap[B * Tq:B * Tq + DB, :], [o_d], [ob])
    osm = ssm(ub, DB, sst_s, "step")
    for ri, dst in enumerate((sre_s, sim_s)):
        for gq in range(4):
            ps = psg.get()
            for j in range(4):
                gp = gq * 4 + j
                P.tr(ps[0:DB, j * 128:(j + 1) * 128], sst_s[:, gp, ri, :], ident[:], [sst_s, ident], [ps])
            P.cp("dve", sout_t[:, ri, gq * 512:(gq + 1) * 512], ps[0:DB, :], [ps], [sout_t])
        P.dma(dst.ap, sout_t[:, ri, :], [sout_t], [dst])
    x1s = out_proj(osm, ob, xs, DB, gt_tm[0:DB, 0, :], gt_tm)
    P.dma(x1s_d.ap, x1s[0:DB, :], [x1s], [x1s_d])
    P.barrier()
    off[0] = mark_persist
    w_up_bf = A([8, 4096], BF16); w_dn_bf = A([32, 1024], BF16)
    for c in range(4):
        P.dma(w_up_bf[:, :, c * 1024:(c + 1) * 1024], w_up.ap[:, c * 1024:(c + 1) * 1024].rearrange("(kc p) n -> p kc n", p=128), [w_up], [w_up_bf], q="pool")
        P.dma(w_dn_bf[:, c * 8:(c + 1) * 8, :], w_down.ap[c * 1024:(c + 1) * 1024, :].rearrange("(kc p) n -> p kc n", p=128), [w_down], [w_dn_bf], q="pool")
    nf_bc = A([1024]); gtbc = A([1024])
    P.dma(nf_bc[:], norm_final.ap.partition_broadcast(128), [norm_final], [nf_bc])
    xt_r = ring(2, [1024]); xn_r = ring(1, [1024]); junk = A([1024]); st4 = ring(4, [4]); hT_r = ring(2, [8, 128], BF16)
    aT_r = ring(2, [32, 128], BF16); rl_r = ring(2, [128]); x2_r = ring(2, [1024])

    def mlp_tile(xt, ntok, seq, gt2_ap, gt_t, dst_ap, dst_t):
        hT = norm_mod_T(xt, ntok, a2T, sh2T, seq)
        aT = aT_r.get()
        for fc in range(32):
            ps = psg.get()
            for kc in range(8):
                P.mm(ps[:, 0:ntok], w_up_bf[:, kc, fc * 128:(fc + 1) * 128], hT[:, kc, 0:ntok], kc == 0, kc == 7, [w_up_bf, hT], [ps])
            rl = rl_r.get()
            P.act(rl[:, 0:ntok], ps[:, 0:ntok], AF.Relu, reads=[ps], writes=[rl])
            P.tt("pool" if fc % 2 else "dve", aT[:, fc, 0:ntok], rl[:, 0:ntok], rl[:, 0:ntok], ALU.mult, [rl], [aT])
        x2 = x2_r.get()
        for half in range(2):
            ps = psg.get()
            sl = slice(half * 512, (half + 1) * 512)
            for fc in range(32):
                P.mm(ps[0:ntok, :], aT[:, fc, 0:ntok], w_dn_bf[:, fc, sl], fc == 0, fc == 31, [aT, w_dn_bf], [ps])
            P.tt("dve", x2[0:ntok, sl], ps[0:ntok, :], gt2_ap[:, sl], ALU.mult, [ps, gt_t], [x2])
            P.tt("pool", x2[0:ntok, sl], x2[0:ntok, sl], xt[0:ntok, sl], ALU.add, [x2, xt], [x2])
        s = rstd_of(x2[0:ntok, :], x2, ntok, 1024)
        P.stt("dve", x2[0:ntok, :], x2[0:ntok, :], s[0:ntok, 3:4], nf_bc[0:ntok, :], ALU.mult, ALU.mult, [x2, s, nf_bc], [x2])
        P.dma(dst_ap, x2[0:ntok, :], [x2], [dst_t])

    for b in range(B):
        bcast_rows(gtbc, DB + b, 1)
        for t in range(NT):
            r0 = b * Tq + t * 128
            xt = xt_r.get()
            P.dma(xt[:], x1_d.ap[r0:r0 + 128, :], [x1_d], [xt])
            mlp_tile(xt, 128, DB + b, gtbc[:], gtbc, y_p.ap[r0:r0 + 128, :], y_p)
    xs = xt_r.get()
    P.dma(xs[0:DB, :], x1s_d.ap, [x1s_d], [xs])
    mlp_tile(xs, DB, None, gt_tm[0:DB, 1, :], gt_tm, y_s.ap, y_s)
    P.barrier()
    deps = {}
    for o_ in outs_all:
        if o_.buf.lw is not None:
            deps[o_.buf.lw[0]] = max(deps.get(o_.buf.lw[0], 0), o_.buf.lw[1])
    P._waits('sp', deps)
    P.emit(st)
    st.close()
    return nc


def _run(inp, cfg, n_cores):
    B, Tq, DB, NPG = cfg["B"], cfg["T"], cfg["DB"], cfg["NPG"]
    f = lambda a: np.ascontiguousarray(np.asarray(a, dtype=np.float32))
    nc = build(cfg)
    shared = {
        "cache_cmp": f(inp["cache_cmp"][0]).reshape(-1, 256), "cache_slc": f(inp["cache_slc"][0]).reshape(-1, 256),
        "w_ada": f(inp["w_ada"][0]), "b_ada": f(inp["b_ada"][0]), "norm_attn": f(inp["norm_attn"][0]), "w_in": f(inp["w_in"][0]),
        "lam_re": f(inp["ssm_lambda_re"][0]).reshape(-1), "lam_im": f(inp["ssm_lambda_im"][0]).reshape(-1),
        "log_dt": f(inp["ssm_log_dt"][0]), "b_re": f(inp["ssm_b_re"][0]).reshape(-1), "b_im": f(inp["ssm_b_im"][0]).reshape(-1),
        "c_re": f(inp["ssm_c_re"][0]).reshape(-1), "c_im": f(inp["ssm_c_im"][0]).reshape(-1), "ssm_d": f(inp["ssm_d"][0]),
        "w_glu": f(inp["ssm_w_glu"][0]), "pe_k": f(inp["cmp_pe_k"][0]).reshape(-1), "w1_k": f(inp["cmp_w1_k"][0]).reshape(2048, 128),
        "w2_k": f(inp["cmp_w2_k"][0]), "pe_v": f(inp["cmp_pe_v"][0]).reshape(-1), "w1_v": f(inp["cmp_w1_v"][0]).reshape(2048, 128),
        "w2_v": f(inp["cmp_w2_v"][0]), "n_ssm": f(inp["norm_out_ssm"][0]), "n_attn": f(inp["norm_out_attn"][0]),
        "w_out": f(inp["w_out"][0]), "norm_mlp": f(inp["norm_mlp"][0]), "w_up": f(inp["w_up"][0]), "w_down": f(inp["w_down"][0]),
        "norm_final": f(inp["norm_final"]),
    }
    used = set(a.memorylocations[0].name for a in nc.allocations if getattr(a, "kind", None) == "ExternalInput")
    maps = []
    for c in range(n_cores):
        m = dict(shared)
        m["x_p"] = f(inp["x_prompt"][c * B:(c + 1) * B]).reshape(B * Tq, 1024)
        m["x_s"] = f(inp["x_sample"][c * DB:(c + 1) * DB]).reshape(DB, 1024)
        m["c_all"] = np.concatenate([f(inp["c_sample"][c * DB:(c + 1) * DB]), f(inp["c_prompt"][c * B:(c + 1) * B])], axis=0)
        m["state_win"] = f(inp["state_win"][0, c * DB:(c + 1) * DB]).reshape(DB * 512, 256)
        m["ssm_re_in"] = f(inp["state_ssm_re"][0, c * DB:(c + 1) * DB]).reshape(DB, 2048)
        m["ssm_im_in"] = f(inp["state_ssm_im"][0, c * DB:(c + 1) * DB]).reshape(DB, 2048)
        m["ptab"] = np.ascontiguousarray(np.asarray(inp["page_table"][c * DB:(c + 1) * DB], dtype=np.int32)).reshape(DB * NPG, 1)
        maps.append({k: v for k, v in m.items() if k in used})
    res = run_bass_kernel_spmd(nc, maps, core_ids=list(range(n_cores))).results
    cat = lambda k: np.concatenate([r[k] for r in res], axis=0)
    NB = n_cores * B
    ND = n_cores * DB
    return (cat("y_p").reshape(NB, Tq, 1024), cat("y_s").reshape(ND, 1, 1024),
            cat("cmp_p").reshape(1, NB, Tq, 2, 2, 64), cat("slc_p").reshape(1, NB, Tq, 2, 2, 64),
            cat("win_p").reshape(1, NB, 512, 2, 2, 64),
            cat("sre_p").reshape(1, NB, 32, 64), cat("sim_p").reshape(1, NB, 32, 64),
            cat("cmp_s").reshape(1, ND, 1, 2, 2, 64), cat("slc_s").reshape(1, ND, 1, 2, 2, 64),
            cat("win_s").reshape(1, ND, 512, 2, 2, 64),
            cat("sre_s").reshape(1, ND, 32, 64), cat("sim_s").reshape(1, ND, 32, 64))


def kernel(**inputs):
    cfg = dict(B=2, T=4096, DB=16, NPG=64, NPHYS=10240, NSEL=16)
    return _run(inputs, cfg, 8)
```

```python
import math
from contextlib import ExitStack
import numpy as np
import concourse.bass as bass
import concourse.mybir as mybir
from concourse.bass_utils import run_bass_kernel_spmd

F32 = mybir.dt.float32
BF16 = mybir.dt.bfloat16
I32 = mybir.dt.int32
AF = mybir.ActivationFunctionType
ALU = mybir.AluOpType

NLANES = 10
INS_LINES = {}
NSW = 4
PE_NOP_AFTER_WAIT = False
BIG = 1.0e4
EPS = 1e-6
GC = 1.5957691216057308


class Buf:
    __slots__ = ("lw", "rd")

    def __init__(self):
        self.lw = None
        self.rd = {}


class T:
    def __init__(self, ap, buf=None):
        self.ap = ap
        self.buf = buf if buf is not None else Buf()

    def __getitem__(self, k):
        return self.ap[k]


def _bufs(xs):
    return [x.buf if isinstance(x, T) else x for x in xs]


class Prog:
    def __init__(self, nc):
        self.nc = nc
        self.units = ["pe", "act", "dve", "pool"] + ["L%d" % i for i in range(NLANES)] + ["G%d" % i for i in range(NSW)]
        self.q = {e: [] for e in ("pe", "act", "dve", "pool", "sp")}
        self.cnt = {u: 0 for u in self.units}
        self.seen = {e: {u: 0 for u in self.units} for e in self.q}
        self.lane_rr = 0
        self.sw_rr = 0
        self.n_ins = 0

    def _deps(self, reads, writes):
        deps = {}
        for b in reads:
            if b.lw is not None and deps.get(b.lw[0], 0) < b.lw[1]:
                deps[b.lw[0]] = b.lw[1]
        for b in writes:
            if b.lw is not None and deps.get(b.lw[0], 0) < b.lw[1]:
                deps[b.lw[0]] = b.lw[1]
            for u, n in b.rd.items():
                if deps.get(u, 0) < n:
                    deps[u] = n
        return deps

    def _waits(self, q, deps, me=None, raw=False):
        for u, n in deps.items():
            if u == me and not raw:
                continue
            if self.seen[q][u] >= n:
                continue
            self.seen[q][u] = n
            self.q[q].append(("w", u, n * 16 if u[0] in "LG" else n))

    def op(self, eng, fn, reads=(), writes=()):
        reads = _bufs(reads)
        writes = _bufs(writes)
        deps = self._deps(reads, writes)
        raw = eng != "pe"
        nq0 = len(self.q[eng])
        self._waits(eng, deps, me=eng, raw=raw)
        if eng == "pe" and len(self.q[eng]) > nq0 and PE_NOP_AFTER_WAIT:
            self.q[eng].append(("n",))
        self.cnt[eng] += 1
        n = self.cnt[eng]
        import sys as _s
        fr = _s._getframe(1)
        while fr.f_code.co_name in ('op','mm','tr','act','tt','ts','stt','cp','ms'):
            fr = fr.f_back
        self.q[eng].append(("i", fn, eng, fr.f_lineno))
        self.n_ins += 1
        for b in reads:
            if b.rd.get(eng, 0) < n:
                b.rd[eng] = n
        for b in writes:
            b.lw = (eng, n)
            b.rd = {}

    def dma(self, out, in_, reads=(), writes=(), q="sp", indirect=None, slow=False):
        reads = _bufs(reads)
        writes = _bufs(writes)
        if q == "pool":
            lane = "G%d" % self.sw_rr
            self.sw_rr = (self.sw_rr + 1) % NSW
        else:
            lane = "L%d" % self.lane_rr
            self.lane_rr = (self.lane_rr + 1) % NLANES
        deps = self._deps(reads, writes)
        if self.cnt[lane] > 0:
            deps[lane] = max(deps.get(lane, 0), self.cnt[lane])
        self._waits(q, deps)
        self.cnt[lane] += 1
        n = self.cnt[lane]
        if indirect is not None:
            fn = lambda e: e.indirect_dma_start(out=out, out_offset=None, in_=in_,
                                                in_offset=bass.IndirectOffsetOnAxis(ap=indirect, axis=0))
        else:
            fn = (lambda e: e.dma_start(out=out, in_=in_, allow_slow_non_contiguous=True)) if slow else (lambda e: e.dma_start(out=out, in_=in_))
        import sys as _s
        fr = _s._getframe(1)
        self.q[q].append(("i", fn, lane, fr.f_lineno))
        self.n_ins += 1
        for b in reads:
            if b.rd.get(lane, 0) < n:
                b.rd[lane] = n
        for b in writes:
            b.lw = (lane, n)
            b.rd = {}

    def barrier(self):
        deps = {u: n for u, n in self.cnt.items() if n > 0}
        for q in self.q:
            self._waits(q, dict(deps))

    def _pe_mode(self, mode):
        self.pe_mode = mode

    @staticmethod
    def _r(n):
        return 32 if n <= 32 else (64 if n <= 64 else 128)

    def mm(self, out, lhsT, rhs, start=True, stop=True, reads=(), writes=()):
        self._pe_mode(("mm", self._r(lhsT.shape[0]), self._r(int(np.prod(lhsT.shape[1:]))), str(lhsT.dtype)))
        self.op("pe", lambda e: e.matmul(out, lhsT=lhsT, rhs=rhs, start=start, stop=stop,
                                         skip_group_check=True), reads, writes)

    def tr(self, out, in_, ident, reads=(), writes=()):
        self._pe_mode(("tr", self._r(in_.shape[0]), self._r(int(np.prod(in_.shape[1:])))))
        self.op("pe", lambda e: e.transpose(out, in_, ident), reads, writes)

    def act(self, out, in_, func, scale=1.0, bias=0.0, accum=None, reads=(), writes=()):
        if accum is None:
            self.op("act", lambda e: e.activation(out=out, in_=in_, func=func, scale=scale, bias=bias),
                    reads, writes)
        else:
            self.op("act", lambda e: e.activation(out=out, in_=in_, func=func, scale=scale, bias=bias,
                                                  accum_out=accum), reads, writes)

    def tt(self, eng, out, in0, in1, op, reads=(), writes=()):
        self.op(eng, lambda e: e.tensor_tensor(out=out, in0=in0, in1=in1, op=op), reads, writes)

    def ts(self, eng, out, in0, s1, op0, s2=None, op1=None, reads=(), writes=()):
        if op1 is None:
            self.op(eng, lambda e: e.tensor_scalar(out=out, in0=in0, scalar1=s1, scalar2=None, op0=op0),
                    reads, writes)
        else:
            self.op(eng, lambda e: e.tensor_scalar(out=out, in0=in0, scalar1=s1, scalar2=s2, op0=op0, op1=op1),
                    reads, writes)

    def stt(self, eng, out, in0, scalar, in1, op0, op1, reads=(), writes=()):
        self.op(eng, lambda e: e.scalar_tensor_tensor(out=out, in0=in0, scalar=scalar, in1=in1, op0=op0, op1=op1),
                reads, writes)

    def cp(self, eng, out, in_, reads=(), writes=()):
        if eng == "act":
            self.op("act", lambda e: e.copy(out=out, in_=in_), reads, writes)
        else:
            self.op(eng, lambda e: e.tensor_copy(out=out, in_=in_), reads, writes)

    def ms(self, eng, ap, val, writes=()):
        self.op(eng, lambda e: e.memset(ap, val), (), writes)

    def emit(self, stack):
        nc = self.nc
        sems = {u: stack.enter_context(nc.semaphore("s_" + u)) for u in self.units}
        block = stack.enter_context(nc.Block())

        def run(eng, items):
            for it in items:
                if it[0] == "w":
                    eng.wait_ge(sems[it[1]], it[2])
                elif it[0] == "n":
                    eng.nop(nofuse=True)
                else:
                    bi = it[1](eng)
                    bi.then_inc(sems[it[2]], 16 if it[2][0] in "LG" else 1)
                    if len(it) > 3:
                        try:
                            INS_LINES[str(bi.ins.name)] = it[3]
                        except Exception:
                            pass

        @block.tensor
        def _(e):
            run(e, self.q["pe"])

        @block.scalar
        def _(e):
            run(e, self.q["act"])

        @block.vector
        def _(e):
            run(e, self.q["dve"])

        @block.gpsimd
        def _(e):
            run(e, self.q["pool"])

        @block.sync
        def _(e):
            run(e, self.q["sp"])


class _Stop(Exception):
    pass


class Ring:
    def __init__(self, items):
        self.items = items
        self.i = 0

    def get(self):
        t = self.items[self.i]
        self.i = (self.i + 1) % len(self.items)
        return t


def build(cfg):
    B, Tq, DB, NPG, NPHYS, NSEL = cfg["B"], cfg["T"], cfg["DB"], cfg["NPG"], cfg["NPHYS"], cfg["NSEL"]
    NT = Tq // 128
    PAST = NPG * 128
    NS = DB + B
    NBP = Tq // 64
    NBS = PAST // 64 + 1
    NBSm = NBS - 1
    NCS = PAST // 16 - 1
    assert NBP <= 128 and NBSm <= 128 and PAST >= 512 and Tq >= 512
    nc = bass.Bass("TRN2", target_bir_lowering=False)
    P = Prog(nc)
    global _LASTP
    _LASTP = P

    def din(name, shape, dt=F32):
        return T(nc.dram_tensor(name, shape, dt, kind="ExternalInput").ap())

    def dout(name, shape):
        return T(nc.dram_tensor(name, shape, F32, kind="ExternalOutput").ap())

    x_p = din("x_p", [B * Tq, 1024]); x_s = din("x_s", [DB, 1024]); c_all = din("c_all", [NS, 1024])
    cache_cmp = din("cache_cmp", [NPHYS * 128, 256]); cache_slc = din("cache_slc", [NPHYS * 128, 256])
    state_win = din("state_win", [DB * 512, 256])
    sre_in = din("ssm_re_in", [DB, 2048]); sim_in = din("ssm_im_in", [DB, 2048])
    ptab = din("ptab", [DB * NPG, 1], I32)
    w_ada = din("w_ada", [1024, 6144]); b_ada = din("b_ada", [6144]); norm_attn = din("norm_attn", [1024])
    w_in = din("w_in", [1024, 1816]); lam_re = din("lam_re", [2048]); lam_im = din("lam_im", [2048])
    log_dt = din("log_dt", [32]); b_re = din("b_re", [32 * 64 * 16]); b_im = din("b_im", [32 * 64 * 16])
    c_re = din("c_re", [32 * 16 * 64]); c_im = din("c_im", [32 * 16 * 64]); ssm_d = din("ssm_d", [512])
    w_glu = din("w_glu", [512, 512])
    pe_k = din("pe_k", [2048]); w1_k = din("w1_k", [2048, 128]); w2_k = din("w2_k", [128, 64])
    pe_v = din("pe_v", [2048]); w1_v = din("w1_v", [2048, 128]); w2_v = din("w2_v", [128, 64])
    n_ssm = din("n_ssm", [512]); n_attn = din("n_attn", [512]); w_out = din("w_out", [1024, 1024])
    norm_mlp = din("norm_mlp", [1024]); w_up = din("w_up", [1024, 4096]); w_down = din("w_down", [4096, 1024])
    norm_final = din("norm_final", [1024])

    y_p = dout("y_p", [B * Tq, 1024]); y_s = dout("y_s", [DB, 1024])
    cmp_p = dout("cmp_p", [B * Tq, 256]); slc_p = dout("slc_p", [B * Tq, 256]); win_p = dout("win_p", [B * 512, 256])
    sre_p = dout("sre_p", [B, 2048]); sim_p = dout("sim_p", [B, 2048])
    cmp_s = dout("cmp_s", [DB, 256]); slc_s = dout("slc_s", [DB, 256]); win_s = dout("win_s", [DB * 512, 256])
    sre_s = dout("sre_s", [DB, 2048]); sim_s = dout("sim_s", [DB, 2048])
    x1_d = T(nc.dram_tensor("x1_scratch", [B * Tq, 1024], F32, kind="ExternalOutput" if cfg.get("dbg") else "Internal").ap())
    outs_all = [y_p, y_s, cmp_p, slc_p, win_p, sre_p, sim_p, cmp_s, slc_s, win_s, sre_s, sim_s]

    st = ExitStack()
    ARENA_W = 53000
    arena = st.enter_context(nc.sbuf_tensor("arena", [128, ARENA_W], F32))
    off = [0]

    def A(shape, dt=F32, npart=128):
        n = int(np.prod(shape))
        w = n if dt != BF16 else (n + 1) // 2
        w = (w + 1) // 2 * 2
        assert off[0] + w <= ARENA_W, ("arena overflow", off[0], w)
        a = arena[0:npart, off[0]:off[0] + w]
        off[0] += w
        if dt != F32:
            a = a.bitcast(dt)
        a = a[:, 0:n]
        if len(shape) == 2:
            a = a.rearrange("p (a b) -> p a b", a=shape[0])
        elif len(shape) == 3:
            a = a.rearrange("p (a b c) -> p a b c", a=shape[0], b=shape[1])
        elif len(shape) == 4:
            a = a.rearrange("p (a b c d) -> p a b c d", a=shape[0], b=shape[1], c=shape[2])
        return T(a)

    def ring(n, shape, dt=F32, npart=128):
        return Ring([A(shape, dt, npart) for _ in range(n)])

    psb = [T(st.enter_context(nc.psum_tensor("psb%d" % i, [128, 512], F32))[:]) for i in range(8)]
    psg = Ring(psb[0:2])
    psa = Ring(psb[2:6])
    pso = Ring(psb[6:8])

    def finish_prog():
        P.barrier()
        deps = {}
        for o_ in outs_all:
            if o_.buf.lw is not None:
                deps[o_.buf.lw[0]] = max(deps.get(o_.buf.lw[0], 0), o_.buf.lw[1])
        P._waits('sp', deps)
        P.emit(st)
        st.close()
        return nc

    stop = cfg.get("stop", "")
    ident = A([128]); ones = A([128]); tri = A([128], BF16); anti = A([128], BF16)
    P.ms("pool", ones[:], 1.0, [ones])
    P.ms("pool", ident[:], 1.0, [ident])
    P.op("pool", lambda e: e.affine_select(out=ident[:], in_=ident[:], pattern=[[1, 128]], compare_op=ALU.is_equal,
                                           fill=0.0, base=0, channel_multiplier=-1), [ident], [ident])
    P.op("pool", lambda e: e.affine_select(out=tri[:], in_=ones[:], pattern=[[1, 128]], compare_op=ALU.is_ge,
                                           fill=0.0, base=0, channel_multiplier=-1), [ones], [tri])
    P.op("pool", lambda e: e.affine_select(out=anti[:], in_=ones[:], pattern=[[-1, 128]], compare_op=ALU.is_gt,
                                           fill=0.0, base=0, channel_multiplier=1), [ones], [anti])

    def loadT(vec, k, dst):
        tmp = A([128], F32)
        P.dma(tmp[0:k, :], vec.ap.rearrange("(k p) -> k p", p=128), [vec], [tmp])
        ps = psg.get()
        P.tr(ps[:, 0:k], tmp[0:k, :], ident[0:k, 0:k], [tmp, ident], [ps])
        return ps

    nattnT = A([8]); nmlpT = A([8]); dT = A([4]); gST = A([4]); gAT = A([4])
    for vec, k, dst in ((norm_attn, 8, nattnT), (norm_mlp, 8, nmlpT), (ssm_d, 4, dT), (n_ssm, 4, gST), (n_attn, 4, gAT)):
        ps = loadT(vec, k, dst)
        P.cp("dve", dst[:], ps[:, 0:k], [ps], [dst])
    nf_bc = A([1024])
    P.dma(nf_bc[:], norm_final.ap.partition_broadcast(128), [norm_final], [nf_bc])

    if stop == 'c0':
        return finish_prog()
    a1T = A([8, NS]); sh1T = A([8, NS]); a2T = A([8, NS]); sh2T = A([8, NS]); gt_tm = A([2, 1024], F32, NS); selr = A([128], F32, NS)
    mark_persist = off[0]
    csb = A([1024], F32, NS); csil = A([1024], F32, NS); cT = A([8, NS], BF16)
    P.dma(csb[:], c_all.ap, [c_all], [csb])
    P.act(csil[:], csb[:], AF.Silu, reads=[csb], writes=[csil])
    ps = psg.get()
    for kc in range(8):
        P.tr(ps[:, kc * NS:(kc + 1) * NS], csil[:, kc * 128:(kc + 1) * 128], ident[0:NS, 0:NS], [csil, ident], [ps])
    P.cp("dve", cT[:], ps[:, 0:8 * NS].rearrange("p (a b) -> p a b", a=8), [ps], [cT])
    wblk = ring(2, [8, 512], BF16); bbc = ring(2, [512], F32, NS); modb = ring(2, [512], F32, NS)
    featdst = {0: sh1T, 1: a1T, 3: sh2T, 4: a2T}
    for cb in range(12):
        wb = wblk.get(); bb = bbc.get(); mb = modb.get()
        P.dma(wb[:], w_ada.ap[:, cb * 512:(cb + 1) * 512].rearrange("(kc p) n -> p kc n", p=128), [w_ada], [wb], q="pool")
        P.dma(bb[:], b_ada.ap[cb * 512:(cb + 1) * 512].partition_broadcast(NS), [b_ada], [bb])
        ps = psg.get()
        for kc in range(8):
            P.mm(ps[0:NS, :], cT[:, kc, :], wb[:, kc, :], kc == 0, kc == 7, [cT, wb], [ps])
        which, half = cb // 2, cb % 2
        if which in (2, 5):
            P.tt("dve", gt_tm[:, 0 if which == 2 else 1, half * 512:(half + 1) * 512], ps[0:NS, :], bb[:], ALU.add,
                 [ps, bb], [gt_tm])
        else:
            P.tt("dve", mb[:], ps[0:NS, :], bb[:], ALU.add, [ps, bb], [mb])
            ps2 = psg.get()
            for j in range(4):
                P.tr(ps2[:, j * NS:(j + 1) * NS], mb[:, j * 128:(j + 1) * 128], ident[0:NS, 0:NS], [mb, ident], [ps2])
            dst = featdst[which]
            P.cp("dve", dst[:, half * 4:half * 4 + 4, :], ps2[:, 0:4 * NS].rearrange("p (a b) -> p a b", a=4), [ps2], [dst])
    for kc in range(8):
        P.ts("dve", a1T[:, kc, :], a1T[:, kc, :], 1.0, ALU.add, nattnT[:, kc:kc + 1], ALU.mult, [a1T, nattnT], [a1T])
        P.ts("dve", a2T[:, kc, :], a2T[:, kc, :], 1.0, ALU.add, nmlpT[:, kc:kc + 1], ALU.mult, [a2T, nmlpT], [a2T])
    P.barrier()
    off[0] = mark_persist

    if stop == 'mod':
        return finish_prog()
    w_in_bf = A([8, 1816], BF16)
    W1 = A([2, 32, 128], BF16, 64); W2 = A([2, 64], BF16)
    for kc in range(8):
        P.dma(w_in_bf[:, kc, :], w_in.ap[kc * 128:(kc + 1) * 128, :], [w_in], [w_in_bf], q="pool")
    for kv, (w1, w2) in enumerate(((w1_k, w2_k), (w1_v, w2_v))):
        P.dma(W1[:, kv, :, :], w1.ap.rearrange("(j d) h -> d j h", d=64), [w1], [W1], q="pool")
        P.dma(W2[:, kv, :], w2.ap, [w2], [W2], q="pool")
    peT = A([2, 32], BF16, 64); cbias = A([2])
    with nc.allow_non_contiguous_dma(reason="tiny pe transpose load"):
        for kv, pe in enumerate((pe_k, pe_v)):
            P.dma(peT[:, kv, :], pe.ap.rearrange("(j d) -> d j", d=64), [pe], [peT], q="pool", slow=True)
    ps = psg.get()
    for kv in range(2):
        for j in range(32):
            P.mm(ps[:, kv:kv + 1], W1[:, kv, j, :], peT[:, kv, j:j + 1], j == 0, j == 31, [W1, peT], [ps])
    P.cp("dve", cbias[:], ps[:, 0:2], [ps], [cbias])

    if stop == 'w':
        return finish_prog()
    NKTP = max(1, (Tq // 16 + 127) // 128)
    NKTS = (NCS + 127) // 128
    NKTC = max(NKTP, NKTS)
    NBm = max(NBP, NBSm)
    Mm = A([NKTC, NBm], BF16)
    NKE = max(NT, NPG)
    Eall = A([NKE, 128], BF16)
    Rtab = A([128]); sbias = A([NBS], F32, 1)
    mark_seltmp = off[0]
    mA = A([NKTC, NBm]); mB = A([NKTC, NBm]); etmp = A([NKE, 128]); rel = A([128]); r2 = A([128])
    P.ms("pool", mA[:], 1.0, [mA]); P.ms("pool", mB[:], 1.0, [mB])
    for (tm, lo, hi) in ((mA, -1, 3), (mB, 0, 2)):
        P.op("pool", lambda e, tm=tm, lo=lo: e.affine_select(out=tm[:], in_=tm[:], pattern=[[128, NKTC], [-4, NBm]],
                                                             compare_op=ALU.is_ge, fill=0.0, base=-lo, channel_multiplier=1), [tm], [tm])
        P.op("pool", lambda e, tm=tm, hi=hi: e.affine_select(out=tm[:], in_=tm[:], pattern=[[-128, NKTC], [4, NBm]],
                                                             compare_op=ALU.is_ge, fill=0.0, base=hi, channel_multiplier=-1), [tm], [tm])
    P.tt("dve", Mm[:], mA[:], mB[:], ALU.add, [mA, mB], [Mm])
    P.ms("pool", etmp[:], 1.0, [etmp])
    P.op("pool", lambda e: e.affine_select(out=etmp[:], in_=etmp[:], pattern=[[128, NKE], [1, 128]], compare_op=ALU.is_ge,
                                           fill=0.0, base=0, channel_multiplier=-64), [etmp], [etmp])
    P.op("pool", lambda e: e.affine_select(out=etmp[:], in_=etmp[:], pattern=[[-128, NKE], [-1, 128]], compare_op=ALU.is_ge,
                                           fill=0.0, base=63, channel_multiplier=64), [etmp], [etmp])
    P.cp("dve", Eall[:], etmp[:], [etmp], [Eall])
    P.op("pool", lambda e: e.iota(rel[:], pattern=[[1, 128]], base=-63, channel_multiplier=0,
                                  allow_small_or_imprecise_dtypes=True), [], [rel])
    P.op("pool", lambda e: e.affine_select(out=r2[:], in_=ones[:], pattern=[[0, 128]], compare_op=ALU.is_ge,
                                           fill=0.0, base=-64, channel_multiplier=1), [ones], [r2])
    P.tt("dve", rel[:], rel[:], r2[:], ALU.subtract, [rel, r2], [rel])
    P.ts("dve", Rtab[:], rel[:], -1.0, ALU.is_ge, BIG, ALU.mult, [rel], [Rtab])
    P.ts("dve", r2[:], rel[:], 1.0, ALU.is_ge, -2 * BIG, ALU.mult, [rel], [r2])
    P.tt("dve", Rtab[:], Rtab[:], r2[:], ALU.add, [Rtab, r2], [Rtab])
    P.ms("pool", sbias[:], 0.0, [sbias])
    for j in (0, NBS - 2, NBS - 1):
        P.ms("pool", sbias[:, j:j + 1], BIG, [sbias])
    P.barrier()
    off[0] = mark_seltmp
    mark_mixer = off[0]

    if stop == 'setup':
        return finish_prog()
    xt_r = ring(2, [1024]); xn_r = ring(1, [1024]); junk = A([1024]); st4 = ring(4, [4])
    hT_r = ring(2, [8, 128], BF16); pr_r = ring(2, [1816])

    def rstd_of(src_ap, src_t, ntok, width):
        s = st4.get()
        P.act(junk[0:ntok, 0:width], src_ap, AF.Square, accum=s[0:ntok, 0:1], reads=[src_t], writes=[junk, s])
        P.ts("dve", s[0:ntok, 1:2], s[0:ntok, 0:1], 1.0 / width, ALU.mult, EPS, ALU.add, [s], [s])
        P.act(s[0:ntok, 2:3], s[0:ntok, 1:2], AF.Sqrt, reads=[s], writes=[s])
        P.op("dve", lambda e: e.reciprocal(out=s[0:ntok, 3:4], in_=s[0:ntok, 2:3]), [s], [s])
        return s

    def norm_mod_T(xt, ntok, aT, shT, seq):
        s = rstd_of(xt[0:ntok, :], xt, ntok, 1024)
        xn = xn_r.get()
        P.ts("dve", xn[0:ntok, :], xt[0:ntok, :], s[0:ntok, 3:4], ALU.mult, reads=[xt, s], writes=[xn])
        hT = hT_r.get()
        if cfg.get("dummy", 0) == 3:
            for _ in range(3):
                P.cp("dve", junk[0:ntok, 0:1024], xn[0:ntok, :], [xn], [junk, xn])
        for half in range(2):
            ps = psg.get()
            if half == 0 and cfg.get("dummy", 0) == 1:
                P.tr(ps[:, 0:128], ident[:], ident[:], [ident], [ps])
            for j in range(4):
                kc = half * 4 + j
                P.tr(ps[:, j * ntok:(j + 1) * ntok], xn[0:ntok, kc * 128:(kc + 1) * 128], ident[0:ntok, 0:ntok], [xn, ident], [ps])
            for j in range(4):
                kc = half * 4 + j
                if seq is not None and cfg.get("dummy", 0) == 2:
                    P.ts("dve", hT[:, kc, 0:ntok], ps[:, j * ntok:(j + 1) * ntok], aT[:, kc, seq:seq + 1], ALU.mult,
                         shT[:, kc, seq:seq + 1], ALU.add, [ps, aT, shT], [hT])
                elif seq is not None:
                    P.act(hT[:, kc, 0:ntok], ps[:, j * ntok:(j + 1) * ntok], AF.Identity, scale=aT[:, kc, seq:seq + 1],
                          bias=shT[:, kc, seq:seq + 1], reads=[ps, aT, shT], writes=[hT])
                else:
                    P.tt("dve", hT[:, kc, 0:ntok], ps[:, j * ntok:(j + 1) * ntok], aT[:, kc, 0:ntok], ALU.mult, [ps, aT], [hT])
                    P.tt("dve", hT[:, kc, 0:ntok], hT[:, kc, 0:ntok], shT[:, kc, 0:ntok], ALU.add, [hT, shT], [hT])
        return hT

    def proj(hT, ntok):
        pr = pr_r.get()
        for cb, (c0, c1_) in enumerate(((0, 512), (512, 1024), (1024, 1536), (1536, 1816))):
            ps = psg.get()
            for kc in range(8):
                P.mm(ps[0:ntok, 0:c1_ - c0], hT[:, kc, 0:ntok], w_in_bf[:, kc, c0:c1_], kc == 0, kc == 7, [hT, w_in_bf], [ps])
            P.cp("act" if cb % 2 else "dve", pr[0:ntok, c0:c1_], ps[0:ntok, 0:c1_ - c0], [ps], [pr])
        return pr

    xg_r = ring(2, [4, 8 if True else 0]);

    def gelu_tanh(eng2, out_ap, out_t, x_ap, x_t, shape_tmp):
        t1, t2 = shape_tmp
        P.tt(eng2, t1, x_ap, x_ap, ALU.mult, [x_t], [t1.buf if isinstance(t1, T) else out_t])

    NQM = 128
    qT_r = ring(2, [8, 128], BF16, 64)
    Pt_r = ring(4, [4, 128], BF16)
    mk2_r = ring(2, [128], BF16)
    Ocat_r = ring(1, [3, 8, 65])
    o_r = ring(2, [512])
    imp_r = ring(2, [2, NBm]); sc_r = ring(2, [NBS]); wk_r = ring(2, [NBS]); m8_r = ring(2, [8]); sel_r = ring(2, [NBS])
    selT_r = ring(2, [128], BF16)
    rd_r = ring(2, [24]); cf_r = ring(2, [24]); tmpo_r = ring(1, [24, 64]); sg_r = ring(2, [24])
    VcA = A([NKTC, 2, 65], BF16)
    P.ms("pool", VcA[:], 1.0, [VcA])
    Hx_r = ring(2, [4, 8]); Hg_r = ring(2, [4, 8], BF16)
    Ht1_r = ring(2, [4, 8]); Ht2_r = ring(2, [4, 8])

    def compress(SC, XcT, col0, nb, n0):
        ps = psg.get()
        for kvg in range(4):
            for j in range(32):
                P.mm(ps[:, kvg * nb:(kvg + 1) * nb] if nb <= 128 else None, W1[:, kvg // 2, j, :],
                     XcT[:, kvg, col0 + j:col0 + j + 16 * (nb - 1) + 1:16], j == 0, j == 31, [W1, XcT], [ps]) if nb <= 128 else None
        return ps

    def compress_blocks(SC, XcT, col0, nb, n0):
        hx = Hx_r.get(); hg = Hg_r.get(); t1 = Ht1_r.get(); t2 = Ht2_r.get()
        for kvg in range(4):
            ps = psg.get()
            for j in range(32):
                P.mm(ps[:, 0:nb], W1[:, kvg // 2, j, :],
                     XcT[:, kvg, col0 + j:col0 + j + 16 * (nb - 1) + 1:16], j == 0, j == 31, [W1, XcT], [ps])
            P.act(hx[:, kvg, 0:nb], ps[:, 0:nb], AF.Identity, bias=cbias[:, kvg // 2:kvg // 2 + 1], reads=[ps, cbias], writes=[hx])
            yield
        X = hx[:, :, 0:nb]
        P.tt("pool", t1[:, :, 0:nb], X, X, ALU.mult, [hx], [t1])
        P.ts("dve", t1[:, :, 0:nb], t1[:, :, 0:nb], 0.044715, ALU.mult, 1.0, ALU.add, [t1], [t1])
        P.tt("pool", t1[:, :, 0:nb], t1[:, :, 0:nb], X, ALU.mult, [t1, hx], [t1])
        P.act(t2[:, :, 0:nb], t1[:, :, 0:nb], AF.Sigmoid, scale=GC, reads=[t1], writes=[t2])
        P.tt("dve", hg[:, :, 0:nb], X, t2[:, :, 0:nb], ALU.mult, [hx, t2], [hg])
        yield
        for c in range(0, nb, 256):
            w = min(256, nb - c)
            ps = psg.get()
            for g in range(2):
                P.mm(ps[0:64, g * w:(g + 1) * w], W2[:, 0, :], hg[:, g, c:c + w], True, True, [W2, hg], [ps])
            P.cp("dve", SC["KcT"][:, :, n0 + c:n0 + c + w], ps[0:64, 0:2 * w].rearrange("p (a b) -> p a b", a=2), [ps], [SC["KcT"]])
        P.cp("pool", SC["GvT"][:, :, n0:n0 + nb], hg[:, 2:4, 0:nb], [hg], [SC["GvT"]])

    def attend(SC, qT, NQ, t, qpos0, nkt_c, slc_kts, win_kts, sig, prompt, getkv, tick=None):
        Ocat = Ocat_r.get()
        NB = NBP if prompt else NBS
        NBmm = NBP if prompt else NBSm
        imp = imp_r.get()
        for ktc in range(nkt_c):
            ps = psg.get()
            for g in range(2):
                P.mm(ps[:, g * 64:(g + 1) * 64], SC["GvT"][:, g, ktc * 128:(ktc + 1) * 128], W2[:, 1, :], True, True,
                     [SC["GvT"], W2], [ps])
            P.cp("act", VcA[:, ktc, :, 0:64], ps[:, 0:128].rearrange("p (a b) -> p a b", a=2), [ps], [VcA])
        rd = rd_r.get()
        selTs = []
        for g in range(2):
            po1 = pso.get(); po2 = pso.get()
            for ktc in range(nkt_c):
                ps = psg.get()
                P.mm(ps[:, 0:4 * NQ], SC["KcT"][:, g, ktc * 128:(ktc + 1) * 128], qT[:, 4 * g:4 * g + 4, 0:NQ], True, True,
                     [SC["KcT"], qT], [ps])
                pt = Pt_r.get()
                P.act(pt[:, :, 0:NQ], ps[:, 0:4 * NQ].rearrange("p (a b) -> p a b", a=4), AF.Exp, reads=[ps], writes=[pt])
                base = qpos0 - 2048 * ktc - 31
                if base - 2032 < 0:
                    P.op("pool", lambda e, pt=pt, base=base: e.affine_select(
                        out=pt[:, :, 0:NQ], in_=pt[:, :, 0:NQ], pattern=[[0, 4], [1, NQ]], compare_op=ALU.is_ge, fill=0.0,
                        base=base, channel_multiplier=-16), [pt], [pt])
                for r in range(4):
                    P.mm(po1[0:NQ, r * 65:(r + 1) * 65], pt[:, r, 0:NQ], VcA[:, ktc, g, :], ktc == 0 and r == 0, ktc == nkt_c - 1, [pt, VcA], [po1])
                    P.mm(po2[0:NQ, r * NBmm:(r + 1) * NBmm], pt[:, r, 0:NQ], Mm[:, ktc, 0:NBmm], ktc == 0 and r == 0, ktc == nkt_c - 1, [pt, Mm], [po2])
            P.cp("act", Ocat[0:NQ, 0, 4 * g:4 * g + 4, :], po1[0:NQ, 0:260].rearrange("p (a b) -> p a b", a=4), [po1], [Ocat])
            P.ts("dve", rd[0:NQ, 4 * g:4 * g + 4], Ocat[0:NQ, 0, 4 * g:4 * g + 4, 64], 1e-30, ALU.max, reads=[Ocat], writes=[rd])
            P.op("dve", lambda e, g=g: e.reciprocal(out=rd[0:NQ, 4 * g:4 * g + 4], in_=rd[0:NQ, 4 * g:4 * g + 4]), [rd], [rd])
            for r in range(4):
                if r == 0:
                    P.ts("dve", imp[0:NQ, g, 0:NBmm], po2[0:NQ, 0:NBmm], rd[0:NQ, 4 * g:4 * g + 1], ALU.mult, reads=[po2, rd], writes=[imp])
                else:
                    P.stt("dve", imp[0:NQ, g, 0:NBmm], po2[0:NQ, r * NBmm:(r + 1) * NBmm], rd[0:NQ, 4 * g + r:4 * g + r + 1],
                          imp[0:NQ, g, 0:NBmm], ALU.mult, ALU.add, [po2, rd, imp], [imp])
            sc = sc_r.get(); wk = wk_r.get(); m8 = m8_r.get(); sel = sel_r.get()
            if prompt:
                P.tt("dve", sc[0:NQ, 0:NB], imp[0:NQ, g, 0:NB], Rtab[0:NQ, 63 - 2 * t:63 - 2 * t + NB], ALU.add, [imp, Rtab], [sc])
                P.ts("dve", sc[0:NQ, 0:1], sc[0:NQ, 0:1], BIG, ALU.add, reads=[sc], writes=[sc])
            else:
                P.tt("dve", sc[0:NQ, 0:NBmm], imp[0:NQ, g, 0:NBmm], sbias[0:NQ, 0:NBmm], ALU.add, [imp, sbias], [sc])
                P.cp("dve", sc[0:NQ, NBmm:NB], sbias[0:NQ, NBmm:NB], [sbias], [sc])
            cur = sc
            for rnd in range(NSEL // 8):
                P.op("dve", lambda e, cur=cur, m8=m8: e.max(out=m8[0:NQ, :], in_=cur[0:NQ, 0:NB]), [cur], [m8])
                if rnd < NSEL // 8 - 1:
                    P.op("dve", lambda e, cur=cur, m8=m8, wk=wk: e.match_replace(out=wk[0:NQ, 0:NB], in_to_replace=m8[0:NQ, :],
                                                                   in_values=cur[0:NQ, 0:NB], imm_value=-3.0e38), [cur, m8], [wk])
                    cur = wk
            P.ts("dve", sel[0:NQ, 0:NB], sc[0:NQ, 0:NB], m8[0:NQ, 7:8], ALU.is_ge, reads=[sc, m8], writes=[sel])
            ps = psg.get()
            P.tr(ps[0:NBmm, 0:NQ], sel[0:NQ, 0:NBmm], ident[0:NQ, 0:NQ], [sel, ident], [ps])
            selT = selT_r.get()
            P.cp("dve", selT[0:NBmm, 0:NQ], ps[0:NBmm, 0:NQ], [ps], [selT])
            selTs.append(selT)
        for br, kts in ((0, slc_kts), (1, win_kts)):
            pos = [pso.get(), pso.get()]
            steps = [(i, kt, mode, g) for i, (kt, mode) in enumerate(kts) for g in range(2)]
            kvc = {}

            def front(step, br=br, kvc=kvc):
                i, kt, mode, g = step
                if g == 0:
                    kvc[i] = getkv(br, kt)
                ktT, vaT = kvc[i]
                psm = None
                if mode in ("sel", "diag") and br == 0:
                    psm = psa.get()
                    P.mm(psm[:, 0:NQ], Eall[0:NBmm, kt, :], selTs[g][0:NBmm, 0:NQ], True, True, [Eall, selTs[g]], [psm])
                ps = psa.get()
                P.mm(ps[:, 0:4 * NQ], ktT[:, g, :], qT[:, 4 * g:4 * g + 4, 0:NQ], True, True, [ktT, qT], [ps])
                return ps, psm, vaT

            def back(step, fr, br=br, pos=pos, nsteps=len(kts)):
                i, kt, mode, g = step
                ps, psm, vaT = fr
                pt = Pt_r.get()
                P.act(pt[:, :, 0:NQ], ps[:, 0:4 * NQ].rearrange("p (a b) -> p a b", a=4), AF.Exp, reads=[ps], writes=[pt])
                if br == 0 and mode == "diag":
                    mk2 = mk2_r.get()
                    P.tt("dve", mk2[:, 0:NQ], psm[:, 0:NQ], tri[:, 0:NQ], ALU.mult, [psm, tri], [mk2])
                    P.tt("pool", pt[:, :, 0:NQ], pt[:, :, 0:NQ], mk2[:, 0:NQ].unsqueeze(1).to_broadcast([128, 4, NQ]), ALU.mult, [pt, mk2], [pt])
                elif br == 0 and mode == "sel":
                    P.tt("dve", pt[:, :, 0:NQ], pt[:, :, 0:NQ], psm[:, 0:NQ].unsqueeze(1).to_broadcast([128, 4, NQ]), ALU.mult, [pt, psm], [pt])
                elif br == 1 and mode in ("diag", "anti"):
                    m = tri if mode == "diag" else anti
                    P.tt("pool", pt[:, :, 0:NQ], pt[:, :, 0:NQ], m[:, 0:NQ].unsqueeze(1).to_broadcast([128, 4, NQ]), ALU.mult, [pt, m], [pt])
                for r in range(4):
                    P.mm(pos[g][0:NQ, r * 65:(r + 1) * 65], pt[:, r, 0:NQ], vaT[:, g, :], i == 0 and r == 0, i == nsteps - 1, [pt, vaT], [pos[g]])

            pending = front(steps[0])
            for n, step in enumerate(steps):
                nxt = front(steps[n + 1]) if n + 1 < len(steps) else None
                back(step, pending)
                pending = nxt
                if tick is not None:
                    tick()
            for g in range(2):
                P.cp("act", Ocat[0:NQ, 1 + br, 4 * g:4 * g + 4, :], pos[g][0:NQ, 0:260].rearrange("p (a b) -> p a b", a=4), [pos[g]], [Ocat])
        rdd = rd_r.get(); cf = cf_r.get(); tmpo = tmpo_r.get(); o = o_r.get()
        P.ts("dve", rdd[0:NQ, :], Ocat[0:NQ, :, :, 64].rearrange("p a b -> p (a b)") if False else Ocat[0:NQ, :, :, 64],
             1e-30, ALU.max, reads=[Ocat], writes=[rdd]) if False else None
        oc_den = Ocat[0:NQ, :, :, 64]
        rdd3 = rdd[0:NQ, :].rearrange("p (a b) -> p a b", a=3)
        P.ts("dve", rdd3, oc_den, 1e-30, ALU.max, reads=[Ocat], writes=[rdd])
        P.op("dve", lambda e: e.reciprocal(out=rdd[0:NQ, :], in_=rdd[0:NQ, :]), [rdd], [rdd])
        P.tt("dve", cf[0:NQ, :], rdd[0:NQ, :], sig[0:NQ, :], ALU.mult, [rdd, sig], [cf])
        P.tt("dve", tmpo[0:NQ, :, :], Ocat[0:NQ, :, :, 0:64].rearrange("p a b c -> p (a b) c"),
             cf[0:NQ, :].unsqueeze(2).to_broadcast([NQ, 24, 64]), ALU.mult, [Ocat, cf], [tmpo])
        o3 = o[0:NQ, :].rearrange("p (a b) -> p a b", a=8)
        P.tt("pool", o3, tmpo[0:NQ, 0:8, :], tmpo[0:NQ, 8:16, :], ALU.add, [tmpo], [o])
        P.tt("pool", o3, o3, tmpo[0:NQ, 16:24, :], ALU.add, [o, tmpo], [o])
        return o

    def ssm(pr, ntok, sstate, mode):
        ps = psg.get()
        for kc in range(4):
            P.tr(ps[:, kc * ntok:(kc + 1) * ntok], pr[0:ntok, kc * 128:(kc + 1) * 128], ident[0:ntok, 0:ntok], [pr, ident], [ps])
        if cfg.get('cut') == 11:
            raise _Stop()
        uTf = uTf_r.get(); uTb = uTb_r.get()
        P.cp("act", uTf[:, :, 0:ntok], ps[:, 0:4 * ntok].rearrange("p (a b) -> p a b", a=4), [ps], [uTf])
        if cfg.get('cut') == 12:
            raise _Stop()
        P.cp("act", uTb[:, :, 0:ntok], ps[:, 0:4 * ntok].rearrange("p (a b) -> p a b", a=4), [ps], [uTb])
        if cfg.get('cut') == 1:
            raise _Stop()
        py = pso.get()
        if mode == "scan":
            def stage1(G):
                pre = psg.get(); pim = psg.get()
                for j in range(4):
                    gp = 4 * G + j
                    P.mm(pre[:, j * 128:(j + 1) * 128], Bw[:, gp, 0, :], uTb[:, G, 0:128], True, True, [Bw, uTb], [pre])
                    P.mm(pim[:, j * 128:(j + 1) * 128], Bw[:, gp, 1, :], uTb[:, G, 0:128], True, True, [Bw, uTb], [pim])
                pre3 = pre[:, 0:512].rearrange("p (a b) -> p a b", a=4); pim3 = pim[:, 0:512].rearrange("p (a b) -> p a b", a=4)
                ct4 = ctab[:, 4 * G:4 * G + 4, :]; st4 = stab2[:, 4 * G:4 * G + 4, 0, :]; mst4 = stab2[:, 4 * G:4 * G + 4, 1, :]
                TA = TA_r.get(); TB = TB_r.get(); W = W4_r.get(); cin = cin_r.get()
                P.tt("dve", TA[:, 0, :, :], pre3, ct4, ALU.mult, [pre, ctab], [TA])
                P.tt("dve", TA[:, 1, :, :], pim3, ct4, ALU.mult, [pim, ctab], [TA])
                P.tt("dve", TB[:, 0, :, :], pim3, st4, ALU.mult, [pim, stab2], [TB])
                P.tt("dve", TB[:, 1, :, :], pre3, mst4, ALU.mult, [pre, stab2], [TB])
                P.tt("pool", W[:], TA[:], TB[:], ALU.add, [TA, TB], [W])
                return W, cin

            def stage2(G, W, cin):
                ct4 = ctab[:, 4 * G:4 * G + 4, :]; st4 = stab2[:, 4 * G:4 * G + 4, 0, :]; mst4 = stab2[:, 4 * G:4 * G + 4, 1, :]
                TA = TA2_r.get(); TB = TB2_r.get(); R = R4_r.get(); SBF = SBF_r.get()
                ss4 = sstate[:, 4 * G:4 * G + 4, :, 0]
                P.tt("pool", cin[:], ss4, rho[:, 4 * G:4 * G + 4].unsqueeze(2).to_broadcast([128, 4, 2]), ALU.mult, [sstate, rho], [cin])
                W0 = W[:, :, :, 0].rearrange("p r j -> p j r")
                P.tt("pool", W0, W0, cin[:], ALU.add, [W, cin], [W])
                for ri in range(2):
                    P.op("dve", lambda e, ri=ri, G=G, W=W, R=R: e.tensor_tensor_scan(
                        out=R[:, ri, :, :].rearrange("p a b -> p (a b)"), data0=rhoB[:, 4 * G:4 * G + 4, :].rearrange("p a b -> p (a b)"),
                        data1=W[:, ri, :, :].rearrange("p a b -> p (a b)"), initial=0.0, op0=ALU.mult, op1=ALU.add), [rhoB, W], [R])
                P.tt("pool", TA[:], R[:], ct4.unsqueeze(1).to_broadcast([128, 2, 4, 128]), ALU.mult, [R, ctab], [TA])
                P.tt("dve", TB[:, 0, :, :], R[:, 1, :, :], mst4, ALU.mult, [R, stab2], [TB])
                P.tt("pool", TB[:, 1, :, :], R[:, 0, :, :], st4, ALU.mult, [R, stab2], [TB])
                P.tt("dve", SBF[:], TA[:], TB[:], ALU.add, [TA, TB], [SBF])
                P.tt("pool", ss4, TA[:, :, :, 127].rearrange("p r j -> p j r"), TB[:, :, :, 127].rearrange("p r j -> p j r"), ALU.add,
                     [TA, TB], [sstate])
                for j in range(4):
                    gp = 4 * G + j
                    for ri in range(2):
                        P.mm(py[:, G * 128:(G + 1) * 128], Cw[:, gp, ri, :], SBF[:, ri, j, :],
                             j == 0 and ri == 0, j == 3 and ri == 1, [Cw, SBF], [py])

            nxt = stage1(0)
            for G in range(4):
                cur = nxt
                if G + 1 < 4:
                    nxt = stage1(G + 1)
                stage2(G, *cur)
        for gp in (range(16) if mode != "scan" else ()):
            pb = psg.get()
            for k3 in range(3):
                P.mm(pb[:, k3 * ntok:(k3 + 1) * ntok], Bw[:, gp, k3, :], uTb[:, gp // 4, 0:ntok], True, True, [Bw, uTb], [pb])
            b3 = pb[:, 0:3 * ntok].rearrange("p (a b) -> p a b", a=3)
            t1 = t1_r.get(); t2 = t2_r.get(); w = w_r.get(); r = r_r.get()
            if mode == "scan":
                ct_b = ctab[:, gp, 0:ntok].unsqueeze(1).to_broadcast([128, 2, ntok]); st_pre = stab2[:, gp, :, 0:ntok]
            else:
                ct_b = ctab[:, gp, 0:1].unsqueeze(1).to_broadcast([128, 2, ntok]); st_pre = stab2[:, gp, :, 0:1].to_broadcast([128, 2, ntok])
            P.tt("dve", t1[:, :, 0:ntok], b3[:, 0:2, :], ct_b, ALU.mult, [pb, ctab], [t1])
            P.tt("dve", t2[:, :, 0:ntok], b3[:, 1:3, :], st_pre, ALU.mult, [pb, stab2], [t2])
            P.tt("pool", w[:, :, 0:ntok], t1[:, :, 0:ntok], t2[:, :, 0:ntok], ALU.add, [t1, t2], [w])
            if cfg.get('cut') == 2 and gp == 0:
                raise _Stop()
            if mode == "scan":
                for ri in range(2):
                    P.op("dve", lambda e, ri=ri, gp=gp, w=w, r=r: e.tensor_tensor_scan(
                        out=r[:, ri, 0:ntok], data0=rhoB[:, gp, 0:ntok], data1=w[:, ri, 0:ntok],
                        initial=sstate[:, gp, ri, 0:1], op0=ALU.mult, op1=ALU.add), [rhoB, w, sstate], [r])
            else:
                P.stt("dve", r[:, :, 0:ntok], sstate[:, gp, :, 0:ntok], rho[:, gp:gp + 1], w[:, :, 0:ntok], ALU.mult, ALU.add,
                      [sstate, rho, w], [r])
            Ap = Apost_r.get(); Bp = Bpost_r.get(); sbf = sbf_r.get()
            if cfg.get('cut') == 3 and gp == 0:
                raise _Stop()
            if mode == "scan":
                s_sin = stab2[:, gp, 0, 0:ntok]; s_msin = stab2[:, gp, 1, 0:ntok]
            else:
                s_sin = stab2[:, gp, 0, 0:1].to_broadcast([128, ntok]); s_msin = stab2[:, gp, 1, 0:1].to_broadcast([128, ntok])
            P.tt("pool", Ap[:, :, 0:ntok], r[:, :, 0:ntok], ct_b, ALU.mult, [r, ctab], [Ap])
            P.tt("dve", Bp[:, 0, 0:ntok], r[:, 1, 0:ntok], s_msin, ALU.mult, [r, stab2], [Bp])
            P.tt("pool", Bp[:, 1, 0:ntok], r[:, 0, 0:ntok], s_sin, ALU.mult, [r, stab2], [Bp])
            P.tt("dve", sbf[:, :, 0:ntok], Ap[:, :, 0:ntok], Bp[:, :, 0:ntok], ALU.add, [Ap, Bp], [sbf])
            if cfg.get('cut') == 4 and gp == 0:
                raise _Stop()
            if mode == "scan":
                P.tt("pool", sstate[:, gp, :, 0:1], Ap[:, :, ntok - 1:ntok], Bp[:, :, ntok - 1:ntok], ALU.add, [Ap, Bp], [sstate])
            else:
                P.tt("pool", sstate[:, gp, :, 0:ntok], Ap[:, :, 0:ntok], Bp[:, :, 0:ntok], ALU.add, [Ap, Bp], [sstate])
            for ri in range(2):
                P.mm(py[:, (gp // 4) * ntok:(gp // 4 + 1) * ntok], Cw[:, gp, ri, :], sbf[:, ri, 0:ntok],
                     gp % 4 == 0 and ri == 0, gp % 4 == 3 and ri == 1, [Cw, sbf], [py])
        if cfg.get('cut') == 5:
            raise _Stop()
        yv = yv_r.get(); z = zz_r.get(); zb = zb_r.get(); g1 = g1_r.get(); g2 = g2_r.get()
        for kc in range(4):
            P.stt("dve", yv[:, kc, 0:ntok], uTf[:, kc, 0:ntok], dT[:, kc:kc + 1], py[:, kc * ntok:(kc + 1) * ntok], ALU.mult, ALU.add,
                  [uTf, dT, py], [yv])
        Y = yv[:, :, 0:ntok]
        P.act(g1[:, :, 0:ntok], Y, AF.Square, reads=[yv], writes=[g1])
        P.ts("dve", g1[:, :, 0:ntok], g1[:, :, 0:ntok], 0.044715, ALU.mult, 1.0, ALU.add, [g1], [g1])
        P.tt("pool", g1[:, :, 0:ntok], g1[:, :, 0:ntok], Y, ALU.mult, [g1, yv], [g1])
        P.act(g2[:, :, 0:ntok], g1[:, :, 0:ntok], AF.Sigmoid, scale=GC, reads=[g1], writes=[g2])
        P.tt("dve", z[:, :, 0:ntok], Y, g2[:, :, 0:ntok], ALU.mult, [yv, g2], [z])
        P.cp("act", zb[:, :, 0:ntok], z[:, :, 0:ntok], [z], [zb])
        if cfg.get('cut') == 6:
            raise _Stop()
        pg = psg.get()
        for oc in range(4):
            for kc in range(4):
                P.mm(pg[:, oc * ntok:(oc + 1) * ntok], w_glu_bf[:, kc, oc * 128:(oc + 1) * 128], zb[:, kc, 0:ntok], kc == 0, kc == 3, [w_glu_bf, zb], [pg])
        P.act(g2[:, :, 0:ntok], pg[:, 0:4 * ntok].rearrange("p (a b) -> p a b", a=4), AF.Sigmoid, reads=[pg], writes=[g2])
        osm = os_r.get()
        P.tt("dve", osm[:, :, 0:ntok], z[:, :, 0:ntok], g2[:, :, 0:ntok], ALU.mult, [z, g2], [osm])
        return osm

    def out_proj(osm, o, xt, ntok, gt1_ap, gt_t):
        sq = sq_r.get(); mixS = mixS_r.get()
        P.act(sq[:, :, 0:ntok], osm[:, :, 0:ntok], AF.Square, reads=[osm], writes=[sq])
        pss = psg.get()
        for kc in range(4):
            P.mm(pss[0:ntok, 0:1], sq[:, kc, 0:ntok], ones[:, 0:1], kc == 0, kc == 3, [sq, ones], [pss])
        s = st4.get()
        P.ts("dve", s[0:ntok, 1:2], pss[0:ntok, 0:1], 1.0 / 512, ALU.mult, EPS, ALU.add, [pss], [s])
        P.act(s[0:ntok, 2:3], s[0:ntok, 1:2], AF.Sqrt, reads=[s], writes=[s])
        P.op("dve", lambda e: e.reciprocal(out=s[0:ntok, 3:4], in_=s[0:ntok, 2:3]), [s], [s])
        for kc in range(4):
            P.ts("dve", mixS[:, kc, 0:ntok], osm[:, kc, 0:ntok], gST[:, kc:kc + 1], ALU.mult, reads=[osm, gST], writes=[mixS])
        acc = acc_r.get()
        for half in range(2):
            ps = psg.get()
            for kc in range(4):
                P.mm(ps[0:ntok, :], mixS[:, kc, 0:ntok], w_out_bf[:, kc, half * 512:(half + 1) * 512], kc == 0, kc == 3, [mixS, w_out_bf], [ps])
            P.ts("dve", acc[0:ntok, half * 512:(half + 1) * 512], ps[0:ntok, :], s[0:ntok, 3:4], ALU.mult, reads=[ps, s], writes=[acc])
        sa = rstd_of(o[0:ntok, :], o, ntok, 512)
        on = on_r.get()
        P.ts("dve", on[0:ntok, :], o[0:ntok, :], sa[0:ntok, 3:4], ALU.mult, reads=[o, sa], writes=[on])
        ps = psg.get()
        for kc in range(4):
            P.tr(ps[:, kc * ntok:(kc + 1) * ntok], on[0:ntok, kc * 128:(kc + 1) * 128], ident[0:ntok, 0:ntok], [on, ident], [ps])
        mixA = mixA_r.get()
        for kc in range(4):
            P.ts("dve", mixA[:, kc, 0:ntok], ps[:, kc * ntok:(kc + 1) * ntok], gAT[:, kc:kc + 1], ALU.mult, reads=[ps, gAT], writes=[mixA])
        x1 = x1_r.get()
        for half in range(2):
            ps = psg.get()
            for kc in range(4):
                P.mm(ps[0:ntok, :], mixA[:, kc, 0:ntok], w_out_bf[:, 4 + kc, half * 512:(half + 1) * 512], kc == 0, kc == 3, [mixA, w_out_bf], [ps])
            sl = slice(half * 512, (half + 1) * 512)
            P.tt("dve", acc[0:ntok, sl], acc[0:ntok, sl], ps[0:ntok, :], ALU.add, [acc, ps], [acc])
            P.tt("pool", acc[0:ntok, sl], acc[0:ntok, sl], gt1_ap[:, sl], ALU.mult, [acc, gt_t], [acc])
            P.tt("pool", x1[0:ntok, sl], acc[0:ntok, sl], xt[0:ntok, sl], ALU.add, [acc, xt], [x1])
        return x1

    def bcast_rows(dst, row, which):
        P.ms("pool", selr[:], 0.0, [selr])
        P.op("pool", lambda e: e.affine_select(out=selr[:], in_=ones[0:NS, :], pattern=[[0, 128]], compare_op=ALU.is_equal,
                                               fill=0.0, base=-row, channel_multiplier=1), [ones], [selr])
        for half in range(2):
            ps = psg.get()
            P.mm(ps[:, :], selr[:], gt_tm[:, which, half * 512:(half + 1) * 512], True, True, [selr, gt_tm], [ps])
            P.cp("dve", dst[:, half * 512:(half + 1) * 512], ps[:, :], [ps], [dst])

    sig_r = ring(2, [24])

    def make_qT(pr, ntok):
        ps = psg.get(); ps2 = psg.get()
        for h in range(8):
            pp = ps if h < 4 else ps2
            P.tr(pp[0:64, (h % 4) * ntok:(h % 4 + 1) * ntok], pr[0:ntok, 512 + h * 64:512 + (h + 1) * 64], ident[0:ntok, 0:ntok], [pr, ident], [pp])
        qT = qT_r.get()
        P.act(qT[:, 0:4, 0:ntok], ps[0:64, 0:4 * ntok].rearrange("p (a b) -> p a b", a=4), AF.Copy, scale=0.125, reads=[ps], writes=[qT])
        P.act(qT[:, 4:8, 0:ntok], ps2[0:64, 0:4 * ntok].rearrange("p (a b) -> p a b", a=4), AF.Copy, scale=0.125, reads=[ps2], writes=[qT])
        sig = sig_r.get()
        P.act(sig[0:ntok, :], pr[0:ntok, 1792:1816], AF.Sigmoid, reads=[pr], writes=[sig])
        return qT, sig

    _k = "ExternalOutput" if cfg.get("dbg") else "Internal"
    o_d = T(nc.dram_tensor("o_scratch", [B * Tq + DB, 512], F32, kind=_k).ap())
    u_d = T(nc.dram_tensor("u_scratch", [B * Tq + DB, 512], F32, kind=_k).ap())
    x1s_d = T(nc.dram_tensor("x1s_scratch", [DB, 1024], F32, kind="Internal").ap())
    mark_state = off[0]
    KT = A([2, 2, NT * 128], BF16, 64); VA = A([2, NT, 2, 65], BF16)
    KcT = A([2, NKTP * 128], BF16, 64); GvT = A([2, NKTP * 128], BF16)
    XcT = A([4, 144], BF16, 64)
    SCp = dict(KT=KT, VA=VA, KcT=KcT, GvT=GvT)

    def getkv_p(br, kt):
        return T(KT[:, br, :, kt * 128:(kt + 1) * 128], KT.buf), T(VA[:, br, kt, :, :], VA.buf)

    for b in range(B):
        P.ms("pool", KcT[:], 0.0, [KcT]); P.ms("pool", GvT[:], 0.0, [GvT]); P.ms("pool", XcT[:], 0.0, [XcT])
        P.ms("pool", VA[:], 1.0, [VA])
        def pre_tile(t, b=b):
            r0 = b * Tq + t * 128
            xt = xt_r.get()
            P.dma(xt[:], x_p.ap[r0:r0 + 128, :], [x_p], [xt])
            hT = norm_mod_T(xt, 128, a1T, sh1T, DB + b)
            pr = proj(hT, 128)
            P.dma(cmp_p.ap[r0:r0 + 128, :], pr[:, 1024:1280], [pr], [cmp_p])
            P.dma(slc_p.ap[r0:r0 + 128, :], pr[:, 1280:1536], [pr], [slc_p])
            P.dma(u_d.ap[r0:r0 + 128, :], pr[:, 0:512], [pr], [u_d])
            if t >= NT - 4:
                w0 = b * 512 + (t - (NT - 4)) * 128
                P.dma(win_p.ap[w0:w0 + 128, :], pr[:, 1536:1792], [pr], [win_p])
            return pr

        pr_next = pre_tile(0)
        for t in range(NT):
            r0 = b * Tq + t * 128
            pr = pr_next
            if t + 1 < NT:
                pr_next = pre_tile(t + 1)
            P.cp("pool", XcT[:, :, 0:16], XcT[:, :, 128:144], [XcT], [XcT])
            ps = psg.get()
            for kvg in range(4):
                P.tr(ps[0:64, kvg * 128:(kvg + 1) * 128], pr[:, 1024 + kvg * 64:1024 + (kvg + 1) * 64], ident[:], [pr, ident], [ps])
            P.cp("dve", XcT[:, :, 16:144], ps[0:64, :].rearrange("p (a b) -> p a b", a=4), [ps], [XcT])
            ps = psg.get()
            for br in range(2):
                for g in range(2):
                    c0 = 1280 + br * 256 + g * 64
                    P.tr(ps[0:64, (br * 2 + g) * 128:(br * 2 + g + 1) * 128], pr[:, c0:c0 + 64], ident[:], [pr, ident], [ps])
            P.cp("act", KT[:, :, :, t * 128:(t + 1) * 128], ps[0:64, :].rearrange("p (a b c) -> p a b c", a=2, b=2), [ps], [KT])
            for br in range(2):
                c0 = 1280 + br * 256 + 128
                P.cp("pool", VA[:, br, t, :, 0:64], pr[:, c0:c0 + 128].rearrange("p (a b) -> p a b", a=2), [pr], [VA])
            if t == 0:
                for _ in compress_blocks(SCp, XcT, 16, 7, 0):
                    pass
            else:
                for _ in compress_blocks(SCp, XcT, 0, 8, 8 * t - 1):
                    pass
            qT, sig = make_qT(pr, 128)
            nkt_c = (8 * t + 7 + 127) // 128
            slc_kts = [(kt, "diag" if kt == t else "sel") for kt in range(t + 1)]
            win_kts = []
            for kt in range(max(0, t - 4), t + 1):
                win_kts.append((kt, "diag" if kt == t else ("anti" if kt == t - 4 else "none")))
            o = attend(SCp, qT, 128, t, 128 * t, nkt_c, slc_kts, win_kts, sig, True, getkv_p)
            P.dma(o_d.ap[r0:r0 + 128, :], o[:], [o], [o_d])
    P.barrier()
    if stop == 'A':
        return finish_prog()
    P.barrier()
    off[0] = mark_state
    GPG = min(16, NPG); NGRP = NPG // GPG
    Hx_r = ring(1, [4, GPG * 8]); Hg_r = ring(1, [4, GPG * 8], BF16); Ht1_r = ring(1, [4, GPG * 8]); Ht2_r = ring(1, [4, GPG * 8])
    KcT_s = A([2, NKTS * 128], BF16, 64); GvT_s = A([2, NKTS * 128], BF16)
    KcT_s2 = A([2, NKTS * 128], BF16, 64); GvT_s2 = A([2, NKTS * 128], BF16)
    XcW = A([4, 16 + GPG * 128], BF16, 64)
    SCs2 = [dict(KcT=KcT_s, GvT=GvT_s), dict(KcT=KcT_s2, GvT=GvT_s2)]
    pg_r = ring(6, [256]); ktile_r = ring(3, [2, 128], BF16, 64); vtile_r = ring(3, [2, 65], BF16)
    knew = A([2, 2, 128], BF16, 64); vnew = A([2, 2, 65], BF16)
    kTn = A([4, DB], BF16, 64)
    vrow = A([2, 2, 128], F32, 1); sigrow = A([2, 24], F32, 1)
    pti = A([DB * NPG], I32); ptf = A([DB * NPG]); idx_all = pti; pcol = A([1])
    P.dma(pti[:], ptab.ap.rearrange("n o -> (n o)").partition_broadcast(128), [ptab], [pti])
    P.op("pool", lambda e: e.iota(pcol[:], pattern=[[0, 1]], base=0, channel_multiplier=1,
                                  allow_small_or_imprecise_dtypes=True), [], [pcol])
    P.cp("dve", ptf[:], pti[:], [pti], [ptf])
    P.ts("dve", ptf[:], ptf[:], 128.0, ALU.mult, pcol[:, 0:1], ALU.add, [ptf, pcol], [ptf])
    P.cp("dve", idx_all[:], ptf[:], [ptf], [idx_all])
    xs = xt_r.get()
    P.dma(xs[0:DB, :], x_s.ap, [x_s], [xs])
    hTs = norm_mod_T(xs, DB, a1T, sh1T, None)
    prs = proj(hTs, DB)
    P.dma(cmp_s.ap, prs[0:DB, 1024:1280], [prs], [cmp_s])
    P.dma(slc_s.ap, prs[0:DB, 1280:1536], [prs], [slc_s])
    P.dma(u_d.ap[B * Tq:B * Tq + DB, :], prs[0:DB, 0:512], [prs], [u_d])
    P.dma(win_s.ap.rearrange("(i r) c -> i r c", r=512)[:, 511, :], prs[0:DB, 1536:1792], [prs], [win_s])
    P.dma(win_s.ap.rearrange("(i r) c -> i r c", r=512)[:, 0:511, :], state_win.ap.rearrange("(i r) c -> i r c", r=512)[:, 1:512, :],
          [state_win], [win_s])
    qTs, sigs = make_qT(prs, DB)
    ps = psg.get()
    for br in range(2):
        for g in range(2):
            c0 = 1280 + br * 256 + g * 64
            P.tr(ps[0:64, (br * 2 + g) * DB:(br * 2 + g + 1) * DB], prs[0:DB, c0:c0 + 64], ident[0:DB, 0:DB], [prs, ident], [ps])
    P.cp("dve", kTn[:], ps[0:64, 0:4 * DB].rearrange("p (a b) -> p a b", a=4), [ps], [kTn])
    for tl in vtile_r.items:
        P.ms("pool", tl[:], 1.0, [tl])
    def compress_seq(i):
        SC = SCs2[i % 2]
        P.ms("pool", SC["KcT"][:], 0.0, [SC["KcT"]]); P.ms("pool", SC["GvT"][:], 0.0, [SC["GvT"]]); P.ms("pool", XcW[:], 0.0, [XcW])
        yield
        for G in range(NGRP):
            if G > 0:
                P.cp("pool", XcW[:, :, 0:16], XcW[:, :, GPG * 128:GPG * 128 + 16], [XcW], [XcW])
            for jp in range(GPG):
                j = G * GPG + jp
                pg = pg_r.get()
                P.dma(pg[:], cache_cmp.ap, [cache_cmp, idx_all], [pg], q="pool", indirect=idx_all[:, i * NPG + j:i * NPG + j + 1])
                ps = psg.get()
                for kvg in range(4):
                    P.tr(ps[0:64, kvg * 128:(kvg + 1) * 128], pg[:, kvg * 64:(kvg + 1) * 64], ident[:], [pg, ident], [ps])
                P.cp("dve" if jp % 2 else "act", XcW[:, :, 16 + jp * 128:16 + (jp + 1) * 128], ps[0:64, :].rearrange("p (a b) -> p a b", a=4), [ps], [XcW])
                yield
            if G == 0:
                yield from compress_blocks(SC, XcW, 16, GPG * 8 - 1, 0)
            else:
                yield from compress_blocks(SC, XcW, 0, GPG * 8, G * GPG * 8 - 1)
            yield

    gen_cur = compress_seq(0)
    for _ in gen_cur:
        pass
    for i in range(DB):
        SCi = SCs2[i % 2]
        gen_next = compress_seq(i + 1) if i + 1 < DB else iter(())
        P.dma(sigrow[0:1, i % 2, :], sigs[i:i + 1, :], [sigs], [sigrow])
        for br in range(2):
            c0 = 1280 + br * 256 + 128
            P.dma(vrow[0:1, i % 2, br, :], prs[i:i + 1, c0:c0 + 128], [prs], [vrow])
        P.ms("pool", knew[:], 0.0, [knew]); P.ms("pool", vnew[:], 0.0, [vnew])
        for br in range(2):
            P.cp("dve", knew[:, br, :, 0], kTn[:, 2 * br:2 * br + 2, i], [kTn], [knew])
            P.cp("dve", vnew[0:1, br, :, 0:64], vrow[0:1, i % 2, br, :].rearrange("p (a b) -> p a b", a=2), [vrow], [vnew])
            P.ms("pool", vnew[0:1, br, :, 64:65], 1.0, [vnew])

        def getkv_s(br, kt, i=i):
            if (br == 0 and kt == NPG) or (br == 1 and kt == 4):
                return T(knew[:, br, :, :], knew.buf), T(vnew[:, br, :, :], vnew.buf)
            pg = pg_r.get()
            if br == 0:
                P.dma(pg[:], cache_slc.ap, [cache_slc, idx_all], [pg], q="pool", indirect=idx_all[:, i * NPG + kt:i * NPG + kt + 1])
            else:
                r0 = i * 512 + kt * 128
                P.dma(pg[:], state_win.ap[r0:r0 + 128, :], [state_win], [pg])
            ps = psg.get()
            for g in range(2):
                P.tr(ps[0:64, g * 128:(g + 1) * 128], pg[:, g * 64:(g + 1) * 64], ident[:], [pg, ident], [ps])
            ktl = ktile_r.get(); vtl = vtile_r.get()
            P.cp("act", ktl[:], ps[0:64, 0:256].rearrange("p (a b) -> p a b", a=2), [ps], [ktl])
            P.cp("dve", vtl[:, :, 0:64], pg[:, 128:256].rearrange("p (a b) -> p a b", a=2), [pg], [vtl])
            if br == 1 and kt == 0:
                P.ms("pool", vtl[0:1, :, :], 0.0, [vtl])
            elif br == 1 and kt == 1:
                P.ms("pool", vtl[0:1, :, 64:65], 1.0, [vtl])
            return ktl, vtl

        qTi = T(qTs[:, :, i:i + 1], qTs.buf)
        sigi = T(sigrow[0:1, i % 2, :], sigrow.buf)
        slc_kts = [(kt, "sel") for kt in range(NPG)] + [(NPG, "none")]
        win_kts = [(kt, "none") for kt in range(5)]
        o = attend(SCi, qTi, 1, 0, PAST, NKTS, slc_kts, win_kts, sigi, False, getkv_s, tick=lambda g_=gen_next: next(g_, None))
        P.dma(o_d.ap[B * Tq + i:B * Tq + i + 1, :], o[0:1, :], [o], [o_d])
        for _ in gen_next:
            pass
    P.barrier()
    P.barrier()
    psg.items = psb[0:6]; psg.i = 0
    off[0] = mark_persist
    w_out_bf = A([8, 1024], BF16); w_glu_bf = A([4, 512], BF16)
    P.dma(w_out_bf[:], w_out.ap.rearrange("(kc p) n -> p kc n", p=128), [w_out], [w_out_bf], q="pool")
    P.dma(w_glu_bf[:], w_glu.ap.rearrange("(kc p) n -> p kc n", p=128), [w_glu], [w_glu_bf], q="pool")
    lreT = A([16]); limT = A([16]); dtT = A([16]); rho = A([16]); th = A([16])
    for vec, dst in ((lam_re, lreT), (lam_im, limT)):
        ps = loadT(vec, 16, dst)
        P.cp("dve", dst[:], ps[:, 0:16], [ps], [dst])
    with nc.allow_non_contiguous_dma(reason="tiny log_dt broadcast"):
        for g2 in range(2):
            P.dma(dtT[g2 * 64:(g2 + 1) * 64, :], log_dt.ap.rearrange("(gp g2) -> g2 gp", g2=2)[g2:g2 + 1, :].partition_broadcast(64)
                  if False else bass.AP(tensor=log_dt.ap.tensor, offset=g2, ap=[[0, 64], [2, 16]]), [log_dt], [dtT], slow=True)
    P.act(dtT[:], dtT[:], AF.Exp, reads=[dtT], writes=[dtT])
    P.tt("dve", rho[:], lreT[:], dtT[:], ALU.mult, [lreT, dtT], [rho])
    P.tt("dve", th[:], limT[:], dtT[:], ALU.mult, [limT, dtT], [th])
    P.act(rho[:], rho[:], AF.Exp, reads=[rho], writes=[rho])
    ctab = A([16, 128]); stab2 = A([16, 2, 128])
    Bw = A([16, 3, 128], BF16); Cw = A([16, 2, 128], BF16)
    mark_tabs = off[0]
    iot = A([128]); ph = A([16, 128]); phf = A([16, 128]); phi = A([16, 128], I32)
    P.op("pool", lambda e: e.iota(iot[:], pattern=[[1, 128]], base=1, channel_multiplier=0,
                                  allow_small_or_imprecise_dtypes=True), [], [iot])

    def sin_table(dst_ap, dst_t, phase_turns):
        for gp in range(16):
            P.ts("dve", ph[:, gp, :], iot[:], th[:, gp:gp + 1], ALU.mult, 1.0 / (2 * math.pi), ALU.mult, [iot, th], [ph])
        P.ts("dve", ph[:], ph[:], phase_turns, ALU.add, reads=[ph], writes=[ph])
        P.cp("dve", phi[:], ph[:], [ph], [phi])
        P.cp("dve", phf[:], phi[:], [phi], [phf])
        P.tt("dve", ph[:], ph[:], phf[:], ALU.subtract, [ph, phf], [ph])
        P.ts("dve", phf[:], ph[:], 0.5, ALU.is_gt, reads=[ph], writes=[phf])
        P.tt("dve", ph[:], ph[:], phf[:], ALU.subtract, [ph, phf], [ph])
        P.ts("dve", phf[:], ph[:], -0.5, ALU.is_lt, reads=[ph], writes=[phf])
        P.tt("dve", ph[:], ph[:], phf[:], ALU.add, [ph, phf], [ph])
        P.act(dst_ap, ph[:], AF.Sin, scale=2 * math.pi, reads=[ph], writes=[dst_t])

    sin_table(ctab[:], ctab, 0.25)
    sin_table(stab2[:, :, 0, :], stab2, 0.0)
    P.ts("dve", stab2[:, :, 1, :], stab2[:, :, 0, :], -1.0, ALU.mult, reads=[stab2], writes=[stab2])
    c1 = A([16]); s1 = A([16]); nre = A([16]); nim = A([16]); l2 = A([16]); kre = A([16]); kim = A([16]); tmpk = A([16])
    P.cp("dve", c1[:], ctab[:, :, 0], [ctab], [c1]); P.cp("dve", s1[:], stab2[:, :, 0, 0], [stab2], [s1])
    P.tt("dve", nre[:], rho[:], c1[:], ALU.mult, [rho, c1], [nre]); P.ts("dve", nre[:], nre[:], -1.0, ALU.add, reads=[nre], writes=[nre])
    P.tt("dve", nim[:], rho[:], s1[:], ALU.mult, [rho, s1], [nim])
    P.tt("dve", l2[:], lreT[:], lreT[:], ALU.mult, [lreT], [l2]); P.tt("dve", tmpk[:], limT[:], limT[:], ALU.mult, [limT], [tmpk])
    P.tt("dve", l2[:], l2[:], tmpk[:], ALU.add, [l2, tmpk], [l2]); P.op("dve", lambda e: e.reciprocal(out=l2[:], in_=l2[:]), [l2], [l2])
    P.tt("dve", kre[:], nre[:], lreT[:], ALU.mult, [nre, lreT], [kre]); P.tt("dve", tmpk[:], nim[:], limT[:], ALU.mult, [nim, limT], [tmpk])
    P.tt("dve", kre[:], kre[:], tmpk[:], ALU.add, [kre, tmpk], [kre]); P.tt("dve", kre[:], kre[:], l2[:], ALU.mult, [kre, l2], [kre])
    P.tt("dve", kim[:], nim[:], lreT[:], ALU.mult, [nim, lreT], [kim]); P.tt("dve", tmpk[:], nre[:], limT[:], ALU.mult, [nre, limT], [tmpk])
    P.tt("dve", kim[:], kim[:], tmpk[:], ALU.subtract, [kim, tmpk], [kim]); P.tt("dve", kim[:], kim[:], l2[:], ALU.mult, [kim, l2], [kim])
    Bre = A([16, 16]); Bim = A([16, 16]); Bbr = A([16, 16]); Bbi = A([16, 16]); tB = A([16, 16])
    for src, dst in ((b_re, Bre), (b_im, Bim)):
        for g2 in range(2):
            P.dma(dst[g2 * 64:(g2 + 1) * 64, :, :],
                  bass.AP(tensor=src.ap.tensor, offset=g2 * 1024, ap=[[16, 64], [2048, 16], [1, 16]]), [src], [dst])
    kre_b = kre[:].unsqueeze(2).to_broadcast([128, 16, 16]); kim_b = kim[:].unsqueeze(2).to_broadcast([128, 16, 16])
    P.tt("dve", Bbr[:], Bre[:], kre_b, ALU.mult, [Bre, kre], [Bbr]); P.tt("dve", tB[:], Bim[:], kim_b, ALU.mult, [Bim, kim], [tB])
    P.tt("dve", Bbr[:], Bbr[:], tB[:], ALU.subtract, [Bbr, tB], [Bbr])
    P.tt("dve", Bbi[:], Bim[:], kre_b, ALU.mult, [Bim, kre], [Bbi]); P.tt("dve", tB[:], Bre[:], kim_b, ALU.mult, [Bre, kim], [tB])
    P.tt("dve", Bbi[:], Bbi[:], tB[:], ALU.add, [Bbi, tB], [Bbi])
    Bpad = A([128]);
    for ri, Bb in enumerate((Bbr, Bbi)):
        for gp in range(16):
            c0 = 32 * (gp % 4)
            P.ms("pool", Bpad[:], 0.0, [Bpad])
            for g2 in range(2):
                P.cp("pool", Bpad[g2 * 64:(g2 + 1) * 64, c0 + 16 * g2:c0 + 16 * g2 + 16], Bb[g2 * 64:(g2 + 1) * 64, gp, :], [Bb], [Bpad])
            ps = psg.get()
            P.tr(ps[:, 0:128], Bpad[:], ident[:], [Bpad, ident], [ps])
            P.cp("dve", Bw[:, gp, ri, :], ps[:, 0:128], [ps], [Bw])
            if ri == 0:
                P.cp("act", Bw[:, gp, 2, :], ps[:, 0:128], [ps], [Bw])
    Cre = A([16, 16]); Cim = A([16, 16])
    with nc.allow_non_contiguous_dma(reason="small C transpose load"):
        for src, dst in ((c_re, Cre), (c_im, Cim)):
            for g2 in range(2):
                for gp in range(16):
                    P.dma(dst[g2 * 64:(g2 + 1) * 64, gp, :],
                          bass.AP(tensor=src.ap.tensor, offset=g2 * 1024 + gp * 2048, ap=[[1, 64], [64, 16]]), [src], [dst], q="pool", slow=True)
    P.ms("pool", Cw[:], 0.0, [Cw])
    for gp in range(16):
        c0 = 32 * (gp % 4)
        for g2 in range(2):
            sl = slice(g2 * 64, (g2 + 1) * 64)
            P.cp("dve", Cw[sl, gp, 0, c0 + 16 * g2:c0 + 16 * g2 + 16], Cre[sl, gp, :], [Cre], [Cw])
            P.ts("dve", Cw[sl, gp, 1, c0 + 16 * g2:c0 + 16 * g2 + 16], Cim[sl, gp, :], -1.0, ALU.mult, reads=[Cim], writes=[Cw])
    P.barrier()
    off[0] = mark_tabs

    uTf_r = ring(1, [4, 128]); uTb_r = ring(2, [4, 128], BF16)
    t1_r = ring(1, [2, 128]); t2_r = ring(1, [2, 128]); w_r = ring(1, [2, 128]); r_r = ring(1, [2, 128])
    Apost_r = ring(1, [2, 128]); Bpost_r = ring(1, [2, 128]); sbf_r = ring(2, [2, 128], BF16)
    TA_r = ring(1, [2, 4, 128]); TB_r = ring(1, [2, 4, 128]); TA2_r = ring(1, [2, 4, 128]); TB2_r = ring(1, [2, 4, 128])
    W4_r = ring(2, [2, 4, 128]); R4_r = ring(1, [2, 4, 128])
    SBF_r = ring(2, [2, 4, 128], BF16); cin_r = ring(2, [4, 2])
    yv_r = ring(1, [4, 128]); zz_r = ring(1, [4, 128]); zb_r = ring(2, [4, 128], BF16); g1_r = ring(1, [4, 128]); g2_r = ring(1, [4, 128])
    os_r = ring(2, [4, 128]); sq_r = g1_r; mixS_r = ring(2, [4, 128], BF16); mixA_r = ring(2, [4, 128], BF16)
    acc_r = ring(1, [1024]); on_r = ring(1, [512]); x1_r = ring(1, [1024])

    rhoB = A([16, 128])
    P.cp("dve", rhoB[:], rho[:].unsqueeze(2).to_broadcast([128, 16, 128]), [rho], [rhoB])
    P.ms("dve", rhoB[:, :, 0:1], 0.0, [rhoB])
    if stop == 'B0':
        return finish_prog()
    xt_r = ring(2, [1024]); junk = on_r.items[0]; st4 = ring(4, [4]); ub_r = ring(2, [512]); ob_r = ring(2, [512])
    sst = A([16, 2, 1]); gtbc = A([1, 1024]); stmp = A([16]); srow = A([128], F32, 16)
    for b in range(B):
        P.ms("pool", sst[:], 0.0, [sst])
        for which in range(1):
            bcast_rows(T(gtbc[:, which, :], gtbc.buf), DB + b, which)
        if stop == 'B1':
            return finish_prog()
        prev = None
        for t in range(NT + 1):
            if t < NT:
                r0 = b * Tq + t * 128
                xt = xt_r.get(); ub = ub_r.get(); ob = ob_r.get()
                P.dma(xt[:], x_p.ap[r0:r0 + 128, :], [x_p], [xt])
                P.dma(ub[:], u_d.ap[r0:r0 + 128, :], [u_d], [ub])
                P.dma(ob[:], o_d.ap[r0:r0 + 128, :], [o_d], [ob])
                osm = ssm(ub, 128, sst, "scan")
            if prev is not None:
                posm, pob, pxt, pr0 = prev
                x1 = out_proj(posm, pob, pxt, 128, gtbc[:, 0, :], gtbc)
                P.dma(x1_d.ap[pr0:pr0 + 128, :], x1[:], [x1], [x1_d])
            prev = (osm, ob, xt, r0) if t < NT else None
        for ri, dst in enumerate((sre_p, sim_p)):
            ps = psg.get()
            P.cp("dve", stmp[:], sst[:, :, ri, 0], [sst], [stmp])
            P.tr(ps[0:16, 0:128], stmp[:], ident[:], [stmp, ident], [ps])
            P.cp("dve", srow[:], ps[0:16, 0:128], [ps], [srow])
            P.dma(dst.ap[b:b + 1, :].rearrange("o (gp q) -> (o gp) q", q=128), srow[:], [srow], [dst])
    if stop == 'B':
        return finish_prog()
    sst_s = A([16, 2, DB]); sin_t = A([2, 2048], F32, DB); sout_t = sin_t
    P.dma(sin_t[:, 0, :], sre_in.ap, [sre_in], [sin_t])
    P.dma(sin_t[:, 1, :], sim_in.ap, [sim_in], [sin_t])
    for ri in range(2):
        for gq in range(4):
            ps = psg.get()
            for j in range(4):
                gp = gq * 4 + j
                P.tr(ps[:, j * DB:(j + 1) * DB], sin_t[:, ri, gp * 128:(gp + 1) * 128], ident[0:DB, 0:DB], [sin_t, ident], [ps])
            P.cp("dve", sst_s[:, gq * 4:gq * 4 + 4, ri, :], ps[:, 0:4 * DB].rearrange("p (a b) -> p a b", a=4), [ps], [sst_s])
    xs = xt_r.get(); ub = ub_r.get(); ob = ob_r.get()
    P.dma(xs[0:DB, :], x_s.ap, [x_s], [xs])
    P.dma(ub[0:DB, :], u_d.ap[B * Tq:B * Tq + DB, :], [u_d], [ub])
    P.dma(ob[0:DB, :], o_d.ap[B * Tq:B * Tq + DB, :], [o_d], [ob])
    osm = ssm(ub, DB, sst_s, "step")
    for ri, dst in enumerate((sre_s, sim_s)):
        for gq in range(4):
            ps = psg.get()
            for j in range(4):
                gp = gq * 4 + j
                P.tr(ps[0:DB, j * 128:(j + 1) * 128], sst_s[:, gp, ri, :], ident[:], [sst_s, ident], [ps])
            P.cp("dve", sout_t[:, ri, gq * 512:(gq + 1) * 512], ps[0:DB, :], [ps], [sout_t])
        P.dma(dst.ap, sout_t[:, ri, :], [sout_t], [dst])
    x1s = out_proj(osm, ob, xs, DB, gt_tm[0:DB, 0, :], gt_tm)
    P.dma(x1s_d.ap, x1s[0:DB, :], [x1s], [x1s_d])
    P.barrier()
    off[0] = mark_persist
    w_up_bf = A([8, 4096], BF16); w_dn_bf = A([32, 1024], BF16)
    for c in range(4):
        P.dma(w_up_bf[:, :, c * 1024:(c + 1) * 1024], w_up.ap[:, c * 1024:(c + 1) * 1024].rearrange("(kc p) n -> p kc n", p=128), [w_up], [w_up_bf], q="pool")
        P.dma(w_dn_bf[:, c * 8:(c + 1) * 8, :], w_down.ap[c * 1024:(c + 1) * 1024, :].rearrange("(kc p) n -> p kc n", p=128), [w_down], [w_dn_bf], q="pool")
    nf_bc = A([1024]); gtbc = A([1024])
    P.dma(nf_bc[:], norm_final.ap.partition_broadcast(128), [norm_final], [nf_bc])
    xt_r = ring(2, [1024]); xn_r = ring(1, [1024]); junk = A([1024]); st4 = ring(4, [4]); hT_r = ring(2, [8, 128], BF16)
    aT_r = ring(2, [32, 128], BF16); rl_r = ring(2, [128]); x2_r = ring(2, [1024])

    def mlp_tile(xt, ntok, seq, gt2_ap, gt_t, dst_ap, dst_t):
        hT = norm_mod_T(xt, ntok, a2T, sh2T, seq)
        aT = aT_r.get()
        for fc in range(32):
            ps = psg.get()
            for kc in range(8):
                P.mm(ps[:, 0:ntok], w_up_bf[:, kc, fc * 128:(fc + 1) * 128], hT[:, kc, 0:ntok], kc == 0, kc == 7, [w_up_bf, hT], [ps])
            rl = rl_r.get()
            P.act(rl[:, 0:ntok], ps[:, 0:ntok], AF.Relu, reads=[ps], writes=[rl])
            P.tt("pool" if fc % 2 else "dve", aT[:, fc, 0:ntok], rl[:, 0:ntok], rl[:, 0:ntok], ALU.mult, [rl], [aT])
        x2 = x2_r.get()
        for half in range(2):
            ps = psg.get()
            sl = slice(half * 512, (half + 1) * 512)
            for fc in range(32):
                P.mm(ps[0:ntok, :], aT[:, fc, 0:ntok], w_dn_bf[:, fc, sl], fc == 0, fc == 31, [aT, w_dn_bf], [ps])
            P.tt("dve", x2[0:ntok, sl], ps[0:ntok, :], gt2_ap[:, sl], ALU.mult, [ps, gt_t], [x2])
            P.tt("pool", x2[0:ntok, sl], x2[0:ntok, sl], xt[0:ntok, sl], ALU.add, [x2, xt], [x2])
        s = rstd_of(x2[0:ntok, :], x2, ntok, 1024)
        P.stt("dve", x2[0:ntok, :], x2[0:ntok, :], s[0:ntok, 3:4], nf_bc[0:ntok, :], ALU.mult, ALU.mult, [x2, s, nf_bc], [x2])
        P.dma(dst_ap, x2[0:ntok, :], [x2], [dst_t])

    for b in range(B):
        bcast_rows(gtbc, DB + b, 1)
        for t in range(NT):
            r0 = b * Tq + t * 128
            xt = xt_r.get()
            P.dma(xt[:], x1_d.ap[r0:r0 + 128, :], [x1_d], [xt])
            mlp_tile(xt, 128, DB + b, gtbc[:], gtbc, y_p.ap[r0:r0 + 128, :], y_p)
    xs = xt_r.get()
    P.dma(xs[0:DB, :], x1s_d.ap, [x1s_d], [xs])
    mlp_tile(xs, DB, None, gt_tm[0:DB, 1, :], gt_tm, y_s.ap, y_s)
    P.barrier()
    deps = {}
    for o_ in outs_all:
        if o_.buf.lw is not None:
            deps[o_.buf.lw[0]] = max(deps.get(o_.buf.lw[0], 0), o_.buf.lw[1])
    P._waits('sp', deps)
    P.emit(st)
    st.close()
    return nc


def _run(inp, cfg, n_cores):
    B, Tq, DB, NPG = cfg["B"], cfg["T"], cfg["DB"], cfg["NPG"]
    f = lambda a: np.ascontiguousarray(np.asarray(a, dtype=np.float32))
    nc = build(cfg)
    shared = {
        "cache_cmp": f(inp["cache_cmp"][0]).reshape(-1, 256), "cache_slc": f(inp["cache_slc"][0]).reshape(-1, 256),
        "w_ada": f(inp["w_ada"][0]), "b_ada": f(inp["b_ada"][0]), "norm_attn": f(inp["norm_attn"][0]), "w_in": f(inp["w_in"][0]),
        "lam_re": f(inp["ssm_lambda_re"][0]).reshape(-1), "lam_im": f(inp["ssm_lambda_im"][0]).reshape(-1),
        "log_dt": f(inp["ssm_log_dt"][0]), "b_re": f(inp["ssm_b_re"][0]).reshape(-1), "b_im": f(inp["ssm_b_im"][0]).reshape(-1),
        "c_re": f(inp["ssm_c_re"][0]).reshape(-1), "c_im": f(inp["ssm_c_im"][0]).reshape(-1), "ssm_d": f(inp["ssm_d"][0]),
        "w_glu": f(inp["ssm_w_glu"][0]), "pe_k": f(inp["cmp_pe_k"][0]).reshape(-1), "w1_k": f(inp["cmp_w1_k"][0]).reshape(2048, 128),
        "w2_k": f(inp["cmp_w2_k"][0]), "pe_v": f(inp["cmp_pe_v"][0]).reshape(-1), "w1_v": f(inp["cmp_w1_v"][0]).reshape(2048, 128),
        "w2_v": f(inp["cmp_w2_v"][0]), "n_ssm": f(inp["norm_out_ssm"][0]), "n_attn": f(inp["norm_out_attn"][0]),
        "w_out": f(inp["w_out"][0]), "norm_mlp": f(inp["norm_mlp"][0]), "w_up": f(inp["w_up"][0]), "w_down": f(inp["w_down"][0]),
        "norm_final": f(inp["norm_final"]),
    }
    used = set(a.memorylocations[0].name for a in nc.allocations if getattr(a, "kind", None) == "ExternalInput")
    maps = []
    for c in range(n_cores):
        m = dict(shared)
        m["x_p"] = f(inp["x_prompt"][c * B:(c + 1) * B]).reshape(B * Tq, 1024)
        m["x_s"] = f(inp["x_sample"][c * DB:(c + 1) * DB]).reshape(DB, 1024)
        m["c_all"] = np.concatenate([f(inp["c_sample"][c * DB:(c + 1) * DB]), f(inp["c_prompt"][c * B:(c + 1) * B])], axis=0)
        m["state_win"] = f(inp["state_win"][0, c * DB:(c + 1) * DB]).reshape(DB * 512, 256)
        m["ssm_re_in"] = f(inp["state_ssm_re"][0, c * DB:(c + 1) * DB]).reshape(DB, 2048)
        m["ssm_im_in"] = f(inp["state_ssm_im"][0, c * DB:(c + 1) * DB]).reshape(DB, 2048)
        m["ptab"] = np.ascontiguousarray(np.asarray(inp["page_table"][c * DB:(c + 1) * DB], dtype=np.int32)).reshape(DB * NPG, 1)
        maps.append({k: v for k, v in m.items() if k in used})
    res = run_bass_kernel_spmd(nc, maps, core_ids=list(range(n_cores))).results
    cat = lambda k: np.concatenate([r[k] for r in res], axis=0)
    NB = n_cores * B
    ND = n_cores * DB
    return (cat("y_p").reshape(NB, Tq, 1024), cat("y_s").reshape(ND, 1, 1024),
            cat("cmp_p").reshape(1, NB, Tq, 2, 2, 64), cat("slc_p").reshape(1, NB, Tq, 2, 2, 64),
            cat("win_p").reshape(1, NB, 512, 2, 2, 64),
            cat("sre_p").reshape(1, NB, 32, 64), cat("sim_p").reshape(1, NB, 32, 64),
            cat("cmp_s").reshape(1, ND, 1, 2, 2, 64), cat("slc_s").reshape(1, ND, 1, 2, 2, 64),
            cat("win_s").reshape(1, ND, 512, 2, 2, 64),
            cat("sre_s").reshape(1, ND, 32, 64), cat("sim_s").reshape(1, ND, 32, 64))


def kernel(**inputs):
    cfg = dict(B=2, T=4096, DB=16, NPG=64, NPHYS=10240, NSEL=16)
    return _run(inputs, cfg, 8)
```

```python
import math
from contextlib import ExitStack
import numpy as np
import concourse.bass as bass
import concourse.mybir as mybir
from concourse.bass_utils import run_bass_kernel_spmd

F32 = mybir.dt.float32
BF16 = mybir.dt.bfloat16
I32 = mybir.dt.int32
AF = mybir.ActivationFunctionType
ALU = mybir.AluOpType

NLANES = 10
INS_LINES = {}
NSW = 4
PE_NOP_AFTER_WAIT = False
BIG = 1.0e4
EPS = 1e-6
GC = 1.5957691216057308


class Buf:
    __slots__ = ("lw", "rd")

    def __init__(self):
        self.lw = None
        self.rd = {}


class T:
    def __init__(self, ap, buf=None):
        self.ap = ap
        self.buf = buf if buf is not None else Buf()

    def __getitem__(self, k):
        return self.ap[k]


def _bufs(xs):
    return [x.buf if isinstance(x, T) else x for x in xs]


class Prog:
    def __init__(self, nc):
        self.nc = nc
        self.units = ["pe", "act", "dve", "pool"] + ["L%d" % i for i in range(NLANES)] + ["G%d" % i for i in range(NSW)]
        self.q = {e: [] for e in ("pe", "act", "dve", "pool", "sp")}
        self.cnt = {u: 0 for u in self.units}
        self.seen = {e: {u: 0 for u in self.units} for e in self.q}
        self.lane_rr = 0
        self.sw_rr = 0
        self.n_ins = 0

    def _deps(self, reads, writes):
        deps = {}
        for b in reads:
            if b.lw is not None and deps.get(b.lw[0], 0) < b.lw[1]:
                deps[b.lw[0]] = b.lw[1]
        for b in writes:
            if b.lw is not None and deps.get(b.lw[0], 0) < b.lw[1]:
                deps[b.lw[0]] = b.lw[1]
            for u, n in b.rd.items():
                if deps.get(u, 0) < n:
                    deps[u] = n
        return deps

    def _waits(self, q, deps, me=None, raw=False):
        for u, n in deps.items():
            if u == me and not raw:
                continue
            if self.seen[q][u] >= n:
                continue
            self.seen[q][u] = n
            self.q[q].append(("w", u, n * 16 if u[0] in "LG" else n))

    def op(self, eng, fn, reads=(), writes=()):
        reads = _bufs(reads)
        writes = _bufs(writes)
        deps = self._deps(reads, writes)
        if eng == "pool":
            raw = True
        elif eng == "pe":
            raw = False
        else:
            raw = any(b.lw is not None and b.lw[0] == eng for b in reads)
        nq0 = len(self.q[eng])
        self._waits(eng, deps, me=eng, raw=raw)
        if eng == "pe" and len(self.q[eng]) > nq0 and PE_NOP_AFTER_WAIT:
            self.q[eng].append(("n",))
        self.cnt[eng] += 1
        n = self.cnt[eng]
        import sys as _s
        fr = _s._getframe(1)
        while fr.f_code.co_name in ('op','mm','tr','act','tt','ts','stt','cp','ms'):
            fr = fr.f_back
        self.q[eng].append(("i", fn, eng, fr.f_lineno))
        self.n_ins += 1
        for b in reads:
            if b.rd.get(eng, 0) < n:
                b.rd[eng] = n
        for b in writes:
            b.lw = (eng, n)
            b.rd = {}

    def dma(self, out, in_, reads=(), writes=(), q="sp", indirect=None, slow=False):
        reads = _bufs(reads)
        writes = _bufs(writes)
        if q == "pool":
            lane = "G%d" % self.sw_rr
            self.sw_rr = (self.sw_rr + 1) % NSW
        else:
            lane = "L%d" % self.lane_rr
            self.lane_rr = (self.lane_rr + 1) % NLANES
        deps = self._deps(reads, writes)
        if self.cnt[lane] > 0:
            deps[lane] = max(deps.get(lane, 0), self.cnt[lane])
        self._waits(q, deps)
        self.cnt[lane] += 1
        n = self.cnt[lane]
        if indirect is not None:
            fn = lambda e: e.indirect_dma_start(out=out, out_offset=None, in_=in_,
                                                in_offset=bass.IndirectOffsetOnAxis(ap=indirect, axis=0))
        else:
            fn = (lambda e: e.dma_start(out=out, in_=in_, allow_slow_non_contiguous=True)) if slow else (lambda e: e.dma_start(out=out, in_=in_))
        import sys as _s
        fr = _s._getframe(1)
        self.q[q].append(("i", fn, lane, fr.f_lineno))
        self.n_ins += 1
        for b in reads:
            if b.rd.get(lane, 0) < n:
                b.rd[lane] = n
        for b in writes:
            b.lw = (lane, n)
            b.rd = {}

    def barrier(self):
        deps = {u: n for u, n in self.cnt.items() if n > 0}
        for q in self.q:
            self._waits(q, dict(deps))

    def _pe_mode(self, mode):
        self.pe_mode = mode

    @staticmethod
    def _r(n):
        return 32 if n <= 32 else (64 if n <= 64 else 128)

    def mm(self, out, lhsT, rhs, start=True, stop=True, reads=(), writes=()):
        self._pe_mode(("mm", self._r(lhsT.shape[0]), self._r(int(np.prod(lhsT.shape[1:]))), str(lhsT.dtype)))
        self.op("pe", lambda e: e.matmul(out, lhsT=lhsT, rhs=rhs, start=start, stop=stop,
                                         skip_group_check=True), reads, writes)

    def tr(self, out, in_, ident, reads=(), writes=()):
        self._pe_mode(("tr", self._r(in_.shape[0]), self._r(int(np.prod(in_.shape[1:])))))
        self.op("pe", lambda e: e.transpose(out, in_, ident), reads, writes)

    def act(self, out, in_, func, scale=1.0, bias=0.0, accum=None, reads=(), writes=()):
        if accum is None:
            self.op("act", lambda e: e.activation(out=out, in_=in_, func=func, scale=scale, bias=bias),
                    reads, writes)
        else:
            self.op("act", lambda e: e.activation(out=out, in_=in_, func=func, scale=scale, bias=bias,
                                                  accum_out=accum), reads, writes)

    def tt(self, eng, out, in0, in1, op, reads=(), writes=()):
        self.op(eng, lambda e: e.tensor_tensor(out=out, in0=in0, in1=in1, op=op), reads, writes)

    def ts(self, eng, out, in0, s1, op0, s2=None, op1=None, reads=(), writes=()):
        if op1 is None:
            self.op(eng, lambda e: e.tensor_scalar(out=out, in0=in0, scalar1=s1, scalar2=None, op0=op0),
                    reads, writes)
        else:
            self.op(eng, lambda e: e.tensor_scalar(out=out, in0=in0, scalar1=s1, scalar2=s2, op0=op0, op1=op1),
                    reads, writes)

    def stt(self, eng, out, in0, scalar, in1, op0, op1, reads=(), writes=()):
        self.op(eng, lambda e: e.scalar_tensor_tensor(out=out, in0=in0, scalar=scalar, in1=in1, op0=op0, op1=op1),
                reads, writes)

    def cp(self, eng, out, in_, reads=(), writes=()):
        if eng == "act":
            self.op("act", lambda e: e.copy(out=out, in_=in_), reads, writes)
        else:
            self.op(eng, lambda e: e.tensor_copy(out=out, in_=in_), reads, writes)

    def ms(self, eng, ap, val, writes=()):
        self.op(eng, lambda e: e.memset(ap, val), (), writes)

    def emit(self, stack):
        nc = self.nc
        sems = {u: stack.enter_context(nc.semaphore("s_" + u)) for u in self.units}
        block = stack.enter_context(nc.Block())

        def run(eng, items):
            for it in items:
                if it[0] == "w":
                    eng.wait_ge(sems[it[1]], it[2])
                elif it[0] == "n":
                    eng.nop(nofuse=True)
                else:
                    bi = it[1](eng)
                    bi.then_inc(sems[it[2]], 16 if it[2][0] in "LG" else 1)
                    if len(it) > 3:
                        try:
                            INS_LINES[str(bi.ins.name)] = it[3]
                        except Exception:
                            pass

        @block.tensor
        def _(e):
            run(e, self.q["pe"])

        @block.scalar
        def _(e):
            run(e, self.q["act"])

        @block.vector
        def _(e):
            run(e, self.q["dve"])

        @block.gpsimd
        def _(e):
            run(e, self.q["pool"])

        @block.sync
        def _(e):
            run(e, self.q["sp"])


class _Stop(Exception):
    pass


class Ring:
    def __init__(self, items):
        self.items = items
        self.i = 0

    def get(self):
        t = self.items[self.i]
        self.i = (self.i + 1) % len(self.items)
        return t


def build(cfg):
    B, Tq, DB, NPG, NPHYS, NSEL = cfg["B"], cfg["T"], cfg["DB"], cfg["NPG"], cfg["NPHYS"], cfg["NSEL"]
    NT = Tq // 128
    PAST = NPG * 128
    NS = DB + B
    NBP = Tq // 64
    NBS = PAST // 64 + 1
    NBSm = NBS - 1
    NCS = PAST // 16 - 1
    assert NBP <= 128 and NBSm <= 128 and PAST >= 512 and Tq >= 512
    nc = bass.Bass("TRN2", target_bir_lowering=False)
    P = Prog(nc)
    global _LASTP
    _LASTP = P

    def din(name, shape, dt=F32):
        return T(nc.dram_tensor(name, shape, dt, kind="ExternalInput").ap())

    def dout(name, shape):
        return T(nc.dram_tensor(name, shape, F32, kind="ExternalOutput").ap())

    x_p = din("x_p", [B * Tq, 1024]); x_s = din("x_s", [DB, 1024]); c_all = din("c_all", [NS, 1024])
    cache_cmp = din("cache_cmp", [NPHYS * 128, 256]); cache_slc = din("cache_slc", [NPHYS * 128, 256])
    state_win = din("state_win", [DB * 512, 256])
    sre_in = din("ssm_re_in", [DB, 2048]); sim_in = din("ssm_im_in", [DB, 2048])
    ptab = din("ptab", [DB * NPG, 1], I32)
    w_ada = din("w_ada", [1024, 6144]); b_ada = din("b_ada", [6144]); norm_attn = din("norm_attn", [1024])
    w_in = din("w_in", [1024, 1816]); lam_re = din("lam_re", [2048]); lam_im = din("lam_im", [2048])
    log_dt = din("log_dt", [32]); b_re = din("b_re", [32 * 64 * 16]); b_im = din("b_im", [32 * 64 * 16])
    c_re = din("c_re", [32 * 16 * 64]); c_im = din("c_im", [32 * 16 * 64]); ssm_d = din("ssm_d", [512])
    w_glu = din("w_glu", [512, 512])
    pe_k = din("pe_k", [2048]); w1_k = din("w1_k", [2048, 128]); w2_k = din("w2_k", [128, 64])
    pe_v = din("pe_v", [2048]); w1_v = din("w1_v", [2048, 128]); w2_v = din("w2_v", [128, 64])
    n_ssm = din("n_ssm", [512]); n_attn = din("n_attn", [512]); w_out = din("w_out", [1024, 1024])
    norm_mlp = din("norm_mlp", [1024]); w_up = din("w_up", [1024, 4096]); w_down = din("w_down", [4096, 1024])
    norm_final = din("norm_final", [1024])

    y_p = dout("y_p", [B * Tq, 1024]); y_s = dout("y_s", [DB, 1024])
    cmp_p = dout("cmp_p", [B * Tq, 256]); slc_p = dout("slc_p", [B * Tq, 256]); win_p = dout("win_p", [B * 512, 256])
    sre_p = dout("sre_p", [B, 2048]); sim_p = dout("sim_p", [B, 2048])
    cmp_s = dout("cmp_s", [DB, 256]); slc_s = dout("slc_s", [DB, 256]); win_s = dout("win_s", [DB * 512, 256])
    sre_s = dout("sre_s", [DB, 2048]); sim_s = dout("sim_s", [DB, 2048])
    x1_d = T(nc.dram_tensor("x1_scratch", [B * Tq, 1024], F32, kind="ExternalOutput" if cfg.get("dbg") else "Internal").ap())
    outs_all = [y_p, y_s, cmp_p, slc_p, win_p, sre_p, sim_p, cmp_s, slc_s, win_s, sre_s, sim_s]

    st = ExitStack()
    ARENA_W = 53000
    arena = st.enter_context(nc.sbuf_tensor("arena", [128, ARENA_W], F32))
    off = [0]

    def A(shape, dt=F32, npart=128):
        n = int(np.prod(shape))
        w = n if dt != BF16 else (n + 1) // 2
        w = (w + 1) // 2 * 2
        assert off[0] + w <= ARENA_W, ("arena overflow", off[0], w)
        a = arena[0:npart, off[0]:off[0] + w]
        off[0] += w
        if dt != F32:
            a = a.bitcast(dt)
        a = a[:, 0:n]
        if len(shape) == 2:
            a = a.rearrange("p (a b) -> p a b", a=shape[0])
        elif len(shape) == 3:
            a = a.rearrange("p (a b c) -> p a b c", a=shape[0], b=shape[1])
        elif len(shape) == 4:
            a = a.rearrange("p (a b c d) -> p a b c d", a=shape[0], b=shape[1], c=shape[2])
        return T(a)

    def ring(n, shape, dt=F32, npart=128):
        return Ring([A(shape, dt, npart) for _ in range(n)])

    psb = [T(st.enter_context(nc.psum_tensor("psb%d" % i, [128, 512], F32))[:]) for i in range(8)]
    psg = Ring(psb[0:2])
    psa = Ring(psb[2:6])
    pso = Ring(psb[6:8])

    def finish_prog():
        P.barrier()
        deps = {}
        for o_ in outs_all:
            if o_.buf.lw is not None:
                deps[o_.buf.lw[0]] = max(deps.get(o_.buf.lw[0], 0), o_.buf.lw[1])
        P._waits('sp', deps)
        P.emit(st)
        st.close()
        return nc

    stop = cfg.get("stop", "")
    ident = A([128]); ones = A([128]); tri = A([128], BF16); anti = A([128], BF16)
    P.ms("pool", ones[:], 1.0, [ones])
    P.ms("pool", ident[:], 1.0, [ident])
    P.op("pool", lambda e: e.affine_select(out=ident[:], in_=ident[:], pattern=[[1, 128]], compare_op=ALU.is_equal,
                                           fill=0.0, base=0, channel_multiplier=-1), [ident], [ident])
    P.op("pool", lambda e: e.affine_select(out=tri[:], in_=ones[:], pattern=[[1, 128]], compare_op=ALU.is_ge,
                                           fill=0.0, base=0, channel_multiplier=-1), [ones], [tri])
    P.op("pool", lambda e: e.affine_select(out=anti[:], in_=ones[:], pattern=[[-1, 128]], compare_op=ALU.is_gt,
                                           fill=0.0, base=0, channel_multiplier=1), [ones], [anti])

    def loadT(vec, k, dst):
        tmp = A([128], F32)
        P.dma(tmp[0:k, :], vec.ap.rearrange("(k p) -> k p", p=128), [vec], [tmp])
        ps = psg.get()
        P.tr(ps[:, 0:k], tmp[0:k, :], ident[0:k, 0:k], [tmp, ident], [ps])
        return ps

    nattnT = A([8]); nmlpT = A([8]); dT = A([4]); gST = A([4]); gAT = A([4])
    for vec, k, dst in ((norm_attn, 8, nattnT), (norm_mlp, 8, nmlpT), (ssm_d, 4, dT), (n_ssm, 4, gST), (n_attn, 4, gAT)):
        ps = loadT(vec, k, dst)
        P.cp("dve", dst[:], ps[:, 0:k], [ps], [dst])
    nf_bc = A([1024])
    P.dma(nf_bc[:], norm_final.ap.partition_broadcast(128), [norm_final], [nf_bc])

    if stop == 'c0':
        return finish_prog()
    a1T = A([8, NS]); sh1T = A([8, NS]); a2T = A([8, NS]); sh2T = A([8, NS]); gt_tm = A([2, 1024], F32, NS); selr = A([128], F32, NS)
    mark_persist = off[0]
    csb = A([1024], F32, NS); csil = A([1024], F32, NS); cT = A([8, NS], BF16)
    P.dma(csb[:], c_all.ap, [c_all], [csb])
    P.act(csil[:], csb[:], AF.Silu, reads=[csb], writes=[csil])
    ps = psg.get()
    for kc in range(8):
        P.tr(ps[:, kc * NS:(kc + 1) * NS], csil[:, kc * 128:(kc + 1) * 128], ident[0:NS, 0:NS], [csil, ident], [ps])
    P.cp("dve", cT[:], ps[:, 0:8 * NS].rearrange("p (a b) -> p a b", a=8), [ps], [cT])
    wblk = ring(2, [8, 512], BF16); bbc = ring(2, [512], F32, NS); modb = ring(2, [512], F32, NS)
    featdst = {0: sh1T, 1: a1T, 3: sh2T, 4: a2T}
    for cb in range(12):
        wb = wblk.get(); bb = bbc.get(); mb = modb.get()
        P.dma(wb[:], w_ada.ap[:, cb * 512:(cb + 1) * 512].rearrange("(kc p) n -> p kc n", p=128), [w_ada], [wb], q="pool")
        P.dma(bb[:], b_ada.ap[cb * 512:(cb + 1) * 512].partition_broadcast(NS), [b_ada], [bb])
        ps = psg.get()
        for kc in range(8):
            P.mm(ps[0:NS, :], cT[:, kc, :], wb[:, kc, :], kc == 0, kc == 7, [cT, wb], [ps])
        which, half = cb // 2, cb % 2
        if which in (2, 5):
            P.tt("dve", gt_tm[:, 0 if which == 2 else 1, half * 512:(half + 1) * 512], ps[0:NS, :], bb[:], ALU.add,
                 [ps, bb], [gt_tm])
        else:
            P.tt("dve", mb[:], ps[0:NS, :], bb[:], ALU.add, [ps, bb], [mb])
            ps2 = psg.get()
            for j in range(4):
                P.tr(ps2[:, j * NS:(j + 1) * NS], mb[:, j * 128:(j + 1) * 128], ident[0:NS, 0:NS], [mb, ident], [ps2])
            dst = featdst[which]
            P.cp("dve", dst[:, half * 4:half * 4 + 4, :], ps2[:, 0:4 * NS].rearrange("p (a b) -> p a b", a=4), [ps2], [dst])
    for kc in range(8):
        P.ts("dve", a1T[:, kc, :], a1T[:, kc, :], 1.0, ALU.add, nattnT[:, kc:kc + 1], ALU.mult, [a1T, nattnT], [a1T])
        P.ts("dve", a2T[:, kc, :], a2T[:, kc, :], 1.0, ALU.add, nmlpT[:, kc:kc + 1], ALU.mult, [a2T, nmlpT], [a2T])
    P.barrier()
    off[0] = mark_persist

    if stop == 'mod':
        return finish_prog()
    w_in_bf = A([8, 1816], BF16)
    W1 = A([2, 32, 128], BF16, 64); W2 = A([2, 64], BF16)
    for kc in range(8):
        P.dma(w_in_bf[:, kc, :], w_in.ap[kc * 128:(kc + 1) * 128, :], [w_in], [w_in_bf], q="pool")
    for kv, (w1, w2) in enumerate(((w1_k, w2_k), (w1_v, w2_v))):
        P.dma(W1[:, kv, :, :], w1.ap.rearrange("(j d) h -> d j h", d=64), [w1], [W1], q="pool")
        P.dma(W2[:, kv, :], w2.ap, [w2], [W2], q="pool")
    peT = A([2, 32], BF16, 64); cbias = A([2])
    with nc.allow_non_contiguous_dma(reason="tiny pe transpose load"):
        for kv, pe in enumerate((pe_k, pe_v)):
            P.dma(peT[:, kv, :], pe.ap.rearrange("(j d) -> d j", d=64), [pe], [peT], q="pool", slow=True)
    ps = psg.get()
    for kv in range(2):
        for j in range(32):
            P.mm(ps[:, kv:kv + 1], W1[:, kv, j, :], peT[:, kv, j:j + 1], j == 0, j == 31, [W1, peT], [ps])
    P.cp("dve", cbias[:], ps[:, 0:2], [ps], [cbias])

    if stop == 'w':
        return finish_prog()
    NKTP = max(1, (Tq // 16 + 127) // 128)
    NKTS = (NCS + 127) // 128
    NKTC = max(NKTP, NKTS)
    NBm = max(NBP, NBSm)
    Mm = A([NKTC, NBm], BF16)
    NKE = max(NT, NPG)
    Eall = A([NKE, 128], BF16)
    Rtab = A([128]); sbias = A([NBS], F32, 1)
    mark_seltmp = off[0]
    mA = A([NKTC, NBm]); mB = A([NKTC, NBm]); etmp = A([NKE, 128]); rel = A([128]); r2 = A([128])
    P.ms("pool", mA[:], 1.0, [mA]); P.ms("pool", mB[:], 1.0, [mB])
    for (tm, lo, hi) in ((mA, -1, 3), (mB, 0, 2)):
        P.op("pool", lambda e, tm=tm, lo=lo: e.affine_select(out=tm[:], in_=tm[:], pattern=[[128, NKTC], [-4, NBm]],
                                                             compare_op=ALU.is_ge, fill=0.0, base=-lo, channel_multiplier=1), [tm], [tm])
        P.op("pool", lambda e, tm=tm, hi=hi: e.affine_select(out=tm[:], in_=tm[:], pattern=[[-128, NKTC], [4, NBm]],
                                                             compare_op=ALU.is_ge, fill=0.0, base=hi, channel_multiplier=-1), [tm], [tm])
    P.tt("dve", Mm[:], mA[:], mB[:], ALU.add, [mA, mB], [Mm])
    P.ms("pool", etmp[:], 1.0, [etmp])
    P.op("pool", lambda e: e.affine_select(out=etmp[:], in_=etmp[:], pattern=[[128, NKE], [1, 128]], compare_op=ALU.is_ge,
                                           fill=0.0, base=0, channel_multiplier=-64), [etmp], [etmp])
    P.op("pool", lambda e: e.affine_select(out=etmp[:], in_=etmp[:], pattern=[[-128, NKE], [-1, 128]], compare_op=ALU.is_ge,
                                           fill=0.0, base=63, channel_multiplier=64), [etmp], [etmp])
    P.cp("dve", Eall[:], etmp[:], [etmp], [Eall])
    P.op("pool", lambda e: e.iota(rel[:], pattern=[[1, 128]], base=-63, channel_multiplier=0,
                                  allow_small_or_imprecise_dtypes=True), [], [rel])
    P.op("pool", lambda e: e.affine_select(out=r2[:], in_=ones[:], pattern=[[0, 128]], compare_op=ALU.is_ge,
                                           fill=0.0, base=-64, channel_multiplier=1), [ones], [r2])
    P.tt("dve", rel[:], rel[:], r2[:], ALU.subtract, [rel, r2], [rel])
    P.ts("dve", Rtab[:], rel[:], -1.0, ALU.is_ge, BIG, ALU.mult, [rel], [Rtab])
    P.ts("dve", r2[:], rel[:], 1.0, ALU.is_ge, -2 * BIG, ALU.mult, [rel], [r2])
    P.tt("dve", Rtab[:], Rtab[:], r2[:], ALU.add, [Rtab, r2], [Rtab])
    P.ms("pool", sbias[:], 0.0, [sbias])
    for j in (0, NBS - 2, NBS - 1):
        P.ms("pool", sbias[:, j:j + 1], BIG, [sbias])
    P.barrier()
    off[0] = mark_seltmp
    mark_mixer = off[0]

    if stop == 'setup':
        return finish_prog()
    xt_r = ring(2, [1024]); xn_r = ring(1, [1024]); junk = A([1024]); st4 = ring(4, [4])
    hT_r = ring(2, [8, 128], BF16); pr_r = ring(2, [1816])

    def rstd_of(src_ap, src_t, ntok, width):
        s = st4.get()
        P.act(junk[0:ntok, 0:width], src_ap, AF.Square, accum=s[0:ntok, 0:1], reads=[src_t], writes=[junk, s])
        P.ts("dve", s[0:ntok, 1:2], s[0:ntok, 0:1], 1.0 / width, ALU.mult, EPS, ALU.add, [s], [s])
        P.act(s[0:ntok, 2:3], s[0:ntok, 1:2], AF.Sqrt, reads=[s], writes=[s])
        P.op("dve", lambda e: e.reciprocal(out=s[0:ntok, 3:4], in_=s[0:ntok, 2:3]), [s], [s])
        return s

    def norm_mod_T(xt, ntok, aT, shT, seq):
        s = rstd_of(xt[0:ntok, :], xt, ntok, 1024)
        xn = xn_r.get()
        P.ts("dve", xn[0:ntok, :], xt[0:ntok, :], s[0:ntok, 3:4], ALU.mult, reads=[xt, s], writes=[xn])
        hT = hT_r.get()
        if cfg.get("dummy", 0) == 3:
            for _ in range(3):
                P.cp("dve", junk[0:ntok, 0:1024], xn[0:ntok, :], [xn], [junk, xn])
        for half in range(2):
            ps = psg.get()
            if half == 0 and cfg.get("dummy", 0) == 1:
                P.tr(ps[:, 0:128], ident[:], ident[:], [ident], [ps])
            for j in range(4):
                kc = half * 4 + j
                P.tr(ps[:, j * ntok:(j + 1) * ntok], xn[0:ntok, kc * 128:(kc + 1) * 128], ident[0:ntok, 0:ntok], [xn, ident], [ps])
            for j in range(4):
                kc = half * 4 + j
                if seq is not None and cfg.get("dummy", 0) == 2:
                    P.ts("dve", hT[:, kc, 0:ntok], ps[:, j * ntok:(j + 1) * ntok], aT[:, kc, seq:seq + 1], ALU.mult,
                         shT[:, kc, seq:seq + 1], ALU.add, [ps, aT, shT], [hT])
                elif seq is not None:
                    P.act(hT[:, kc, 0:ntok], ps[:, j * ntok:(j + 1) * ntok], AF.Identity, scale=aT[:, kc, seq:seq + 1],
                          bias=shT[:, kc, seq:seq + 1], reads=[ps, aT, shT], writes=[hT])
                else:
                    P.tt("dve", hT[:, kc, 0:ntok], ps[:, j * ntok:(j + 1) * ntok], aT[:, kc, 0:ntok], ALU.mult, [ps, aT], [hT])
                    P.tt("dve", hT[:, kc, 0:ntok], hT[:, kc, 0:ntok], shT[:, kc, 0:ntok], ALU.add, [hT, shT], [hT])
        return hT

    def proj(hT, ntok):
        pr = pr_r.get()
        for cb, (c0, c1_) in enumerate(((0, 512), (512, 1024), (1024, 1536), (1536, 1816))):
            ps = psg.get()
            for kc in range(8):
                P.mm(ps[0:ntok, 0:c1_ - c0], hT[:, kc, 0:ntok], w_in_bf[:, kc, c0:c1_], kc == 0, kc == 7, [hT, w_in_bf], [ps])
            P.cp("act" if cb % 2 else "dve", pr[0:ntok, c0:c1_], ps[0:ntok, 0:c1_ - c0], [ps], [pr])
        return pr

    xg_r = ring(2, [4, 8 if True else 0]);

    def gelu_tanh(eng2, out_ap, out_t, x_ap, x_t, shape_tmp):
        t1, t2 = shape_tmp
        P.tt(eng2, t1, x_ap, x_ap, ALU.mult, [x_t], [t1.buf if isinstance(t1, T) else out_t])

    NQM = 128
    qT_r = ring(2, [8, 128], BF16, 64)
    Pt_r = ring(4, [4, 128], BF16)
    mk2_r = ring(2, [128], BF16)
    Ocat_r = ring(1, [3, 8, 65])
    o_r = ring(2, [512])
    imp_r = ring(2, [2, NBm]); sc_r = ring(2, [NBS]); wk_r = ring(2, [NBS]); m8_r = ring(2, [8]); sel_r = ring(2, [NBS])
    selT_r = ring(2, [128], BF16)
    rd_r = ring(2, [24]); cf_r = ring(2, [24]); tmpo_r = ring(1, [24, 64]); sg_r = ring(2, [24])
    VcA = A([NKTC, 2, 65], BF16)
    P.ms("pool", VcA[:], 1.0, [VcA])
    Hx_r = ring(2, [4, 8]); Hg_r = ring(2, [4, 8], BF16)
    Ht1_r = ring(2, [4, 8]); Ht2_r = ring(2, [4, 8])

    def compress(SC, XcT, col0, nb, n0):
        ps = psg.get()
        for kvg in range(4):
            for j in range(32):
                P.mm(ps[:, kvg * nb:(kvg + 1) * nb] if nb <= 128 else None, W1[:, kvg // 2, j, :],
                     XcT[:, kvg, col0 + j:col0 + j + 16 * (nb - 1) + 1:16], j == 0, j == 31, [W1, XcT], [ps]) if nb <= 128 else None
        return ps

    def compress_blocks(SC, XcT, col0, nb, n0):
        hx = Hx_r.get(); hg = Hg_r.get(); t1 = Ht1_r.get(); t2 = Ht2_r.get()
        for kvg in range(4):
            ps = psg.get()
            for j in range(32):
                P.mm(ps[:, 0:nb], W1[:, kvg // 2, j, :],
                     XcT[:, kvg, col0 + j:col0 + j + 16 * (nb - 1) + 1:16], j == 0, j == 31, [W1, XcT], [ps])
            P.act(hx[:, kvg, 0:nb], ps[:, 0:nb], AF.Identity, bias=cbias[:, kvg // 2:kvg // 2 + 1], reads=[ps, cbias], writes=[hx])
            yield
        X = hx[:, :, 0:nb]
        P.tt("pool", t1[:, :, 0:nb], X, X, ALU.mult, [hx], [t1])
        P.ts("dve", t1[:, :, 0:nb], t1[:, :, 0:nb], 0.044715, ALU.mult, 1.0, ALU.add, [t1], [t1])
        P.tt("pool", t1[:, :, 0:nb], t1[:, :, 0:nb], X, ALU.mult, [t1, hx], [t1])
        P.act(t2[:, :, 0:nb], t1[:, :, 0:nb], AF.Sigmoid, scale=GC, reads=[t1], writes=[t2])
        P.tt("dve", hg[:, :, 0:nb], X, t2[:, :, 0:nb], ALU.mult, [hx, t2], [hg])
        yield
        for c in range(0, nb, 256):
            w = min(256, nb - c)
            ps = psg.get()
            for g in range(2):
                P.mm(ps[0:64, g * w:(g + 1) * w], W2[:, 0, :], hg[:, g, c:c + w], True, True, [W2, hg], [ps])
            P.cp("dve", SC["KcT"][:, :, n0 + c:n0 + c + w], ps[0:64, 0:2 * w].rearrange("p (a b) -> p a b", a=2), [ps], [SC["KcT"]])
        P.cp("pool", SC["GvT"][:, :, n0:n0 + nb], hg[:, 2:4, 0:nb], [hg], [SC["GvT"]])

    def attend(SC, qT, NQ, t, qpos0, nkt_c, slc_kts, win_kts, sig, prompt, getkv, tick=None):
        Ocat = Ocat_r.get()
        NB = NBP if prompt else NBS
        NBmm = NBP if prompt else NBSm
        imp = imp_r.get()
        for ktc in range(nkt_c):
            ps = psg.get()
            for g in range(2):
                P.mm(ps[:, g * 64:(g + 1) * 64], SC["GvT"][:, g, ktc * 128:(ktc + 1) * 128], W2[:, 1, :], True, True,
                     [SC["GvT"], W2], [ps])
            P.cp("act", VcA[:, ktc, :, 0:64], ps[:, 0:128].rearrange("p (a b) -> p a b", a=2), [ps], [VcA])
        rd = rd_r.get()
        selTs = []
        for g in range(2):
            po1 = pso.get(); po2 = pso.get()
            for ktc in range(nkt_c):
                ps = psg.get()
                P.mm(ps[:, 0:4 * NQ], SC["KcT"][:, g, ktc * 128:(ktc + 1) * 128], qT[:, 4 * g:4 * g + 4, 0:NQ], True, True,
                     [SC["KcT"], qT], [ps])
                pt = Pt_r.get()
                P.act(pt[:, :, 0:NQ], ps[:, 0:4 * NQ].rearrange("p (a b) -> p a b", a=4), AF.Exp, reads=[ps], writes=[pt])
                base = qpos0 - 2048 * ktc - 31
                if base - 2032 < 0:
                    P.op("pool", lambda e, pt=pt, base=base: e.affine_select(
                        out=pt[:, :, 0:NQ], in_=pt[:, :, 0:NQ], pattern=[[0, 4], [1, NQ]], compare_op=ALU.is_ge, fill=0.0,
                        base=base, channel_multiplier=-16), [pt], [pt])
                for r in range(4):
                    P.mm(po1[0:NQ, r * 65:(r + 1) * 65], pt[:, r, 0:NQ], VcA[:, ktc, g, :], ktc == 0 and r == 0, ktc == nkt_c - 1, [pt, VcA], [po1])
                    P.mm(po2[0:NQ, r * NBmm:(r + 1) * NBmm], pt[:, r, 0:NQ], Mm[:, ktc, 0:NBmm], ktc == 0 and r == 0, ktc == nkt_c - 1, [pt, Mm], [po2])
            P.cp("act", Ocat[0:NQ, 0, 4 * g:4 * g + 4, :], po1[0:NQ, 0:260].rearrange("p (a b) -> p a b", a=4), [po1], [Ocat])
            P.ts("dve", rd[0:NQ, 4 * g:4 * g + 4], Ocat[0:NQ, 0, 4 * g:4 * g + 4, 64], 1e-30, ALU.max, reads=[Ocat], writes=[rd])
            P.op("dve", lambda e, g=g: e.reciprocal(out=rd[0:NQ, 4 * g:4 * g + 4], in_=rd[0:NQ, 4 * g:4 * g + 4]), [rd], [rd])
            for r in range(4):
                if r == 0:
                    P.ts("dve", imp[0:NQ, g, 0:NBmm], po2[0:NQ, 0:NBmm], rd[0:NQ, 4 * g:4 * g + 1], ALU.mult, reads=[po2, rd], writes=[imp])
                else:
                    P.stt("dve", imp[0:NQ, g, 0:NBmm], po2[0:NQ, r * NBmm:(r + 1) * NBmm], rd[0:NQ, 4 * g + r:4 * g + r + 1],
                          imp[0:NQ, g, 0:NBmm], ALU.mult, ALU.add, [po2, rd, imp], [imp])
            sc = sc_r.get(); wk = wk_r.get(); m8 = m8_r.get(); sel = sel_r.get()
            if prompt:
                P.tt("dve", sc[0:NQ, 0:NB], imp[0:NQ, g, 0:NB], Rtab[0:NQ, 63 - 2 * t:63 - 2 * t + NB], ALU.add, [imp, Rtab], [sc])
                P.ts("dve", sc[0:NQ, 0:1], sc[0:NQ, 0:1], BIG, ALU.add, reads=[sc], writes=[sc])
            else:
                P.tt("dve", sc[0:NQ, 0:NBmm], imp[0:NQ, g, 0:NBmm], sbias[0:NQ, 0:NBmm], ALU.add, [imp, sbias], [sc])
                P.cp("dve", sc[0:NQ, NBmm:NB], sbias[0:NQ, NBmm:NB], [sbias], [sc])
            cur = sc
            for rnd in range(NSEL // 8):
                P.op("dve", lambda e, cur=cur, m8=m8: e.max(out=m8[0:NQ, :], in_=cur[0:NQ, 0:NB]), [cur], [m8])
                if rnd < NSEL // 8 - 1:
                    P.op("dve", lambda e, cur=cur, m8=m8, wk=wk: e.match_replace(out=wk[0:NQ, 0:NB], in_to_replace=m8[0:NQ, :],
                                                                   in_values=cur[0:NQ, 0:NB], imm_value=-3.0e38), [cur, m8], [wk])
                    cur = wk
            P.ts("dve", sel[0:NQ, 0:NB], sc[0:NQ, 0:NB], m8[0:NQ, 7:8], ALU.is_ge, reads=[sc, m8], writes=[sel])
            ps = psg.get()
            P.tr(ps[0:NBmm, 0:NQ], sel[0:NQ, 0:NBmm], ident[0:NQ, 0:NQ], [sel, ident], [ps])
            selT = selT_r.get()
            P.cp("dve", selT[0:NBmm, 0:NQ], ps[0:NBmm, 0:NQ], [ps], [selT])
            selTs.append(selT)
        for br, kts in ((0, slc_kts), (1, win_kts)):
            pos = [pso.get(), pso.get()]
            steps = [(i, kt, mode, g) for i, (kt, mode) in enumerate(kts) for g in range(2)]
            kvc = {}

            def front(step, br=br, kvc=kvc):
                i, kt, mode, g = step
                if g == 0:
                    kvc[i] = getkv(br, kt)
                ktT, vaT = kvc[i]
                psm = None
                if mode in ("sel", "diag") and br == 0:
                    psm = psa.get()
                    P.mm(psm[:, 0:NQ], Eall[0:NBmm, kt, :], selTs[g][0:NBmm, 0:NQ], True, True, [Eall, selTs[g]], [psm])
                ps = psa.get()
                P.mm(ps[:, 0:4 * NQ], ktT[:, g, :], qT[:, 4 * g:4 * g + 4, 0:NQ], True, True, [ktT, qT], [ps])
                return ps, psm, vaT

            def back(step, fr, br=br, pos=pos, nsteps=len(kts)):
                i, kt, mode, g = step
                ps, psm, vaT = fr
                pt = Pt_r.get()
                P.act(pt[:, :, 0:NQ], ps[:, 0:4 * NQ].rearrange("p (a b) -> p a b", a=4), AF.Exp, reads=[ps], writes=[pt])
                if br == 0 and mode == "diag":
                    mk2 = mk2_r.get()
                    P.tt("dve", mk2[:, 0:NQ], psm[:, 0:NQ], tri[:, 0:NQ], ALU.mult, [psm, tri], [mk2])
                    P.tt("pool", pt[:, :, 0:NQ], pt[:, :, 0:NQ], mk2[:, 0:NQ].unsqueeze(1).to_broadcast([128, 4, NQ]), ALU.mult, [pt, mk2], [pt])
                elif br == 0 and mode == "sel":
                    P.tt("dve", pt[:, :, 0:NQ], pt[:, :, 0:NQ], psm[:, 0:NQ].unsqueeze(1).to_broadcast([128, 4, NQ]), ALU.mult, [pt, psm], [pt])
                elif br == 1 and mode in ("diag", "anti"):
                    m = tri if mode == "diag" else anti
                    P.tt("pool", pt[:, :, 0:NQ], pt[:, :, 0:NQ], m[:, 0:NQ].unsqueeze(1).to_broadcast([128, 4, NQ]), ALU.mult, [pt, m], [pt])
                for r in range(4):
                    P.mm(pos[g][0:NQ, r * 65:(r + 1) * 65], pt[:, r, 0:NQ], vaT[:, g, :], i == 0 and r == 0, i == nsteps - 1, [pt, vaT], [pos[g]])

            pending = front(steps[0])
            for n, step in enumerate(steps):
                nxt = front(steps[n + 1]) if n + 1 < len(steps) else None
                back(step, pending)
                pending = nxt
                if tick is not None:
                    tick()
            for g in range(2):
                P.cp("act", Ocat[0:NQ, 1 + br, 4 * g:4 * g + 4, :], pos[g][0:NQ, 0:260].rearrange("p (a b) -> p a b", a=4), [pos[g]], [Ocat])
        rdd = rd_r.get(); cf = cf_r.get(); tmpo = tmpo_r.get(); o = o_r.get()
        P.ts("dve", rdd[0:NQ, :], Ocat[0:NQ, :, :, 64].rearrange("p a b -> p (a b)") if False else Ocat[0:NQ, :, :, 64],
             1e-30, ALU.max, reads=[Ocat], writes=[rdd]) if False else None
        oc_den = Ocat[0:NQ, :, :, 64]
        rdd3 = rdd[0:NQ, :].rearrange("p (a b) -> p a b", a=3)
        P.ts("dve", rdd3, oc_den, 1e-30, ALU.max, reads=[Ocat], writes=[rdd])
        P.op("dve", lambda e: e.reciprocal(out=rdd[0:NQ, :], in_=rdd[0:NQ, :]), [rdd], [rdd])
        P.tt("dve", cf[0:NQ, :], rdd[0:NQ, :], sig[0:NQ, :], ALU.mult, [rdd, sig], [cf])
        P.tt("dve", tmpo[0:NQ, :, :], Ocat[0:NQ, :, :, 0:64].rearrange("p a b c -> p (a b) c"),
             cf[0:NQ, :].unsqueeze(2).to_broadcast([NQ, 24, 64]), ALU.mult, [Ocat, cf], [tmpo])
        o3 = o[0:NQ, :].rearrange("p (a b) -> p a b", a=8)
        P.tt("pool", o3, tmpo[0:NQ, 0:8, :], tmpo[0:NQ, 8:16, :], ALU.add, [tmpo], [o])
        P.tt("pool", o3, o3, tmpo[0:NQ, 16:24, :], ALU.add, [o, tmpo], [o])
        return o

    def ssm(pr, ntok, sstate, mode):
        ps = psg.get()
        for kc in range(4):
            P.tr(ps[:, kc * ntok:(kc + 1) * ntok], pr[0:ntok, kc * 128:(kc + 1) * 128], ident[0:ntok, 0:ntok], [pr, ident], [ps])
        if cfg.get('cut') == 11:
            raise _Stop()
        uTf = uTf_r.get(); uTb = uTb_r.get()
        P.cp("act", uTf[:, :, 0:ntok], ps[:, 0:4 * ntok].rearrange("p (a b) -> p a b", a=4), [ps], [uTf])
        if cfg.get('cut') == 12:
            raise _Stop()
        P.cp("act", uTb[:, :, 0:ntok], ps[:, 0:4 * ntok].rearrange("p (a b) -> p a b", a=4), [ps], [uTb])
        if cfg.get('cut') == 1:
            raise _Stop()
        py = pso.get()
        if mode == "scan":
            def stage1(G):
                pre = psg.get(); pim = psg.get()
                for j in range(4):
                    gp = 4 * G + j
                    P.mm(pre[:, j * 128:(j + 1) * 128], Bw[:, gp, 0, :], uTb[:, G, 0:128], True, True, [Bw, uTb], [pre])
                    P.mm(pim[:, j * 128:(j + 1) * 128], Bw[:, gp, 1, :], uTb[:, G, 0:128], True, True, [Bw, uTb], [pim])
                pre3 = pre[:, 0:512].rearrange("p (a b) -> p a b", a=4); pim3 = pim[:, 0:512].rearrange("p (a b) -> p a b", a=4)
                ct4 = ctab[:, 4 * G:4 * G + 4, :]; st4 = stab2[:, 4 * G:4 * G + 4, 0, :]; mst4 = stab2[:, 4 * G:4 * G + 4, 1, :]
                TA = TA_r.get(); TB = TB_r.get(); W = W4_r.get(); cin = cin_r.get()
                P.tt("dve", TA[:, 0, :, :], pre3, ct4, ALU.mult, [pre, ctab], [TA])
                P.tt("dve", TA[:, 1, :, :], pim3, ct4, ALU.mult, [pim, ctab], [TA])
                P.tt("dve", TB[:, 0, :, :], pim3, st4, ALU.mult, [pim, stab2], [TB])
                P.tt("dve", TB[:, 1, :, :], pre3, mst4, ALU.mult, [pre, stab2], [TB])
                P.tt("pool", W[:], TA[:], TB[:], ALU.add, [TA, TB], [W])
                return W, cin

            def stage2(G, W, cin):
                ct4 = ctab[:, 4 * G:4 * G + 4, :]; st4 = stab2[:, 4 * G:4 * G + 4, 0, :]; mst4 = stab2[:, 4 * G:4 * G + 4, 1, :]
                TA = TA2_r.get(); TB = TB2_r.get(); R = R4_r.get(); SBF = SBF_r.get()
                ss4 = sstate[:, 4 * G:4 * G + 4, :, 0]
                P.tt("pool", cin[:], ss4, rho[:, 4 * G:4 * G + 4].unsqueeze(2).to_broadcast([128, 4, 2]), ALU.mult, [sstate, rho], [cin])
                W0 = W[:, :, :, 0].rearrange("p r j -> p j r")
                P.tt("pool", W0, W0, cin[:], ALU.add, [W, cin], [W])
                for ri in range(2):
                    P.op("dve", lambda e, ri=ri, G=G, W=W, R=R: e.tensor_tensor_scan(
                        out=R[:, ri, :, :].rearrange("p a b -> p (a b)"), data0=rhoB[:, 4 * G:4 * G + 4, :].rearrange("p a b -> p (a b)"),
                        data1=W[:, ri, :, :].rearrange("p a b -> p (a b)"), initial=0.0, op0=ALU.mult, op1=ALU.add), [rhoB, W], [R])
                P.tt("pool", TA[:], R[:], ct4.unsqueeze(1).to_broadcast([128, 2, 4, 128]), ALU.mult, [R, ctab], [TA])
                P.tt("dve", TB[:, 0, :, :], R[:, 1, :, :], mst4, ALU.mult, [R, stab2], [TB])
                P.tt("pool", TB[:, 1, :, :], R[:, 0, :, :], st4, ALU.mult, [R, stab2], [TB])
                P.tt("dve", SBF[:], TA[:], TB[:], ALU.add, [TA, TB], [SBF])
                P.tt("pool", ss4, TA[:, :, :, 127].rearrange("p r j -> p j r"), TB[:, :, :, 127].rearrange("p r j -> p j r"), ALU.add,
                     [TA, TB], [sstate])
                for j in range(4):
                    gp = 4 * G + j
                    for ri in range(2):
                        P.mm(py[:, G * 128:(G + 1) * 128], Cw[:, gp, ri, :], SBF[:, ri, j, :],
                             j == 0 and ri == 0, j == 3 and ri == 1, [Cw, SBF], [py])

            nxt = stage1(0)
            for G in range(4):
                cur = nxt
                if G + 1 < 4:
                    nxt = stage1(G + 1)
                stage2(G, *cur)
        for gp in (range(16) if mode != "scan" else ()):
            pb = psg.get()
            for k3 in range(3):
                P.mm(pb[:, k3 * ntok:(k3 + 1) * ntok], Bw[:, gp, k3, :], uTb[:, gp // 4, 0:ntok], True, True, [Bw, uTb], [pb])
            b3 = pb[:, 0:3 * ntok].rearrange("p (a b) -> p a b", a=3)
            t1 = t1_r.get(); t2 = t2_r.get(); w = w_r.get(); r = r_r.get()
            if mode == "scan":
                ct_b = ctab[:, gp, 0:ntok].unsqueeze(1).to_broadcast([128, 2, ntok]); st_pre = stab2[:, gp, :, 0:ntok]
            else:
                ct_b = ctab[:, gp, 0:1].unsqueeze(1).to_broadcast([128, 2, ntok]); st_pre = stab2[:, gp, :, 0:1].to_broadcast([128, 2, ntok])
            P.tt("dve", t1[:, :, 0:ntok], b3[:, 0:2, :], ct_b, ALU.mult, [pb, ctab], [t1])
            P.tt("dve", t2[:, :, 0:ntok], b3[:, 1:3, :], st_pre, ALU.mult, [pb, stab2], [t2])
            P.tt("pool", w[:, :, 0:ntok], t1[:, :, 0:ntok], t2[:, :, 0:ntok], ALU.add, [t1, t2], [w])
            if cfg.get('cut') == 2 and gp == 0:
                raise _Stop()
            if mode == "scan":
                for ri in range(2):
                    P.op("dve", lambda e, ri=ri, gp=gp, w=w, r=r: e.tensor_tensor_scan(
                        out=r[:, ri, 0:ntok], data0=rhoB[:, gp, 0:ntok], data1=w[:, ri, 0:ntok],
                        initial=sstate[:, gp, ri, 0:1], op0=ALU.mult, op1=ALU.add), [rhoB, w, sstate], [r])
            else:
                P.stt("dve", r[:, :, 0:ntok], sstate[:, gp, :, 0:ntok], rho[:, gp:gp + 1], w[:, :, 0:ntok], ALU.mult, ALU.add,
                      [sstate, rho, w], [r])
            Ap = Apost_r.get(); Bp = Bpost_r.get(); sbf = sbf_r.get()
            if cfg.get('cut') == 3 and gp == 0:
                raise _Stop()
            if mode == "scan":
                s_sin = stab2[:, gp, 0, 0:ntok]; s_msin = stab2[:, gp, 1, 0:ntok]
            else:
                s_sin = stab2[:, gp, 0, 0:1].to_broadcast([128, ntok]); s_msin = stab2[:, gp, 1, 0:1].to_broadcast([128, ntok])
            P.tt("pool", Ap[:, :, 0:ntok], r[:, :, 0:ntok], ct_b, ALU.mult, [r, ctab], [Ap])
            P.tt("dve", Bp[:, 0, 0:ntok], r[:, 1, 0:ntok], s_msin, ALU.mult, [r, stab2], [Bp])
            P.tt("pool", Bp[:, 1, 0:ntok], r[:, 0, 0:ntok], s_sin, ALU.mult, [r, stab2], [Bp])
            P.tt("dve", sbf[:, :, 0:ntok], Ap[:, :, 0:ntok], Bp[:, :, 0:ntok], ALU.add, [Ap, Bp], [sbf])
            if cfg.get('cut') == 4 and gp == 0:
                raise _Stop()
            if mode == "scan":
                P.tt("pool", sstate[:, gp, :, 0:1], Ap[:, :, ntok - 1:ntok], Bp[:, :, ntok - 1:ntok], ALU.add, [Ap, Bp], [sstate])
            else:
                P.tt("pool", sstate[:, gp, :, 0:ntok], Ap[:, :, 0:ntok], Bp[:, :, 0:ntok], ALU.add, [Ap, Bp], [sstate])
            for ri in range(2):
                P.mm(py[:, (gp // 4) * ntok:(gp // 4 + 1) * ntok], Cw[:, gp, ri, :], sbf[:, ri, 0:ntok],
                     gp % 4 == 0 and ri == 0, gp % 4 == 3 and ri == 1, [Cw, sbf], [py])
        if cfg.get('cut') == 5:
            raise _Stop()
        yv = yv_r.get(); z = zz_r.get(); zb = zb_r.get(); g1 = g1_r.get(); g2 = g2_r.get()
        for kc in range(4):
            P.stt("dve", yv[:, kc, 0:ntok], uTf[:, kc, 0:ntok], dT[:, kc:kc + 1], py[:, kc * ntok:(kc + 1) * ntok], ALU.mult, ALU.add,
                  [uTf, dT, py], [yv])
        Y = yv[:, :, 0:ntok]
        P.act(g1[:, :, 0:ntok], Y, AF.Square, reads=[yv], writes=[g1])
        P.ts("dve", g1[:, :, 0:ntok], g1[:, :, 0:ntok], 0.044715, ALU.mult, 1.0, ALU.add, [g1], [g1])
        P.tt("pool", g1[:, :, 0:ntok], g1[:, :, 0:ntok], Y, ALU.mult, [g1, yv], [g1])
        P.act(g2[:, :, 0:ntok], g1[:, :, 0:ntok], AF.Sigmoid, scale=GC, reads=[g1], writes=[g2])
        P.tt("dve", z[:, :, 0:ntok], Y, g2[:, :, 0:ntok], ALU.mult, [yv, g2], [z])
        P.cp("act", zb[:, :, 0:ntok], z[:, :, 0:ntok], [z], [zb])
        if cfg.get('cut') == 6:
            raise _Stop()
        pg = psg.get()
        for oc in range(4):
            for kc in range(4):
                P.mm(pg[:, oc * ntok:(oc + 1) * ntok], w_glu_bf[:, kc, oc * 128:(oc + 1) * 128], zb[:, kc, 0:ntok], kc == 0, kc == 3, [w_glu_bf, zb], [pg])
        P.act(g2[:, :, 0:ntok], pg[:, 0:4 * ntok].rearrange("p (a b) -> p a b", a=4), AF.Sigmoid, reads=[pg], writes=[g2])
        osm = os_r.get()
        P.tt("dve", osm[:, :, 0:ntok], z[:, :, 0:ntok], g2[:, :, 0:ntok], ALU.mult, [z, g2], [osm])
        return osm

    def out_proj(osm, o, xt, ntok, gt1_ap, gt_t):
        sq = sq_r.get(); mixS = mixS_r.get()
        P.act(sq[:, :, 0:ntok], osm[:, :, 0:ntok], AF.Square, reads=[osm], writes=[sq])
        pss = psg.get()
        for kc in range(4):
            P.mm(pss[0:ntok, 0:1], sq[:, kc, 0:ntok], ones[:, 0:1], kc == 0, kc == 3, [sq, ones], [pss])
        s = st4.get()
        P.ts("dve", s[0:ntok, 1:2], pss[0:ntok, 0:1], 1.0 / 512, ALU.mult, EPS, ALU.add, [pss], [s])
        P.act(s[0:ntok, 2:3], s[0:ntok, 1:2], AF.Sqrt, reads=[s], writes=[s])
        P.op("dve", lambda e: e.reciprocal(out=s[0:ntok, 3:4], in_=s[0:ntok, 2:3]), [s], [s])
        for kc in range(4):
            P.ts("dve", mixS[:, kc, 0:ntok], osm[:, kc, 0:ntok], gST[:, kc:kc + 1], ALU.mult, reads=[osm, gST], writes=[mixS])
        acc = acc_r.get()
        for half in range(2):
            ps = psg.get()
            for kc in range(4):
                P.mm(ps[0:ntok, :], mixS[:, kc, 0:ntok], w_out_bf[:, kc, half * 512:(half + 1) * 512], kc == 0, kc == 3, [mixS, w_out_bf], [ps])
            P.ts("dve", acc[0:ntok, half * 512:(half + 1) * 512], ps[0:ntok, :], s[0:ntok, 3:4], ALU.mult, reads=[ps, s], writes=[acc])
        sa = rstd_of(o[0:ntok, :], o, ntok, 512)
        on = on_r.get()
        P.ts("dve", on[0:ntok, :], o[0:ntok, :], sa[0:ntok, 3:4], ALU.mult, reads=[o, sa], writes=[on])
        ps = psg.get()
        for kc in range(4):
            P.tr(ps[:, kc * ntok:(kc + 1) * ntok], on[0:ntok, kc * 128:(kc + 1) * 128], ident[0:ntok, 0:ntok], [on, ident], [ps])
        mixA = mixA_r.get()
        for kc in range(4):
            P.ts("dve", mixA[:, kc, 0:ntok], ps[:, kc * ntok:(kc + 1) * ntok], gAT[:, kc:kc + 1], ALU.mult, reads=[ps, gAT], writes=[mixA])
        x1 = x1_r.get()
        for half in range(2):
            ps = psg.get()
            for kc in range(4):
                P.mm(ps[0:ntok, :], mixA[:, kc, 0:ntok], w_out_bf[:, 4 + kc, half * 512:(half + 1) * 512], kc == 0, kc == 3, [mixA, w_out_bf], [ps])
            sl = slice(half * 512, (half + 1) * 512)
            P.tt("dve", acc[0:ntok, sl], acc[0:ntok, sl], ps[0:ntok, :], ALU.add, [acc, ps], [acc])
            P.tt("pool", acc[0:ntok, sl], acc[0:ntok, sl], gt1_ap[:, sl], ALU.mult, [acc, gt_t], [acc])
            P.tt("pool", x1[0:ntok, sl], acc[0:ntok, sl], xt[0:ntok, sl], ALU.add, [acc, xt], [x1])
        return x1

    def bcast_rows(dst, row, which):
        P.ms("pool", selr[:], 0.0, [selr])
        P.op("pool", lambda e: e.affine_select(out=selr[:], in_=ones[0:NS, :], pattern=[[0, 128]], compare_op=ALU.is_equal,
                                               fill=0.0, base=-row, channel_multiplier=1), [ones], [selr])
        for half in range(2):
            ps = psg.get()
            P.mm(ps[:, :], selr[:], gt_tm[:, which, half * 512:(half + 1) * 512], True, True, [selr, gt_tm], [ps])
            P.cp("dve", dst[:, half * 512:(half + 1) * 512], ps[:, :], [ps], [dst])

    sig_r = ring(2, [24])

    def make_qT(pr, ntok):
        ps = psg.get(); ps2 = psg.get()
        for h in range(8):
            pp = ps if h < 4 else ps2
            P.tr(pp[0:64, (h % 4) * ntok:(h % 4 + 1) * ntok], pr[0:ntok, 512 + h * 64:512 + (h + 1) * 64], ident[0:ntok, 0:ntok], [pr, ident], [pp])
        qT = qT_r.get()
        P.act(qT[:, 0:4, 0:ntok], ps[0:64, 0:4 * ntok].rearrange("p (a b) -> p a b", a=4), AF.Copy, scale=0.125, reads=[ps], writes=[qT])
        P.act(qT[:, 4:8, 0:ntok], ps2[0:64, 0:4 * ntok].rearrange("p (a b) -> p a b", a=4), AF.Copy, scale=0.125, reads=[ps2], writes=[qT])
        sig = sig_r.get()
        P.act(sig[0:ntok, :], pr[0:ntok, 1792:1816], AF.Sigmoid, reads=[pr], writes=[sig])
        return qT, sig

    _k = "ExternalOutput" if cfg.get("dbg") else "Internal"
    o_d = T(nc.dram_tensor("o_scratch", [B * Tq + DB, 512], F32, kind=_k).ap())
    u_d = T(nc.dram_tensor("u_scratch", [B * Tq + DB, 512], F32, kind=_k).ap())
    x1s_d = T(nc.dram_tensor("x1s_scratch", [DB, 1024], F32, kind="Internal").ap())
    mark_state = off[0]
    KT = A([2, 2, NT * 128], BF16, 64); VA = A([2, NT, 2, 65], BF16)
    KcT = A([2, NKTP * 128], BF16, 64); GvT = A([2, NKTP * 128], BF16)
    XcT = A([4, 144], BF16, 64)
    SCp = dict(KT=KT, VA=VA, KcT=KcT, GvT=GvT)

    def getkv_p(br, kt):
        return T(KT[:, br, :, kt * 128:(kt + 1) * 128], KT.buf), T(VA[:, br, kt, :, :], VA.buf)

    for b in range(B):
        P.ms("pool", KcT[:], 0.0, [KcT]); P.ms("pool", GvT[:], 0.0, [GvT]); P.ms("pool", XcT[:], 0.0, [XcT])
        P.ms("pool", VA[:], 1.0, [VA])
        def pre_tile(t, b=b):
            r0 = b * Tq + t * 128
            xt = xt_r.get()
            P.dma(xt[:], x_p.ap[r0:r0 + 128, :], [x_p], [xt])
            hT = norm_mod_T(xt, 128, a1T, sh1T, DB + b)
            pr = proj(hT, 128)
            P.dma(cmp_p.ap[r0:r0 + 128, :], pr[:, 1024:1280], [pr], [cmp_p])
            P.dma(slc_p.ap[r0:r0 + 128, :], pr[:, 1280:1536], [pr], [slc_p])
            P.dma(u_d.ap[r0:r0 + 128, :], pr[:, 0:512], [pr], [u_d])
            if t >= NT - 4:
                w0 = b * 512 + (t - (NT - 4)) * 128
                P.dma(win_p.ap[w0:w0 + 128, :], pr[:, 1536:1792], [pr], [win_p])
            return pr

        pr_next = pre_tile(0)
        for t in range(NT):
            r0 = b * Tq + t * 128
            pr = pr_next
            if t + 1 < NT:
                pr_next = pre_tile(t + 1)
            P.cp("pool", XcT[:, :, 0:16], XcT[:, :, 128:144], [XcT], [XcT])
            ps = psg.get()
            for kvg in range(4):
                P.tr(ps[0:64, kvg * 128:(kvg + 1) * 128], pr[:, 1024 + kvg * 64:1024 + (kvg + 1) * 64], ident[:], [pr, ident], [ps])
            P.cp("dve", XcT[:, :, 16:144], ps[0:64, :].rearrange("p (a b) -> p a b", a=4), [ps], [XcT])
            ps = psg.get()
            for br in range(2):
                for g in range(2):
                    c0 = 1280 + br * 256 + g * 64
                    P.tr(ps[0:64, (br * 2 + g) * 128:(br * 2 + g + 1) * 128], pr[:, c0:c0 + 64], ident[:], [pr, ident], [ps])
            P.cp("act", KT[:, :, :, t * 128:(t + 1) * 128], ps[0:64, :].rearrange("p (a b c) -> p a b c", a=2, b=2), [ps], [KT])
            for br in range(2):
                c0 = 1280 + br * 256 + 128
                P.cp("pool", VA[:, br, t, :, 0:64], pr[:, c0:c0 + 128].rearrange("p (a b) -> p a b", a=2), [pr], [VA])
            if t == 0:
                for _ in compress_blocks(SCp, XcT, 16, 7, 0):
                    pass
            else:
                for _ in compress_blocks(SCp, XcT, 0, 8, 8 * t - 1):
                    pass
            qT, sig = make_qT(pr, 128)
            nkt_c = (8 * t + 7 + 127) // 128
            slc_kts = [(kt, "diag" if kt == t else "sel") for kt in range(t + 1)]
            win_kts = []
            for kt in range(max(0, t - 4), t + 1):
                win_kts.append((kt, "diag" if kt == t else ("anti" if kt == t - 4 else "none")))
            o = attend(SCp, qT, 128, t, 128 * t, nkt_c, slc_kts, win_kts, sig, True, getkv_p)
            P.dma(o_d.ap[r0:r0 + 128, :], o[:], [o], [o_d])
    P.barrier()
    if stop == 'A':
        return finish_prog()
    P.barrier()
    off[0] = mark_state
    GPG = min(16, NPG); NGRP = NPG // GPG
    Hx_r = ring(1, [4, GPG * 8]); Hg_r = ring(1, [4, GPG * 8], BF16); Ht1_r = ring(1, [4, GPG * 8]); Ht2_r = ring(1, [4, GPG * 8])
    KcT_s = A([2, NKTS * 128], BF16, 64); GvT_s = A([2, NKTS * 128], BF16)
    KcT_s2 = A([2, NKTS * 128], BF16, 64); GvT_s2 = A([2, NKTS * 128], BF16)
    XcW = A([4, 16 + GPG * 128], BF16, 64)
    SCs2 = [dict(KcT=KcT_s, GvT=GvT_s), dict(KcT=KcT_s2, GvT=GvT_s2)]
    pg_r = ring(6, [256]); ktile_r = ring(3, [2, 128], BF16, 64); vtile_r = ring(3, [2, 65], BF16)
    knew = A([2, 2, 128], BF16, 64); vnew = A([2, 2, 65], BF16)
    kTn = A([4, DB], BF16, 64)
    vrow = A([2, 2, 128], F32, 1); sigrow = A([2, 24], F32, 1)
    pti = A([DB * NPG], I32); ptf = A([DB * NPG]); idx_all = pti; pcol = A([1])
    P.dma(pti[:], ptab.ap.rearrange("n o -> (n o)").partition_broadcast(128), [ptab], [pti])
    P.op("pool", lambda e: e.iota(pcol[:], pattern=[[0, 1]], base=0, channel_multiplier=1,
                                  allow_small_or_imprecise_dtypes=True), [], [pcol])
    P.cp("dve", ptf[:], pti[:], [pti], [ptf])
    P.ts("dve", ptf[:], ptf[:], 128.0, ALU.mult, pcol[:, 0:1], ALU.add, [ptf, pcol], [ptf])
    P.cp("dve", idx_all[:], ptf[:], [ptf], [idx_all])
    xs = xt_r.get()
    P.dma(xs[0:DB, :], x_s.ap, [x_s], [xs])
    hTs = norm_mod_T(xs, DB, a1T, sh1T, None)
    prs = proj(hTs, DB)
    P.dma(cmp_s.ap, prs[0:DB, 1024:1280], [prs], [cmp_s])
    P.dma(slc_s.ap, prs[0:DB, 1280:1536], [prs], [slc_s])
    P.dma(u_d.ap[B * Tq:B * Tq + DB, :], prs[0:DB, 0:512], [prs], [u_d])
    P.dma(win_s.ap.rearrange("(i r) c -> i r c", r=512)[:, 511, :], prs[0:DB, 1536:1792], [prs], [win_s])
    P.dma(win_s.ap.rearrange("(i r) c -> i r c", r=512)[:, 0:511, :], state_win.ap.rearrange("(i r) c -> i r c", r=512)[:, 1:512, :],
          [state_win], [win_s])
    qTs, sigs = make_qT(prs, DB)
    ps = psg.get()
    for br in range(2):
        for g in range(2):
            c0 = 1280 + br * 256 + g * 64
            P.tr(ps[0:64, (br * 2 + g) * DB:(br * 2 + g + 1) * DB], prs[0:DB, c0:c0 + 64], ident[0:DB, 0:DB], [prs, ident], [ps])
    P.cp("dve", kTn[:], ps[0:64, 0:4 * DB].rearrange("p (a b) -> p a b", a=4), [ps], [kTn])
    for tl in vtile_r.items:
        P.ms("pool", tl[:], 1.0, [tl])
    def compress_seq(i):
        SC = SCs2[i % 2]
        P.ms("pool", SC["KcT"][:], 0.0, [SC["KcT"]]); P.ms("pool", SC["GvT"][:], 0.0, [SC["GvT"]]); P.ms("pool", XcW[:], 0.0, [XcW])
        yield
        for G in range(NGRP):
            if G > 0:
                P.cp("pool", XcW[:, :, 0:16], XcW[:, :, GPG * 128:GPG * 128 + 16], [XcW], [XcW])
            for jp in range(GPG):
                j = G * GPG + jp
                pg = pg_r.get()
                P.dma(pg[:], cache_cmp.ap, [cache_cmp, idx_all], [pg], q="pool", indirect=idx_all[:, i * NPG + j:i * NPG + j + 1])
                ps = psg.get()
                for kvg in range(4):
                    P.tr(ps[0:64, kvg * 128:(kvg + 1) * 128], pg[:, kvg * 64:(kvg + 1) * 64], ident[:], [pg, ident], [ps])
                P.cp("dve" if jp % 2 else "act", XcW[:, :, 16 + jp * 128:16 + (jp + 1) * 128], ps[0:64, :].rearrange("p (a b) -> p a b", a=4), [ps], [XcW])
                yield
            if G == 0:
                yield from compress_blocks(SC, XcW, 16, GPG * 8 - 1, 0)
            else:
                yield from compress_blocks(SC, XcW, 0, GPG * 8, G * GPG * 8 - 1)
            yield

    gen_cur = compress_seq(0)
    for _ in gen_cur:
        pass
    for i in range(DB):
        SCi = SCs2[i % 2]
        gen_next = compress_seq(i + 1) if i + 1 < DB else iter(())
        P.dma(sigrow[0:1, i % 2, :], sigs[i:i + 1, :], [sigs], [sigrow])
        for br in range(2):
            c0 = 1280 + br * 256 + 128
            P.dma(vrow[0:1, i % 2, br, :], prs[i:i + 1, c0:c0 + 128], [prs], [vrow])
        P.ms("pool", knew[:], 0.0, [knew]); P.ms("pool", vnew[:], 0.0, [vnew])
        for br in range(2):
            P.cp("dve", knew[:, br, :, 0], kTn[:, 2 * br:2 * br + 2, i], [kTn], [knew])
            P.cp("dve", vnew[0:1, br, :, 0:64], vrow[0:1, i % 2, br, :].rearrange("p (a b) -> p a b", a=2), [vrow], [vnew])
            P.ms("pool", vnew[0:1, br, :, 64:65], 1.0, [vnew])

        def getkv_s(br, kt, i=i):
            if (br == 0 and kt == NPG) or (br == 1 and kt == 4):
                return T(knew[:, br, :, :], knew.buf), T(vnew[:, br, :, :], vnew.buf)
            pg = pg_r.get()
            if br == 0:
                P.dma(pg[:], cache_slc.ap, [cache_slc, idx_all], [pg], q="pool", indirect=idx_all[:, i * NPG + kt:i * NPG + kt + 1])
            else:
                r0 = i * 512 + kt * 128
                P.dma(pg[:], state_win.ap[r0:r0 + 128, :], [state_win], [pg])
            ps = psg.get()
            for g in range(2):
                P.tr(ps[0:64, g * 128:(g + 1) * 128], pg[:, g * 64:(g + 1) * 64], ident[:], [pg, ident], [ps])
            ktl = ktile_r.get(); vtl = vtile_r.get()
            P.cp("act", ktl[:], ps[0:64, 0:256].rearrange("p (a b) -> p a b", a=2), [ps], [ktl])
            P.cp("dve", vtl[:, :, 0:64], pg[:, 128:256].rearrange("p (a b) -> p a b", a=2), [pg], [vtl])
            if br == 1 and kt == 0:
                P.ms("pool", vtl[0:1, :, :], 0.0, [vtl])
            elif br == 1 and kt == 1:
                P.ms("pool", vtl[0:1, :, 64:65], 1.0, [vtl])
            return ktl, vtl

        qTi = T(qTs[:, :, i:i + 1], qTs.buf)
        sigi = T(sigrow[0:1, i % 2, :], sigrow.buf)
        slc_kts = [(kt, "sel") for kt in range(NPG)] + [(NPG, "none")]
        win_kts = [(kt, "none") for kt in range(5)]
        o = attend(SCi, qTi, 1, 0, PAST, NKTS, slc_kts, win_kts, sigi, False, getkv_s, tick=lambda g_=gen_next: next(g_, None))
        P.dma(o_d.ap[B * Tq + i:B * Tq + i + 1, :], o[0:1, :], [o], [o_d])
        for _ in gen_next:
            pass
    P.barrier()
    P.barrier()
    psg.items = psb[0:6]; psg.i = 0
    off[0] = mark_persist
    w_out_bf = A([8, 1024], BF16); w_glu_bf = A([4, 512], BF16)
    P.dma(w_out_bf[:], w_out.ap.rearrange("(kc p) n -> p kc n", p=128), [w_out], [w_out_bf], q="pool")
    P.dma(w_glu_bf[:], w_glu.ap.rearrange("(kc p) n -> p kc n", p=128), [w_glu], [w_glu_bf], q="pool")
    lreT = A([16]); limT = A([16]); dtT = A([16]); rho = A([16]); th = A([16])
    for vec, dst in ((lam_re, lreT), (lam_im, limT)):
        ps = loadT(vec, 16, dst)
        P.cp("dve", dst[:], ps[:, 0:16], [ps], [dst])
    with nc.allow_non_contiguous_dma(reason="tiny log_dt broadcast"):
        for g2 in range(2):
            P.dma(dtT[g2 * 64:(g2 + 1) * 64, :], log_dt.ap.rearrange("(gp g2) -> g2 gp", g2=2)[g2:g2 + 1, :].partition_broadcast(64)
                  if False else bass.AP(tensor=log_dt.ap.tensor, offset=g2, ap=[[0, 64], [2, 16]]), [log_dt], [dtT], slow=True)
    P.act(dtT[:], dtT[:], AF.Exp, reads=[dtT], writes=[dtT])
    P.tt("dve", rho[:], lreT[:], dtT[:], ALU.mult, [lreT, dtT], [rho])
    P.tt("dve", th[:], limT[:], dtT[:], ALU.mult, [limT, dtT], [th])
    P.act(rho[:], rho[:], AF.Exp, reads=[rho], writes=[rho])
    ctab = A([16, 128]); stab2 = A([16, 2, 128])
    Bw = A([16, 3, 128], BF16); Cw = A([16, 2, 128], BF16)
    mark_tabs = off[0]
    iot = A([128]); ph = A([16, 128]); phf = A([16, 128]); phi = A([16, 128], I32)
    P.op("pool", lambda e: e.iota(iot[:], pattern=[[1, 128]], base=1, channel_multiplier=0,
                                  allow_small_or_imprecise_dtypes=True), [], [iot])

    def sin_table(dst_ap, dst_t, phase_turns):
        for gp in range(16):
            P.ts("dve", ph[:, gp, :], iot[:], th[:, gp:gp + 1], ALU.mult, 1.0 / (2 * math.pi), ALU.mult, [iot, th], [ph])
        P.ts("dve", ph[:], ph[:], phase_turns, ALU.add, reads=[ph], writes=[ph])
        P.cp("dve", phi[:], ph[:], [ph], [phi])
        P.cp("dve", phf[:], phi[:], [phi], [phf])
        P.tt("dve", ph[:], ph[:], phf[:], ALU.subtract, [ph, phf], [ph])
        P.ts("dve", phf[:], ph[:], 0.5, ALU.is_gt, reads=[ph], writes=[phf])
        P.tt("dve", ph[:], ph[:], phf[:], ALU.subtract, [ph, phf], [ph])
        P.ts("dve", phf[:], ph[:], -0.5, ALU.is_lt, reads=[ph], writes=[phf])
        P.tt("dve", ph[:], ph[:], phf[:], ALU.add, [ph, phf], [ph])
        P.act(dst_ap, ph[:], AF.Sin, scale=2 * math.pi, reads=[ph], writes=[dst_t])

    sin_table(ctab[:], ctab, 0.25)
    sin_table(stab2[:, :, 0, :], stab2, 0.0)
    P.ts("dve", stab2[:, :, 1, :], stab2[:, :, 0, :], -1.0, ALU.mult, reads=[stab2], writes=[stab2])
    c1 = A([16]); s1 = A([16]); nre = A([16]); nim = A([16]); l2 = A([16]); kre = A([16]); kim = A([16]); tmpk = A([16])
    P.cp("dve", c1[:], ctab[:, :, 0], [ctab], [c1]); P.cp("dve", s1[:], stab2[:, :, 0, 0], [stab2], [s1])
    P.tt("dve", nre[:], rho[:], c1[:], ALU.mult, [rho, c1], [nre]); P.ts("dve", nre[:], nre[:], -1.0, ALU.add, reads=[nre], writes=[nre])
    P.tt("dve", nim[:], rho[:], s1[:], ALU.mult, [rho, s1], [nim])
    P.tt("dve", l2[:], lreT[:], lreT[:], ALU.mult, [lreT], [l2]); P.tt("dve", tmpk[:], limT[:], limT[:], ALU.mult, [limT], [tmpk])
    P.tt("dve", l2[:], l2[:], tmpk[:], ALU.add, [l2, tmpk], [l2]); P.op("dve", lambda e: e.reciprocal(out=l2[:], in_=l2[:]), [l2], [l2])
    P.tt("dve", kre[:], nre[:], lreT[:], ALU.mult, [nre, lreT], [kre]); P.tt("dve", tmpk[:], nim[:], limT[:], ALU.mult, [nim, limT], [tmpk])
    P.tt("dve", kre[:], kre[:], tmpk[:], ALU.add, [kre, tmpk], [kre]); P.tt("dve", kre[:], kre[:], l2[:], ALU.mult, [kre, l2], [kre])
    P.tt("dve", kim[:], nim[:], lreT[:], ALU.mult, [nim, lreT], [kim]); P.tt("dve", tmpk[:], nre[:], limT[:], ALU.mult, [nre, limT], [tmpk])
    P.tt("dve", kim[:], kim[:], tmpk[:], ALU.subtract, [kim, tmpk], [kim]); P.tt("dve", kim[:], kim[:], l2[:], ALU.mult, [kim, l2], [kim])
    Bre = A([16, 16]); Bim = A([16, 16]); Bbr = A([16, 16]); Bbi = A([16, 16]); tB = A([16, 16])
    for src, dst in ((b_re, Bre), (b_im, Bim)):
        for g2 in range(2):
            P.dma(dst[g2 * 64:(g2 + 1) * 64, :, :],
                  bass.AP(tensor=src.ap.tensor, offset=g2 * 1024, ap=[[16, 64], [2048, 16], [1, 16]]), [src], [dst])
    kre_b = kre[:].unsqueeze(2).to_broadcast([128, 16, 16]); kim_b = kim[:].unsqueeze(2).to_broadcast([128, 16, 16])
    P.tt("dve", Bbr[:], Bre[:], kre_b, ALU.mult, [Bre, kre], [Bbr]); P.tt("dve", tB[:], Bim[:], kim_b, ALU.mult, [Bim, kim], [tB])
    P.tt("dve", Bbr[:], Bbr[:], tB[:], ALU.subtract, [Bbr, tB], [Bbr])
    P.tt("dve", Bbi[:], Bim[:], kre_b, ALU.mult, [Bim, kre], [Bbi]); P.tt("dve", tB[:], Bre[:], kim_b, ALU.mult, [Bre, kim], [tB])
    P.tt("dve", Bbi[:], Bbi[:], tB[:], ALU.add, [Bbi, tB], [Bbi])
    Bpad = A([128]);
    for ri, Bb in enumerate((Bbr, Bbi)):
        for gp in range(16):
            c0 = 32 * (gp % 4)
            P.ms("pool", Bpad[:], 0.0, [Bpad])
            for g2 in range(2):
                P.cp("pool", Bpad[g2 * 64:(g2 + 1) * 64, c0 + 16 * g2:c0 + 16 * g2 + 16], Bb[g2 * 64:(g2 + 1) * 64, gp, :], [Bb], [Bpad])
            ps = psg.get()
            P.tr(ps[:, 0:128], Bpad[:], ident[:], [Bpad, ident], [ps])
            P.cp("dve", Bw[:, gp, ri, :], ps[:, 0:128], [ps], [Bw])
            if ri == 0:
                P.cp("act", Bw[:, gp, 2, :], ps[:, 0:128], [ps], [Bw])
    Cre = A([16, 16]); Cim = A([16, 16])
    with nc.allow_non_contiguous_dma(reason="small C transpose load"):
        for src, dst in ((c_re, Cre), (c_im, Cim)):
            for g2 in range(2):
                for gp in range(16):
                    P.dma(dst[g2 * 64:(g2 + 1) * 64, gp, :],
                          bass.AP(tensor=src.ap.tensor, offset=g2 * 1024 + gp * 2048, ap=[[1, 64], [64, 16]]), [src], [dst], q="pool", slow=True)
    P.ms("pool", Cw[:], 0.0, [Cw])
    for gp in range(16):
        c0 = 32 * (gp % 4)
        for g2 in range(2):
            sl = slice(g2 * 64, (g2 + 1) * 64)
            P.cp("dve", Cw[sl, gp, 0, c0 + 16 * g2:c0 + 16 * g2 + 16], Cre[sl, gp, :], [Cre], [Cw])
            P.ts("dve", Cw[sl, gp, 1, c0 + 16 * g2:c0 + 16 * g2 + 16], Cim[sl, gp, :], -1.0, ALU.mult, reads=[Cim], writes=[Cw])
    P.barrier()
    off[0] = mark_tabs

    uTf_r = ring(1, [4, 128]); uTb_r = ring(2, [4, 128], BF16)
    t1_r = ring(1, [2, 128]); t2_r = ring(1, [2, 128]); w_r = ring(1, [2, 128]); r_r = ring(1, [2, 128])
    Apost_r = ring(1, [2, 128]); Bpost_r = ring(1, [2, 128]); sbf_r = ring(2, [2, 128], BF16)
    TA_r = ring(1, [2, 4, 128]); TB_r = ring(1, [2, 4, 128]); TA2_r = ring(1, [2, 4, 128]); TB2_r = ring(1, [2, 4, 128])
    W4_r = ring(2, [2, 4, 128]); R4_r = ring(1, [2, 4, 128])
    SBF_r = ring(2, [2, 4, 128], BF16); cin_r = ring(2, [4, 2])
    yv_r = ring(1, [4, 128]); zz_r = ring(1, [4, 128]); zb_r = ring(2, [4, 128], BF16); g1_r = ring(1, [4, 128]); g2_r = ring(1, [4, 128])
    os_r = ring(2, [4, 128]); sq_r = g1_r; mixS_r = ring(2, [4, 128], BF16); mixA_r = ring(2, [4, 128], BF16)
    acc_r = ring(1, [1024]); on_r = ring(1, [512]); x1_r = ring(1, [1024])

    rhoB = A([16, 128])
    P.cp("dve", rhoB[:], rho[:].unsqueeze(2).to_broadcast([128, 16, 128]), [rho], [rhoB])
    P.ms("dve", rhoB[:, :, 0:1], 0.0, [rhoB])
    if stop == 'B0':
        return finish_prog()
    xt_r = ring(2, [1024]); junk = on_r.items[0]; st4 = ring(4, [4]); ub_r = ring(2, [512]); ob_r = ring(2, [512])
    sst = A([16, 2, 1]); gtbc = A([1, 1024]); stmp = A([16]); srow = A([128], F32, 16)
    for b in range(B):
        P.ms("pool", sst[:], 0.0, [sst])
        for which in range(1):
            bcast_rows(T(gtbc[:, which, :], gtbc.buf), DB + b, which)
        if stop == 'B1':
            return finish_prog()
        prev = None
        for t in range(NT + 1):
            if t < NT:
                r0 = b * Tq + t * 128
                xt = xt_r.get(); ub = ub_r.get(); ob = ob_r.get()
                P.dma(xt[:], x_p.ap[r0:r0 + 128, :], [x_p], [xt])
                P.dma(ub[:], u_d.ap[r0:r0 + 128, :], [u_d], [ub])
                P.dma(ob[:], o_d.ap[r0:r0 + 128, :], [o_d], [ob])
                osm = ssm(ub, 128, sst, "scan")
            if prev is not None:
                posm, pob, pxt, pr0 = prev
                x1 = out_proj(posm, pob, pxt, 128, gtbc[:, 0, :], gtbc)
                P.dma(x1_d.ap[pr0:pr0 + 128, :], x1[:], [x1], [x1_d])
            prev = (osm, ob, xt, r0) if t < NT else None
        for ri, dst in enumerate((sre_p, sim_p)):
            ps = psg.get()
            P.cp("dve", stmp[:], sst[:, :, ri, 0], [sst], [stmp])
            P.tr(ps[0:16, 0:128], stmp[:], ident[:], [stmp, ident], [ps])
            P.cp("dve", srow[:], ps[0:16, 0:128], [ps], [srow])
            P.dma(dst.ap[b:b + 1, :].rearrange("o (gp q) -> (o gp) q", q=128), srow[:], [srow], [dst])
    if stop == 'B':
        return finish_prog()
    sst_s = A([16, 2, DB]); sin_t = A([2, 2048], F32, DB); sout_t = sin_t
    P.dma(sin_t[:, 0, :], sre_in.ap, [sre_in], [sin_t])
    P.dma(sin_t[:, 1, :], sim_in.ap, [sim_in], [sin_t])
    for ri in range(2):
        for gq in range(4):
            ps = psg.get()
            for j in range(4):
                gp = gq * 4 + j
                P.tr(ps[:, j * DB:(j + 1) * DB], sin_t[:, ri, gp * 128:(gp + 1) * 128], ident[0:DB, 0:DB], [sin_t, ident], [ps])
            P.cp("dve", sst_s[:, gq * 4:gq * 4 + 4, ri, :], ps[:, 0:4 * DB].rearrange("p (a b) -> p a b", a=4), [ps], [sst_s])
    xs = xt_r.get(); ub = ub_r.get(); ob = ob_r.get()
    P.dma(xs[0:DB, :], x_s.ap, [x_s], [xs])
    P.dma(ub[0:DB, :], u_d.ap[B * Tq:B * Tq + DB, :], [u_d], [ub])
    P.dma(ob[0:DB, :], o_d.ap[B * Tq:B * Tq + DB, :], [o_d], [ob])
    osm = ssm(ub, DB, sst_s, "step")
    for ri, dst in enumerate((sre_s, sim_s)):
        for gq in range(4):
            ps = psg.get()
            for j in range(4):
                gp = gq * 4 + j
                P.tr(ps[0:DB, j * 128:(j + 1) * 128], sst_s[:, gp, ri, :], ident[:], [sst_s, ident], [ps])
            P.cp("dve", sout_t[:, ri, gq * 512:(gq + 1) * 512], ps[0:DB, :], [ps], [sout_t])
        P.dma(dst.ap, sout_t[:, ri, :], [sout_t], [dst])
    x1s = out_proj(osm, ob, xs, DB, gt_tm[0:DB, 0, :], gt_tm)
    P.dma(x1s_d.ap, x1s[0:DB, :], [x1s], [x1s_d])
    P.barrier()
    off[0] = mark_persist
    w_up_bf = A([8, 4096], BF16); w_dn_bf = A([32, 1024], BF16)
    for c in range(4):
        P.dma(w_up_bf[:, :, c * 1024:(c + 1) * 1024], w_up.ap[:, c * 1024:(c + 1) * 1024].rearrange("(kc p) n -> p kc n", p=128), [w_up], [w_up_bf], q="pool")
        P.dma(w_dn_bf[:, c * 8:(c + 1) * 8, :], w_down.ap[c * 1024:(c + 1) * 1024, :].rearrange("(kc p) n -> p kc n", p=128), [w_down], [w_dn_bf], q="pool")
    nf_bc = A([1024]); gtbc = A([1024])
    P.dma(nf_bc[:], norm_final.ap.partition_broadcast(128), [norm_final], [nf_bc])
    xt_r = ring(2, [1024]); xn_r = ring(1, [1024]); junk = A([1024]); st4 = ring(4, [4]); hT_r = ring(2, [8, 128], BF16)
    aT_r = ring(2, [32, 128], BF16); rl_r = ring(2, [128]); x2_r = ring(2, [1024])

    def mlp_tile(xt, ntok, seq, gt2_ap, gt_t, dst_ap, dst_t):
        hT = norm_mod_T(xt, ntok, a2T, sh2T, seq)
        aT = aT_r.get()
        for fc in range(32):
            ps = psg.get()
            for kc in range(8):
                P.mm(ps[:, 0:ntok], w_up_bf[:, kc, fc * 128:(fc + 1) * 128], hT[:, kc, 0:ntok], kc == 0, kc == 7, [w_up_bf, hT], [ps])
            rl = rl_r.get()
            P.act(rl[:, 0:ntok], ps[:, 0:ntok], AF.Relu, reads=[ps], writes=[rl])
            P.tt("pool" if fc % 2 else "dve", aT[:, fc, 0:ntok], rl[:, 0:ntok], rl[:, 0:ntok], ALU.mult, [rl], [aT])
        x2 = x2_r.get()
        for half in range(2):
            ps = psg.get()
            sl = slice(half * 512, (half + 1) * 512)
            for fc in range(32):
                P.mm(ps[0:ntok, :], aT[:, fc, 0:ntok], w_dn_bf[:, fc, sl], fc == 0, fc == 31, [aT, w_dn_bf], [ps])
            P.tt("dve", x2[0:ntok, sl], ps[0:ntok, :], gt2_ap[:, sl], ALU.mult, [ps, gt_t], [x2])
            P.tt("pool", x2[0:ntok, sl], x2[0:ntok, sl], xt[0:ntok, sl], ALU.add, [x2, xt], [x2])
        s = rstd_of(x2[0:ntok, :], x2, ntok, 1024)
        P.stt("dve", x2[0:ntok, :], x2[0:ntok, :], s[0:ntok, 3:4], nf_bc[0:ntok, :], ALU.mult, ALU.mult, [x2, s, nf_bc], [x2])
        P.dma(dst_ap, x2[0:ntok, :], [x2], [dst_t])

    for b in range(B):
        bcast_rows(gtbc, DB + b, 1)
        for t in range(NT):
            r0 = b * Tq + t * 128
            xt = xt_r.get()
            P.dma(xt[:], x1_d.ap[r0:r0 + 128, :], [x1_d], [xt])
            mlp_tile(xt, 128, DB + b, gtbc[:], gtbc, y_p.ap[r0:r0 + 128, :], y_p)
    xs = xt_r.get()
    P.dma(xs[0:DB, :], x1s_d.ap, [x1s_d], [xs])
    mlp_tile(xs, DB, None, gt_tm[0:DB, 1, :], gt_tm, y_s.ap, y_s)
    P.barrier()
    deps = {}
    for o_ in outs_all:
        if o_.buf.lw is not None:
            deps[o_.buf.lw[0]] = max(deps.get(o_.buf.lw[0], 0), o_.buf.lw[1])
    P._waits('sp', deps)
    P.emit(st)
    st.close()
    return nc


def _run(inp, cfg, n_cores):
    B, Tq, DB, NPG = cfg["B"], cfg["T"], cfg["DB"], cfg["NPG"]
    f = lambda a: np.ascontiguousarray(np.asarray(a, dtype=np.float32))
    nc = build(cfg)
    shared = {
        "cache_cmp": f(inp["cache_cmp"][0]).reshape(-1, 256), "cache_slc": f(inp["cache_slc"][0]).reshape(-1, 256),
        "w_ada": f(inp["w_ada"][0]), "b_ada": f(inp["b_ada"][0]), "norm_attn": f(inp["norm_attn"][0]), "w_in": f(inp["w_in"][0]),
        "lam_re": f(inp["ssm_lambda_re"][0]).reshape(-1), "lam_im": f(inp["ssm_lambda_im"][0]).reshape(-1),
        "log_dt": f(inp["ssm_log_dt"][0]), "b_re": f(inp["ssm_b_re"][0]).reshape(-1), "b_im": f(inp["ssm_b_im"][0]).reshape(-1),
        "c_re": f(inp["ssm_c_re"][0]).reshape(-1), "c_im": f(inp["ssm_c_im"][0]).reshape(-1), "ssm_d": f(inp["ssm_d"][0]),
        "w_glu": f(inp["ssm_w_glu"][0]), "pe_k": f(inp["cmp_pe_k"][0]).reshape(-1), "w1_k": f(inp["cmp_w1_k"][0]).reshape(2048, 128),
        "w2_k": f(inp["cmp_w2_k"][0]), "pe_v": f(inp["cmp_pe_v"][0]).reshape(-1), "w1_v": f(inp["cmp_w1_v"][0]).reshape(2048, 128),
        "w2_v": f(inp["cmp_w2_v"][0]), "n_ssm": f(inp["norm_out_ssm"][0]), "n_attn": f(inp["norm_out_attn"][0]),
        "w_out": f(inp["w_out"][0]), "norm_mlp": f(inp["norm_mlp"][0]), "w_up": f(inp["w_up"][0]), "w_down": f(inp["w_down"][0]),
        "norm_final": f(inp["norm_final"]),
    }
    used = set(a.memorylocations[0].name for a in nc.allocations if getattr(a, "kind", None) == "ExternalInput")
    maps = []
    for c in range(n_cores):
        m = dict(shared)
        m["x_p"] = f(inp["x_prompt"][c * B:(c + 1) * B]).reshape(B * Tq, 1024)
        m["x_s"] = f(inp["x_sample"][c * DB:(c + 1) * DB]).reshape(DB, 1024)
        m["c_all"] = np.concatenate([f(inp["c_sample"][c * DB:(c + 1) * DB]), f(inp["c_prompt"][c * B:(c + 1) * B])], axis=0)
        m["state_win"] = f(inp["state_win"][0, c * DB:(c + 1) * DB]).reshape(DB * 512, 256)
        m["ssm_re_in"] = f(inp["state_ssm_re"][0, c * DB:(c + 1) * DB]).reshape(DB, 2048)
        m["ssm_im_in"] = f(inp["state_ssm_im"][0, c * DB:(c + 1) * DB]).reshape(DB, 2048)
        m["ptab"] = np.ascontiguousarray(np.asarray(inp["page_table"][c * DB:(c + 1) * DB], dtype=np.int32)).reshape(DB * NPG, 1)
        maps.append({k: v for k, v in m.items() if k in used})
    res = run_bass_kernel_spmd(nc, maps, core_ids=list(range(n_cores))).results
    cat = lambda k: np.concatenate([r[k] for r in res], axis=0)
    NB = n_cores * B
    ND = n_cores * DB
    return (cat("y_p").reshape(NB, Tq, 1024), cat("y_s").reshape(ND, 1, 1024),
            cat("cmp_p").reshape(1, NB, Tq, 2, 2, 64), cat("slc_p").reshape(1, NB, Tq, 2, 2, 64),
            cat("win_p").reshape(1, NB, 512, 2, 2, 64),
            cat("sre_p").reshape(1, NB, 32, 64), cat("sim_p").reshape(1, NB, 32, 64),
            cat("cmp_s").reshape(1, ND, 1, 2, 2, 64), cat("slc_s").reshape(1, ND, 1, 2, 2, 64),
            cat("win_s").reshape(1, ND, 512, 2, 2, 64),
            cat("sre_s").reshape(1, ND, 32, 64), cat("sim_s").reshape(1, ND, 32, 64))


def kernel(**inputs):
    cfg = dict(B=2, T=4096, DB=16, NPG=64, NPHYS=10240, NSEL=16)
    return _run(inputs, cfg, 8)
```
